# Optimizing a Trainium2 kernel written in Bass

```python
import math
import jax
import jax.numpy as jnp
from jax import lax
import numpy as np

D_MODEL = 1024
BATCH = 16
SEQ = 256
DEPTH = 2
DEC_BATCH = 2
DEC_SEQ = 4096
PAST_LEN = 256

GRID_W = 64
HEAD_DIM = 64
D_SSM = D_MODEL // 2
SSM_GROUP_CH = 16
SSM_GROUPS = D_SSM // SSM_GROUP_CH
SSM_STATE = 64
N_HEADS_WIN = D_MODEL // (2 * HEAD_DIM)
N_KV_WIN = N_HEADS_WIN // 4
GRP_WIN = N_HEADS_WIN // N_KV_WIN
N_HEADS_GLB = D_MODEL // (2 * HEAD_DIM)
N_KV_GLB = N_HEADS_GLB // 4
GRP_GLB = N_HEADS_GLB // N_KV_GLB
D_WIN = N_HEADS_WIN * HEAD_DIM
D_KV_WIN = N_KV_WIN * HEAD_DIM
D_GLB = N_HEADS_GLB * HEAD_DIM
D_KV_GLB = N_KV_GLB * HEAD_DIM
WINDOW = 128
WIN_BLK = 128
N_SUB = 1 + 2 * WINDOW // WIN_BLK
Q_BLK = 128
ROPE_BASE = 10000.0
D_FF = 4 * D_MODEL
LN_EPS = 1e-5
RMS_EPS = 1e-6
ATTN_SCALE = HEAD_DIM ** -0.5
DEEPNORM_ALPHA = (2.0 * DEPTH) ** 0.25
DEEPNORM_BETA = (8.0 * DEPTH) ** -0.25
NEG_INF = -1e30
IN_SIZES = (D_SSM, D_WIN, D_KV_WIN, D_KV_WIN, D_GLB, D_KV_GLB, D_KV_GLB, D_MODEL, D_MODEL, D_MODEL)
IN_OFFSETS = tuple(int(v) for v in np.cumsum(IN_SIZES)[:-1])
N_IN = sum(IN_SIZES)

kernel_name = 'hybrid_diffusion_s5_window_global_step'


def _adaln(cond, w_mod, b_mod):
    m = jax.nn.silu(cond) @ w_mod + b_mod
    return [t[:, None, :] for t in jnp.split(m, 6, axis=-1)]


def _layer_norm(z, g, b):
    zf = z.astype(jnp.float32)
    mu = jnp.mean(zf, axis=-1, keepdims=True)
    var = jnp.mean(jnp.square(zf - mu), axis=-1, keepdims=True)
    out = (zf - mu) * lax.rsqrt(var + LN_EPS) * g.astype(jnp.float32) + b.astype(jnp.float32)
    return out.astype(z.dtype)


def _deepnorm_residual(x, f, g, b):
    return _layer_norm(DEEPNORM_ALPHA * x + f, g, b)


def _rms_norm(x, g):
    xf = x.astype(jnp.float32)
    out = xf * lax.rsqrt(jnp.mean(jnp.square(xf), axis=-1, keepdims=True) + RMS_EPS) * g.astype(jnp.float32)
    return out.astype(x.dtype)


def _axial_rope_tables(n_tok):
    rows = n_tok // GRID_W
    row = jnp.repeat(jnp.arange(rows, dtype=jnp.float32), GRID_W)
    col = jnp.tile(jnp.arange(GRID_W, dtype=jnp.float32), rows)
    n_freq = HEAD_DIM // 4
    inv = ROPE_BASE ** (-jnp.arange(n_freq, dtype=jnp.float32) / n_freq)
    ang = jnp.concatenate([row[:, None] * inv, col[:, None] * inv], axis=-1)
    return jnp.cos(ang), jnp.sin(ang)


def _rope(x, cos, sin):
    half = HEAD_DIM // 2
    shp = (1, cos.shape[0]) + (1,) * (x.ndim - 3) + (half,)
    c, s = cos.reshape(shp), sin.reshape(shp)
    xf = x.astype(jnp.float32)
    x1, x2 = xf[..., :half], xf[..., half:]
    return jnp.concatenate([x1 * c - x2 * s, x2 * c + x1 * s], axis=-1).astype(x.dtype)


def _ssm_combine(e1, e2):
    a1, b1 = e1
    a2, b2 = e2
    return a1 * a2, a2 * b1 + b2


def _s5_bidirectional(u, lam_re, lam_im, log_step, b_re, b_im, c_re, c_im, d_skip, w_glu, s0):
    f32 = jnp.float32
    bsz, n_tok, _ = u.shape
    uf = u.astype(f32).reshape(bsz, n_tok, SSM_GROUPS, SSM_GROUP_CH)
    lam = lax.complex(lam_re.astype(f32), lam_im.astype(f32))
    step = jnp.exp(log_step.astype(f32))[..., None]
    lam_bar = jnp.exp(lam * step)
    b_bar = ((lam_bar - 1.0) / lam)[..., None] * lax.complex(b_re.astype(f32), b_im.astype(f32))
    c_mat = lax.complex(c_re.astype(f32), c_im.astype(f32))
    y = d_skip.astype(f32).reshape(SSM_GROUPS, SSM_GROUP_CH) * uf
    finals = []
    for d, rev in ((0, False), (1, True)):
        bu = jnp.einsum('blgc,gpc->blgp', uf, b_bar[d])
        if s0 is not None:
            s0c = lax.complex(s0[:, d, 0].astype(f32), s0[:, d, 1].astype(f32))
            bu = bu.at[:, n_tok - 1 if rev else 0].add(lam_bar[d] * s0c)
        a = jnp.broadcast_to(lam_bar[d], bu.shape)
        _, states = lax.associative_scan(_ssm_combine, (a, bu), reverse=rev, axis=1)
        y = y + jnp.einsum('blgp,gcp->blgc', states, c_mat[d]).real
        finals.append(states[:, 0] if rev else states[:, -1])
    fin = jnp.stack(finals, axis=1)
    fin = jnp.stack([fin.real, fin.imag], axis=2).astype(u.dtype)
    y = jax.nn.gelu(y.reshape(bsz, n_tok, D_SSM)).astype(u.dtype)
    y = y * jax.nn.sigmoid(y @ w_glu)
    return y, fin


def _blocked_attention(q, k, v, sink):
    bsz, n_q, hkv, grp, hd = q.shape
    n_k = k.shape[1]
    qb = jnp.moveaxis(q.reshape(bsz, n_q // Q_BLK, Q_BLK, hkv, grp, hd), 1, 0)

    def one_block(qq):
        s = jnp.einsum('bqhgd,bkhd->bhgqk', qq, k).astype(jnp.float32) * ATTN_SCALE
        if sink is not None:
            s_sink = jnp.broadcast_to(sink.astype(jnp.float32)[None, :, :, None, None], s.shape[:-1] + (1,))
            s = jnp.concatenate([s, s_sink], axis=-1)
        p = jax.nn.softmax(s, axis=-1)[..., :n_k]
        return jnp.einsum('bhgqk,bkhd->bqhgd', p.astype(v.dtype), v)

    out = lax.map(one_block, qb)
    return jnp.moveaxis(out, 0, 1).reshape(bsz, n_q, hkv, grp, hd)


def _window_attention(q, k, v, k_ctx, v_ctx, sink):
    bsz, n_tok, hkv, grp, hd = q.shape
    nb = n_tok // WIN_BLK
    nk = N_SUB * WIN_BLK
    qb = q.reshape(bsz, nb, WIN_BLK, hkv, grp, hd)
    pad = ((0, 0), (WINDOW, WINDOW), (0, 0), (0, 0))
    kp, vp = jnp.pad(k, pad), jnp.pad(v, pad)
    kb = jnp.concatenate([kp[:, j * WIN_BLK:j * WIN_BLK + n_tok].reshape(bsz, nb, WIN_BLK, hkv, hd) for j in range(N_SUB)], axis=2)
    vb = jnp.concatenate([vp[:, j * WIN_BLK:j * WIN_BLK + n_tok].reshape(bsz, nb, WIN_BLK, hkv, hd) for j in range(N_SUB)], axis=2)
    blk = jnp.arange(nb)[:, None, None] * WIN_BLK
    qpos = blk + jnp.arange(WIN_BLK)[None, :, None]
    kpos = blk - WINDOW + jnp.arange(nk)[None, None, :]
    mask = (jnp.abs(kpos - qpos) <= WINDOW) & (kpos >= 0) & (kpos < n_tok)
    s_band = jnp.einsum('bnqhgd,bnkhd->bnhgqk', qb, kb).astype(jnp.float32) * ATTN_SCALE
    s_band = jnp.where(mask[None, :, None, None], s_band, NEG_INF)
    s_ctx = jnp.einsum('bnqhgd,bkhd->bnhgqk', qb, k_ctx).astype(jnp.float32) * ATTN_SCALE
    s_sink = jnp.broadcast_to(sink.astype(jnp.float32)[None, None, :, :, None, None], s_band.shape[:-1] + (1,))
    p = jax.nn.softmax(jnp.concatenate([s_band, s_ctx, s_sink], axis=-1), axis=-1)
    n_ctx = k_ctx.shape[1]
    out = (jnp.einsum('bnhgqk,bnkhd->bnqhgd', p[..., :nk].astype(v.dtype), vb)
           + jnp.einsum('bnhgqk,bkhd->bnqhgd', p[..., nk:nk + n_ctx].astype(v.dtype), v_ctx))
    return out.reshape(bsz, n_tok, hkv, grp, hd)


def _merge_branches(ya, yw, yg, ga, gw, gg, w_br_ssm, w_br_win, w_br_glb, w_out):
    m = (jax.nn.sigmoid(ga) * (ya @ w_br_ssm) + jax.nn.sigmoid(gw) * (yw @ w_br_win)
         + jax.nn.sigmoid(gg) * (yg @ w_br_glb))
    return m @ w_out


def _sq_relu_mlp(h, w_up, w_down):
    return jnp.square(jax.nn.relu(h @ w_up)) @ w_down


def _layer(x, cond, lp, ctx=None, rope=None):
    bsz, n_tok, _ = x.shape
    sh1, sc1, g1, sh2, sc2, g2 = _adaln(cond, lp['w_mod'], lp['b_mod'])
    h = x * (1.0 + sc1) + sh1
    u, qw, kw, vw, qg, kg, vg, ga, gw, gg = jnp.split(h @ lp['w_in'], IN_OFFSETS, axis=-1)
    qw = qw.reshape(bsz, n_tok, N_KV_WIN, GRP_WIN, HEAD_DIM)
    kw = kw.reshape(bsz, n_tok, N_KV_WIN, HEAD_DIM)
    vw = vw.reshape(bsz, n_tok, N_KV_WIN, HEAD_DIM)
    qg = _rms_norm(qg.reshape(bsz, n_tok, N_KV_GLB, GRP_GLB, HEAD_DIM), lp['q_norm'])
    kg = _rms_norm(kg.reshape(bsz, n_tok, N_KV_GLB, HEAD_DIM), lp['k_norm'])
    vg = vg.reshape(bsz, n_tok, N_KV_GLB, HEAD_DIM)
    s0 = None if ctx is None else ctx[0]
    ya, s_fin = _s5_bidirectional(u, lp['lam_re'], lp['lam_im'], lp['log_step'], lp['b_re'], lp['b_im'],
                                  lp['c_re'], lp['c_im'], lp['d_skip'], lp['w_glu'], s0)
    if ctx is None:
        yw = _blocked_attention(qw, kw, vw, lp['sink'])
        yg = _blocked_attention(qg, kg, vg, None)
        new_ctx = (s_fin, kw, vw, kg, vg)
    else:
        _, k_wc, v_wc, k_gc, v_gc = ctx
        cos, sin = rope
        qw, kw, qg, kg = (_rope(t, cos, sin) for t in (qw, kw, qg, kg))
        yw = _window_attention(qw, kw, vw, k_wc, v_wc, lp['sink'])
        yg = _blocked_attention(qg, jnp.concatenate([kg, k_gc], axis=1), jnp.concatenate([vg, v_gc], axis=1), None)
        new_ctx = None
    f = _merge_branches(ya, yw.reshape(bsz, n_tok, D_WIN), yg.reshape(bsz, n_tok, D_GLB), ga, gw, gg,
                        lp['w_br_ssm'], lp['w_br_win'], lp['w_br_glb'], lp['w_out'])
    x = _deepnorm_residual(x, g1 * f, lp['ln1_g'], lp['ln1_b'])
    h2 = x * (1.0 + sc2) + sh2
    x = _deepnorm_residual(x, g2 * _sq_relu_mlp(h2, lp['w_up'], lp['w_down']), lp['ln2_g'], lp['ln2_b'])
    return x, new_ctx


def setup_inputs(seed: int = 0) -> dict:
    key = jax.random.key(seed)
    ks = iter(jax.random.split(key, 40))
    f32 = jnp.float32

    def nrm(shape, scale):
        return jax.random.normal(next(ks), shape, f32) * scale

    lam_n = jnp.pi * jnp.arange(SSM_STATE, dtype=f32)
    return {
        'x_prompt': nrm((BATCH, SEQ, D_MODEL), 1.0),
        'x_sample': nrm((DEC_BATCH, DEC_SEQ, D_MODEL), 1.0),
        'state_ssm': nrm((DEC_BATCH, DEPTH, 2, 2, SSM_GROUPS, SSM_STATE), 0.3),
        'cache_k_win': nrm((DEC_BATCH, DEPTH, PAST_LEN, N_KV_WIN, HEAD_DIM), 1.0),
        'cache_v_win': nrm((DEC_BATCH, DEPTH, PAST_LEN, N_KV_WIN, HEAD_DIM), 1.0),
        'cache_k_glb': nrm((DEC_BATCH, DEPTH, PAST_LEN, N_KV_GLB, HEAD_DIM), 1.0),
        'cache_v_glb': nrm((DEC_BATCH, DEPTH, PAST_LEN, N_KV_GLB, HEAD_DIM), 1.0),
        'c': nrm((DEC_BATCH, D_MODEL), 1.0),
        'c_ctx': nrm((D_MODEL,), 1.0),
        'w_mod': nrm((DEPTH, D_MODEL, 6 * D_MODEL), D_MODEL ** -0.5),
        'b_mod': nrm((DEPTH, 6 * D_MODEL), 0.02),
        'w_in': nrm((DEPTH, D_MODEL, N_IN), D_MODEL ** -0.5),
        'ssm_lam_re': -0.5 + nrm((DEPTH, 2, SSM_GROUPS, SSM_STATE), 0.01),
        'ssm_lam_im': lam_n + nrm((DEPTH, 2, SSM_GROUPS, SSM_STATE), 0.01),
        'ssm_log_step': jax.random.uniform(next(ks), (DEPTH, 2, SSM_GROUPS), f32, math.log(1e-3), math.log(1e-1)),
        'ssm_b_re': nrm((DEPTH, 2, SSM_GROUPS, SSM_STATE, SSM_GROUP_CH), (2 * SSM_GROUP_CH) ** -0.5),
        'ssm_b_im': nrm((DEPTH, 2, SSM_GROUPS, SSM_STATE, SSM_GROUP_CH), (2 * SSM_GROUP_CH) ** -0.5),
        'ssm_c_re': nrm((DEPTH, 2, SSM_GROUPS, SSM_GROUP_CH, SSM_STATE), SSM_STATE ** -0.5),
        'ssm_c_im': nrm((DEPTH, 2, SSM_GROUPS, SSM_GROUP_CH, SSM_STATE), SSM_STATE ** -0.5),
        'ssm_d': nrm((DEPTH, D_SSM), 1.0),
        'w_glu': nrm((DEPTH, D_SSM, D_SSM), D_SSM ** -0.5),
        'sink_win': nrm((DEPTH, N_HEADS_WIN), 0.5),
        'q_norm_glb': 1.0 + nrm((DEPTH, HEAD_DIM), 0.02),
        'k_norm_glb': 1.0 + nrm((DEPTH, HEAD_DIM), 0.02),
        'w_br_ssm': nrm((DEPTH, D_SSM, D_MODEL), D_SSM ** -0.5),
        'w_br_win': nrm((DEPTH, D_WIN, D_MODEL), D_WIN ** -0.5),
        'w_br_glb': nrm((DEPTH, D_GLB, D_MODEL), D_GLB ** -0.5),
        'w_out': nrm((DEPTH, D_MODEL, D_MODEL), D_MODEL ** -0.5 * DEEPNORM_BETA),
        'ln1_g': 1.0 + nrm((DEPTH, D_MODEL), 0.02),
        'ln1_b': nrm((DEPTH, D_MODEL), 0.02),
        'w_up': nrm((DEPTH, D_MODEL, D_FF), D_MODEL ** -0.5),
        'w_down': nrm((DEPTH, D_FF, D_MODEL), D_FF ** -0.5 * DEEPNORM_BETA),
        'ln2_g': 1.0 + nrm((DEPTH, D_MODEL), 0.02),
        'ln2_b': nrm((DEPTH, D_MODEL), 0.02),
    }


def reference(x_prompt, x_sample, state_ssm, cache_k_win, cache_v_win, cache_k_glb, cache_v_glb, c, c_ctx,
              w_mod, b_mod, w_in, ssm_lam_re, ssm_lam_im, ssm_log_step, ssm_b_re, ssm_b_im, ssm_c_re, ssm_c_im,
              ssm_d, w_glu, sink_win, q_norm_glb, k_norm_glb, w_br_ssm, w_br_win, w_br_glb, w_out,
              ln1_g, ln1_b, w_up, w_down, ln2_g, ln2_b):
    rope = _axial_rope_tables(x_sample.shape[1])
    xp, xs = x_prompt, x_sample
    new_ssm, new_kw, new_vw, new_kg, new_vg = [], [], [], [], []
    for l in range(DEPTH):
        lp = dict(w_mod=w_mod[l], b_mod=b_mod[l], w_in=w_in[l],
                  lam_re=ssm_lam_re[l], lam_im=ssm_lam_im[l], log_step=ssm_log_step[l],
                  b_re=ssm_b_re[l], b_im=ssm_b_im[l], c_re=ssm_c_re[l], c_im=ssm_c_im[l],
                  d_skip=ssm_d[l], w_glu=w_glu[l], sink=sink_win[l].reshape(N_KV_WIN, GRP_WIN),
                  q_norm=q_norm_glb[l], k_norm=k_norm_glb[l],
                  w_br_ssm=w_br_ssm[l], w_br_win=w_br_win[l], w_br_glb=w_br_glb[l], w_out=w_out[l],
                  ln1_g=ln1_g[l], ln1_b=ln1_b[l], w_up=w_up[l], w_down=w_down[l], ln2_g=ln2_g[l], ln2_b=ln2_b[l])
        xp, (s_fin, kw, vw, kg, vg) = _layer(xp, c_ctx[None, :], lp)
        new_ssm.append(s_fin)
        new_kw.append(kw)
        new_vw.append(vw)
        new_kg.append(kg)
        new_vg.append(vg)
        ctx = (state_ssm[:, l], cache_k_win[:, l], cache_v_win[:, l], cache_k_glb[:, l], cache_v_glb[:, l])
        xs, _ = _layer(xs, c, lp, ctx, rope)
    new_state_ssm = jnp.stack(new_ssm, axis=1)
    new_cache_k_win = jnp.stack(new_kw, axis=1)
    new_cache_v_win = jnp.stack(new_vw, axis=1)
    new_cache_k_glb = jnp.stack(new_kg, axis=1)
    new_cache_v_glb = jnp.stack(new_vg, axis=1)
    return (xp, xs, new_state_ssm, new_cache_k_win, new_cache_v_win, new_cache_k_glb, new_cache_v_glb)
```

```python
import numpy as np
from contextlib import ExitStack
import concourse.bass as bass
import concourse.mybir as mybir
from concourse.bass_utils import run_bass_kernel_spmd

F32 = mybir.dt.float32
BF16 = mybir.dt.bfloat16
I32 = mybir.dt.int32
ALU = mybir.AluOpType
AF = mybir.ActivationFunctionType
GROUPS = [[0, 1, 2, 3], [4, 5, 6, 7]]
NDMA = 40
EPOCH = 6000
ALPHA = float((2.0 * 2) ** 0.25)
TWO_PI = float(2 * np.pi)
PI_LO = 3.1415925
import os
DBGF = set(os.environ.get('KDBG', '').split(','))
SKIP_SELF = False
SKIP_OLD = False


class Buf:
    __slots__ = ("w", "r")

    def __init__(self):
        self.w = None
        self.r = {}


class Prog:
    ENG = ("pe", "act", "dve", "pool", "sp")

    def __init__(self, nc, es):
        self.nc, self.es = nc, es
        self.q = {e: [] for e in self.ENG}
        self.n = {e: 0 for e in self.ENG}
        self.ep = {e: None for e in self.ENG}
        self.last = {e: None for e in self.ENG}
        self.waited = {e: {} for e in self.ENG}
        self.bufs = {}
        self.nsem = 0
        self.dma_sems = [es.enter_context(nc.semaphore(f"dq{i}")) for i in range(NDMA)]
        self.dma_val = [0] * NDMA
        self.dma_tok = [None] * NDMA
        self.dma_rr = 0
        self.dma_rrq = {"sp": 0, "pool": 0, "act": 0}
        self.out_toks = []

    def b(self, *key):
        v = self.bufs.get(key)
        if v is None:
            v = self.bufs[key] = Buf()
        return v

    def _sem(self, e):
        s = self.ep[e]
        if s is None or s[1] >= EPOCH:
            h = self.es.enter_context(self.nc.semaphore(f"pg{e}{self.nsem}"))
            self.nsem += 1
            s = self.ep[e] = [h, 0]
        return s

    def _need(self, eng, tok, cur_big=False):
        if tok is None:
            return
        h, val, src, idx = tok[:4]
        if src == eng:
            if eng in ("pe", "sp"):
                return
            if SKIP_OLD and self.n[eng] - idx > 3:
                return
            if SKIP_SELF and cur_big and len(tok) > 4 and tok[4]:
                return
        w = self.waited[eng]
        if w.get(h, 0) >= val:
            return
        w[h] = val
        self.q[eng].append(lambda e, h=h, val=val: e.wait_ge(h, val))

    def _deps(self, eng, R, W, cur_big=False):
        for b in R:
            self._need(eng, b.w, cur_big)
        for b in W:
            self._need(eng, b.w, cur_big)
            for t in b.r.values():
                self._need(eng, t, cur_big)

    def _upd(self, tok, R, W):
        for b in R:
            b.r[tok[2]] = tok
        for b in W:
            b.w = tok
            b.r = {}

    def op(self, eng, meth, R=(), W=(), **kw):
        big = False
        if not isinstance(meth, list):
            o_ = kw.get("out", kw.get("ap"))
            if o_ is not None:
                fs = 1
                for d_ in o_.shape[1:]:
                    fs *= d_
                big = fs >= 256
        self._deps(eng, R, W, big)
        s = self._sem(eng)
        s[1] += 1
        h, val = s[0], s[1]
        idx = self.n[eng]
        self.n[eng] += 1
        calls = meth if isinstance(meth, list) else [(meth, kw)]

        def thunk(e, calls=calls, h=h):
            for m, k in calls[:-1]:
                getattr(e, m)(**k)
            m, k = calls[-1]
            getattr(e, m)(**k).then_inc(h, 1)
        self.q[eng].append(thunk)
        tok = (h, val, eng, idx, big)
        self.last[eng] = tok
        self._upd(tok, R, W)
        return tok

    def mm(self, calls, R=(), W=()):
        return self.op("pe", [("matmul", c) for c in calls], R=R, W=W)

    def dma_slot(self, qeng):
        half = NDMA // 2
        i = self.dma_rrq[qeng]
        self.dma_rrq[qeng] = (i + 1) % half
        return i + (half if qeng == "pool" else 0)

    def dma(self, qeng, out, in_, R=(), W=(), is_out=False):
        self._deps(qeng, R, W)
        k = self.dma_slot(qeng)
        self._need(qeng, self.dma_tok[k])
        self.dma_val[k] += 16
        val, h = self.dma_val[k], self.dma_sems[k]
        self.q[qeng].append(lambda e, out=out, in_=in_, h=h: e.dma_start(out=out, in_=in_).then_inc(h, 16))
        self.n[qeng] += 1
        tok = (h, val, "dma%d" % k, 0)
        self.dma_tok[k] = tok
        self._upd(tok, R, W)
        if is_out:
            self.out_toks.append(tok)
        return tok

    def coll(self, in_ap, out_ap, R=(), W=()):
        self._deps("pool", R, W)
        h = self.es.enter_context(self.nc.semaphore(f"cc{self.nsem}"))
        self.nsem += 1
        self.q["pool"].append(lambda e, h=h: e.collective_compute(
            "AllGather", ALU.bypass, replica_groups=GROUPS, ins=[in_ap], outs=[out_ap]).then_inc(h, 1))
        self.n["pool"] += 1
        tok = (h, 1, "cc%d" % self.nsem, 0)
        self._upd(tok, R, W)
        return tok

    def barrier(self):
        toks = [self.last[e] for e in self.ENG] + list(self.dma_tok)
        for e in self.ENG:
            for t in toks:
                if t is not None and t[2] != e:
                    self._need(e, t)

    def finish(self, block):
        for t in self.out_toks + [self.last[e] for e in self.ENG] + list(self.dma_tok):
            self._need("sp", t)
        reg = {"pe": block.tensor, "act": block.scalar, "dve": block.vector, "pool": block.gpsimd, "sp": block.sync}
        for e in self.ENG:
            lst = self.q[e]

            def run(eng, lst=lst):
                for f in lst:
                    f(eng)
            reg[e](run)


class Arena:
    def __init__(self, nc):
        self.nc = nc
        self.p = 16512
        self.hi = 229344
        self.k = 0
        self.peak = 0

    def alloc(self, name, shape, dt):
        n = 1
        for s in shape[1:]:
            n *= s
        sz = {F32: 4, BF16: 2, I32: 4}[dt]
        off = (self.p + 63) // 64 * 64
        self.last_off = off
        self.p = off + n * sz
        self.peak = max(self.peak, self.p)
        assert self.p <= self.hi, f"SBUF overflow at {name}: {self.p}"
        self.k += 1
        return self.nc.alloc_sbuf_tensor_at(f"{name}{self.k}", list(shape), dt, offset=off)

    def mark(self):
        return self.p

    def release(self, m):
        self.p = m


def build(debug=(), nlayers=2, stop_after=None):
    nc = bass.Bass("TRN2", target_bir_lowering=False)
    es = ExitStack()
    with es:
        def din(name, shape, dt=F32):
            return nc.dram_tensor(name, list(shape), dt, kind="ExternalInput").ap()

        def dout(name, shape, dt=F32):
            return nc.dram_tensor(name, list(shape), dt, kind="ExternalOutput").ap()

        def dint(name, shape, dt):
            return nc.dram_tensor(name, list(shape), dt, kind="Internal").ap()

        xin = din("xin", [1536, 1024])
        condT = din("condT", [128, 8, 2])
        w_mod = din("w_mod", [2, 1024, 6144])
        b_modT = din("b_modT", [2, 128, 48])
        w_in = din("w_in", [2, 1024, 5120])
        lam = din("lam", [2, 128, 3, 40])
        wbc = din("wbc", [2, 4, 32, 10, 2, 128])
        wcc = din("wcc", [2, 128, 40, 2, 32])
        dskipT = din("dskipT", [2, 128, 4])
        s0h = din("s0h", [2, 128, 2, 8])
        w_glu = din("w_glu", [2, 512, 512])
        w_br = din("w_br", [2, 3, 512, 1024])
        w_out = din("w_out", [2, 1024, 1024])
        w_up = din("w_up", [2, 1024, 4096])
        w_down = din("w_down", [2, 4096, 1024])
        lnp = din("lnp", [2, 128, 4, 8])
        qkn = din("qkn", [2, 128, 2])
        knrow = din("knrow", [2, 128, 64])
        sinkb = din("sinkb", [2, 128, 8])
        ck = din("ck", [2, 2, 256, 128])
        cv = din("cv", [2, 2, 256, 128])
        cmat = din("cmat", [7, 128, 128])
        jidx = din("jidx", [2, 128, 512])
        rope = din("rope", [2, 128, 1024])
        yo = dout("yo", [1536, 1024])
        co = dout("co", [2, 2, 4, 256, 128])
        so = dout("so", [2, 2, 2, 2, 32, 64])
        xu = dint("xu", [512, 1024], BF16)
        xu_g = dint("xu_g", [2048, 1024], BF16)
        xk = dint("xk", [256, 1024], BF16)
        xk_g = dint("xk_g", [1024, 1024], BF16)
        xv = dint("xv", [1024, 256], BF16)
        xv_g = dint("xv_g", [4096, 256], BF16)
        xy = dint("xy", [4, 128, 1024], F32)
        xy_g = dint("xy_g", [2048, 1024], F32)
        yp_d = dint("yp_d", [128, 4, 512], F32)
        dbg_out = {}

        P = Prog(nc, es)
        A = Arena(nc)
        ps = [es.enter_context(nc.psum_tensor(f"ps{i}", [128, 512], F32)) for i in range(8)]
        psb = [P.b("ps", i) for i in range(8)]
        block = es.enter_context(nc.Block())
        pid = None

        def dump(name, ap, shape, bufs, dt=F32):
            if name not in debug:
                return
            d = dout("dbg_" + name, shape, dt)
            P.dma("sp", d, ap, R=bufs, is_out=True)

        xT = A.alloc("xT", [128, 8, 1536], F32)
        cm = A.alloc("cm", [128, 7, 128], BF16)
        identF = A.alloc("identF", [128, 128], F32)
        lnmean = A.alloc("lnmean", [128, 128], BF16)
        onesF = A.alloc("onesF", [128, 64], F32)
        cst = A.alloc("cst", [128, 8], F32)
        ropeT = A.alloc("ropeT", [128, 2, 1024], F32)
        jT = A.alloc("jT", [128, 2, 512], F32)
        modvs = [A.alloc("modv", [128, 48, 2], F32) for _ in range(2)]
        lnpT = A.alloc("lnpT", [128, 4, 8], F32)
        qknT = A.alloc("qknT", [128, 2], F32)
        esink = A.alloc("esink", [128, 8], F32)
        dskT = A.alloc("dskT", [128, 4], F32)
        Bc = P.b("const")
        Blay = P.b("laycst")

        def xb(ft, c):
            return P.b("xT", ft, c)

        P.dma("pool", cm[:], cmat.rearrange("k p c -> p k c"), W=[Bc])
        P.dma("sp", identF[:], cmat[0], W=[Bc])
        P.dma("sp", ropeT[:], rope.rearrange("k p c -> p k c"), W=[Bc])
        P.dma("sp", jT[:], jidx.rearrange("k p c -> p k c"), W=[Bc])
        P.op("pool", "memset", W=[Bc], ap=lnmean[:], constant=1.0 / 1024)
        P.op("pool", "memset", W=[Bc], ap=onesF[:], constant=1.0)
        P.op("pool", "memset", W=[Bc], ap=cst[:, 0:1], constant=1e-5)
        P.op("pool", "memset", W=[Bc], ap=cst[:, 1:2], constant=1e-6)
        P.op("pool", "memset", W=[Bc], ap=cst[:, 2:3], constant=float(np.pi / 2))
        P.op("pool", "memset", W=[Bc], ap=cst[:, 3:4], constant=0.0)
        IDB, ROT, HMEAN, ML, MR, MLF, MRL = range(7)

        m0 = A.mark()
        xtok = [A.alloc("xtok", [128, 1024], F32) for _ in range(2)]
        for tt in range(12):
            xt = xtok[tt % 2]
            bx = P.b("xtok", tt % 2)
            P.dma("sp", xt[:], xin[128 * tt:128 * tt + 128, :], W=[bx])
            for hh in range(2):
                bank = (2 * tt + hh) % 2
                P.mm([dict(out=ps[bank][:, 128 * q:128 * q + 128],
                           lhsT=xt[:, 128 * (4 * hh + q):128 * (4 * hh + q) + 128],
                           rhs=identF[:], start=True, stop=True) for q in range(4)],
                     R=[bx, Bc], W=[psb[bank]])
                c = tt // 4
                o_ap = xT[:, 4 * hh:4 * hh + 4, 128 * tt:128 * tt + 128]
                i_ap = ps[bank][:].rearrange("p (q t) -> p q t", q=4)
                Wx = [xb(ft, c) for ft in range(4 * hh, 4 * hh + 4)]
                if hh:
                    P.op("act", "activation", R=[psb[bank]], W=Wx, out=o_ap, in_=i_ap, func=AF.Copy)
                else:
                    P.op("dve", "tensor_copy", R=[psb[bank]], W=Wx, out=o_ap, in_=i_ap)
        P.barrier()
        A.release(m0)
        dump("xT0", xT[:, 0, :], [128, 1536], [xb(0, c) for c in range(3)])

        _bank = [0]

        def nb():
            _bank[0] = (_bank[0] + 1) % 4
            return _bank[0]

        MUL, ADD, SUB = ALU.mult, ALU.add, ALU.subtract
        (LRE, LIM, LST, STEP, LR, TH, RR, T1, T2, T3, THR, SIN, COS, ARE, AIM, AM1, NRE, NIM, DEN, CFRE, CFIM,
         C512, S512, T4) = range(24)
        kvp = {}

        _pidc = {}

        def pid4(e):
            if "v" not in _pidc:
                _pidc["v"] = e.partition_id()
            return _pidc["v"] % 4

        def dyn_dma(qeng, fn, R=(), W=(), is_out=False):
            P._deps(qeng, R, W)
            k = P.dma_slot(qeng)
            P._need(qeng, P.dma_tok[k])
            P.dma_val[k] += 16
            val, h = P.dma_val[k], P.dma_sems[k]

            def thunk(e, fn=fn, h=h):
                o_, i_ = fn(e)
                e.dma_start(out=o_, in_=i_).then_inc(h, 16)
            P.q[qeng].append(thunk)
            P.n[qeng] += 1
            tok = (h, val, "dma%d" % k, 0)
            P.dma_tok[k] = tok
            P._upd(tok, R, W)
            return tok

        def mod_vectors_gen(l_):
            mv = modvs[l_ % 2]
            bmv = P.b("modv", l_ % 2)
            mM = A.mark()
            sc = A.alloc("scond", [128, 8, 2], F32)
            bm = A.alloc("bmod", [128, 48], F32)
            b_sc, b_bm = P.b("sc"), P.b("bm")
            P.dma("sp", sc[:], condT, W=[b_sc])
            P.op("act", "activation", R=[b_sc], W=[b_sc], out=sc[:], in_=sc[:], func=AF.Silu)
            P.dma("sp", bm[:], b_modT[l_], W=[b_bm])
            wm = [A.alloc("wm", [128, 8, 128], F32) for _ in range(3)]
            wmv = w_mod[l_].rearrange("(kt p) c -> p kt c", p=128)

            def load(T):
                P.dma("sp", wm[T % 3][:], wmv[:, :, 128 * T:128 * T + 128], W=[P.b("wm", T % 3)])
            load(0)
            load(1)
            for T in range(48):
                if T + 2 < 48:
                    load(T + 2)
                w = wm[T % 3]
                P.mm([dict(out=ps[3][:, 2 * T:2 * T + 2], lhsT=w[:, kt, :], rhs=sc[:, kt, :], start=(kt == 0), stop=(kt == 7))
                      for kt in range(8)], R=[P.b("wm", T % 3), b_sc], W=[psb[3]])
                yield T
            pm3 = ps[3][:, 0:96].rearrange("p (t k) -> p t k", k=2)
            for k in range(2):
                P.op("dve", "tensor_tensor", R=[psb[3], b_bm], W=[bmv], out=mv[:, :, k], in0=pm3[:, :, k], in1=bm[:, :], op=ADD)
            for lo in (8, 32):
                P.op("dve", "tensor_scalar", R=[bmv], W=[bmv], out=mv[:, lo:lo + 8, :], in0=mv[:, lo:lo + 8, :],
                     scalar1=1.0, scalar2=None, op0=ADD)
            if l_ == 0:
                P.barrier()
            A.release(mM)

        def mod_vectors(l_):
            for _ in mod_vectors_gen(l_):
                pass

        def out_chunk(c, ybufs):
            n = 0
            for tt in range(4 * c, 4 * c + 4):
                for hh in range(4):
                    bank = n % 2
                    P.mm([dict(out=ps[bank][:, 128 * q:128 * q + 128], lhsT=xT[:, 2 * hh + q, 128 * tt:128 * tt + 128],
                               rhs=identF[:], start=True, stop=True) for q in range(2)],
                         R=[xb(ft, c) for ft in range(2 * hh, 2 * hh + 2)] + [Bc], W=[psb[bank]])
                    yb, byb = ybufs[n % 2], P.b("ytokL", n % 2)
                    if n % 2:
                        P.op("act", "activation", R=[psb[bank]], W=[byb], out=yb[:], in_=ps[bank][:, 0:256], func=AF.Copy)
                    else:
                        P.op("dve", "tensor_copy", R=[psb[bank]], W=[byb], out=yb[:], in_=ps[bank][:, 0:256])
                    P.dma("sp", yo[128 * tt:128 * tt + 128, 256 * hh:256 * hh + 256], yb[:], R=[byb], is_out=True)
                    n += 1

        def layer(l):
            wv = w_in[l].rearrange("(kt p) c -> p kt c", p=128)
            k_of = (0, 1, 1)
            P.dma("sp", lnpT[:], lnp[l], W=[Blay])
            P.dma("sp", qknT[:], qkn[l], W=[Blay])
            P.dma("sp", dskT[:], dskipT[l], W=[Blay])
            mLay = A.mark()
            kpT = A.alloc("kpT", [128, 2, 512], BF16)
            vp = A.alloc("vp", [128, 4, 2, 2, 65], BF16)
            Bkp, Bvp = P.b("kpT"), P.b("vp")
            P.op("pool", "memset", W=[Bvp], ap=vp[:], constant=1.0)

            modv = modvs[l % 2]
            Bmod = P.b("modv", l % 2)
            if l == 0:
                mod_vectors(0)
            sk = A.alloc("sk", [128, 8], F32)
            b_sk = P.b("sk")
            P.dma("sp", sk[:], sinkb[l], W=[b_sk])
            P.op("act", "activation", R=[b_sk], W=[Blay], out=esink[:], in_=sk[:], func=AF.Exp)

            mAB = A.mark()
            ubf = A.alloc("ubf", [128, 4, 512], BF16)
            CS = A.alloc("CS", [128, 24, 40], F32)
            CI = A.alloc("CI", [128, 40], I32)
            S0 = A.alloc("S0", [128, 2, 8], F32)
            car = A.alloc("car", [128, 2, 8], F32)
            tt8 = A.alloc("tt8", [128, 8], F32)
            WB = A.alloc("WB", [128, 40, 2, 128], BF16)
            WC = A.alloc("WC", [128, 40, 2, 32], BF16)
            b_cs, b_S0, b_car, b_tt8, b_WB, b_WC = (P.b(n) for n in ("CS", "S0", "car", "tt8", "WB", "WC"))

            def cs(i_, lo=0, hi=40):
                return CS[:, i_, lo:hi]

            def ctt(o_, a_, b_, op_):
                P.op("dve", "tensor_tensor", R=[b_cs], W=[b_cs], out=cs(o_), in0=cs(a_), in1=cs(b_), op=op_)

            P.dma("sp", CS[:, 0:3, :], lam[l], W=[b_cs])
            P.dma("sp", S0[:], s0h[l], W=[b_S0])
            P.op("pool", "memset", W=[b_WB], ap=WB[:], constant=0.0)
            WB5 = WB[:].rearrange("p (q a) r n -> p q a r n", a=4)
            for a in range(4):
                P.dma("pool", WB5[32 * a:32 * a + 32, :, a, :, :], wbc[l, a], W=[b_WB])
            P.dma("pool", WC[:], wcc[l], W=[b_WC])
            P.op("dve", "tensor_scalar", R=[b_WC], W=[b_WC], out=WC[:, :, 1, :], in0=WC[:, :, 1, :], scalar1=-1.0, scalar2=None, op0=MUL)
            WCn = A.alloc("WCn", [128, 40, 32], BF16)
            P.op("dve", "tensor_scalar", R=[b_WC], W=[b_WC], out=WCn[:], in0=WC[:, :, 0, :], scalar1=-1.0, scalar2=None, op0=MUL)
            P.op("act", "activation", R=[b_cs], W=[b_cs], out=cs(STEP), in_=cs(LST), func=AF.Exp)
            ctt(LR, LRE, STEP, MUL)
            ctt(TH, LIM, STEP, MUL)
            P.op("act", "activation", R=[b_cs], W=[b_cs], out=cs(RR), in_=cs(LR), func=AF.Exp)
            P.op("dve", "tensor_scalar", R=[b_cs], W=[b_cs], out=cs(T1), in0=cs(TH), scalar1=1.0 / TWO_PI, scalar2=None, op0=MUL)
            P.op("dve", "tensor_copy", R=[b_cs], W=[b_cs], out=CI[:], in_=cs(T1))
            P.op("dve", "tensor_copy", R=[b_cs], W=[b_cs], out=cs(T1), in_=CI[:])
            P.op("dve", "scalar_tensor_tensor", R=[b_cs], W=[b_cs], out=cs(THR), in0=cs(T1), scalar=-TWO_PI, in1=cs(TH), op0=MUL, op1=ADD)
            P.op("dve", "tensor_scalar", R=[b_cs], W=[b_cs], out=cs(THR), in0=cs(THR), scalar1=PI_LO, scalar2=-PI_LO, op0=ALU.min, op1=ALU.max)
            P.op("act", "activation", R=[b_cs], W=[b_cs], out=cs(SIN), in_=cs(THR), func=AF.Sin)
            P.op("act", "activation", R=[b_cs], W=[b_cs], out=cs(T2), in_=cs(THR), func=AF.Abs)
            P.op("act", "activation", R=[b_cs, Bc], W=[b_cs], out=cs(COS), in_=cs(T2), func=AF.Sin, scale=-1.0, bias=cst[:, 2:3])
            ctt(ARE, RR, COS, MUL)
            ctt(AIM, RR, SIN, MUL)
            P.op("dve", "tensor_scalar", R=[b_cs], W=[b_cs], out=cs(AM1), in0=cs(ARE), scalar1=-1.0, scalar2=None, op0=ADD)
            ctt(T2, AM1, LRE, MUL)
            ctt(T3, AIM, LIM, MUL)
            ctt(NRE, T2, T3, ADD)
            ctt(T2, AIM, LRE, MUL)
            ctt(T3, AM1, LIM, MUL)
            ctt(NIM, T2, T3, SUB)
            ctt(T2, LRE, LRE, MUL)
            ctt(T3, LIM, LIM, MUL)
            ctt(DEN, T2, T3, ADD)
            P.op("dve", "reciprocal", R=[b_cs], W=[b_cs], out=cs(DEN), in_=cs(DEN))
            ctt(CFRE, NRE, DEN, MUL)
            ctt(CFIM, NIM, DEN, MUL)
            P.op("dve", "tensor_copy", R=[b_cs], W=[b_cs], out=cs(C512), in_=cs(COS))
            P.op("dve", "tensor_copy", R=[b_cs], W=[b_cs], out=cs(S512), in_=cs(SIN))
            for _ in range(9):
                ctt(T2, C512, C512, MUL)
                ctt(T3, S512, S512, MUL)
                ctt(T4, C512, S512, MUL)
                ctt(C512, T2, T3, SUB)
                P.op("dve", "tensor_scalar", R=[b_cs], W=[b_cs], out=cs(S512), in0=cs(T4), scalar1=2.0, scalar2=None, op0=MUL)
            P.op("dve", "tensor_tensor", R=[b_cs, b_S0], W=[b_tt8], out=tt8[:], in0=S0[:, 1, :], in1=cs(SIN, 32, 40), op=MUL)
            P.op("dve", "tensor_tensor", R=[b_cs, b_S0], W=[b_car], out=car[:, 0, :], in0=S0[:, 0, :], in1=cs(COS, 32, 40), op=MUL)
            P.op("dve", "tensor_tensor", R=[b_car, b_tt8], W=[b_car], out=car[:, 0, :], in0=car[:, 0, :], in1=tt8[:], op=SUB)
            P.op("dve", "tensor_tensor", R=[b_cs, b_S0], W=[b_tt8], out=tt8[:], in0=S0[:, 1, :], in1=cs(COS, 32, 40), op=MUL)
            P.op("dve", "tensor_tensor", R=[b_cs, b_S0], W=[b_car], out=car[:, 1, :], in0=S0[:, 0, :], in1=cs(SIN, 32, 40), op=MUL)
            P.op("dve", "tensor_tensor", R=[b_car, b_tt8], W=[b_car], out=car[:, 1, :], in0=car[:, 1, :], in1=tt8[:], op=ADD)
            dump("CS%d" % l, CS[:].rearrange("p a e -> p (a e)"), [128, 960], [b_cs])

            mA = A.mark()
            ubs = A.alloc("ubs", [128, 4, 1024], BF16)
            wA = A.alloc("wA", [128, 8, 1024], BF16)
            hT = A.alloc("hT", [128, 8, 512], BF16)
            ksT = A.alloc("ksT", [128, 2, 1024], BF16)
            vs = A.alloc("vs", [128, 8, 256], BF16)
            x32 = A.alloc("x32", [128, 512], F32)
            sqb = A.alloc("sqb", [128, 512], BF16)
            rsf = A.alloc("rsf", [128, 512], F32)
            xbb = A.alloc("xbb", [128, 512], BF16)
            t1f = A.alloc("t1f", [128, 512], F32)
            t2f = A.alloc("t2f", [128, 512], F32)
            ctok = [A.alloc("ctok", [128, 4, 128], F32) for _ in range(2)]
            sq128 = A.alloc("sq128", [128, 128], F32)
            ssq = A.alloc("ssq", [128, 2], F32)
            knr = A.alloc("knr", [128, 64], F32)
            b_wA, b_hT, b_u, b_ks, b_vs = P.b("wA"), P.b("hT"), P.b("ubf"), P.b("ksT"), P.b("vs")
            b_x32, b_sqb, b_rsf, b_xbb, b_t1, b_t2 = (P.b(n) for n in ("x32", "sqb", "rsf", "xbb", "t1f", "t2f"))
            b_sq128, b_ssq, b_knr = P.b("sq128"), P.b("ssq"), P.b("knr")
            b_xu, b_xk, b_xv, b_xug, b_xkg, b_xvg = (P.b(n) for n in ("xu", "xk", "xv", "xug", "xkg", "xvg"))
            P.dma("sp", knr[:], knrow[l], W=[b_knr])
            for (d0, s0_, n_) in ((0, 0, 512), (512, 1024, 128), (640, 1792, 128), (768, 1152, 128), (896, 1920, 128)):
                P.dma("pool", wA[:, :, d0:d0 + n_], wv[:, :, s0_:s0_ + n_], W=[b_wA])

            def rmsnorm_fm(bank, N, gcol, out_ap, Wb):
                P.op("act", "activation", R=[psb[bank]], W=[b_x32], out=x32[:, :N], in_=ps[bank][:, :N], func=AF.Copy)
                P.op("act", "activation", R=[b_x32], W=[b_sqb], out=sqb[:, :N], in_=x32[:, :N], func=AF.Square)
                P.mm([dict(out=ps[7][:, :N], lhsT=cm[:, HMEAN, :], rhs=sqb[:, :N], start=True, stop=True)],
                     R=[b_sqb, Bc], W=[psb[7]])
                P.op("act", "activation", R=[psb[7], Bc], W=[b_rsf], out=rsf[:, :N], in_=ps[7][:, :N], func=AF.Ln,
                     bias=cst[:, 1:2], scale=1.0)
                P.op("act", "activation", R=[b_rsf], W=[b_rsf], out=rsf[:, :N], in_=rsf[:, :N], func=AF.Exp, scale=-0.5)
                P.op("dve", "scalar_tensor_tensor", R=[b_x32, b_rsf, Blay], W=Wb, out=out_ap, in0=x32[:, :N], scalar=gcol,
                     in1=rsf[:, :N], op0=MUL, op1=MUL)

            def rope_fm(N, pos0, out_ap, Wb):
                P.op("act", "activation", R=[b_x32], W=[b_xbb], out=xbb[:, :N], in_=x32[:, :N], func=AF.Copy)
                P.mm([dict(out=ps[7][:, :N], lhsT=cm[:, ROT, :], rhs=xbb[:, :N], start=True, stop=True)],
                     R=[b_xbb, Bc], W=[psb[7]])
                P.op("dve", "tensor_tensor", R=[b_x32, Bc], W=[b_t1], out=t1f[:, :N], in0=x32[:, :N],
                     in1=ropeT[:, 0, pos0:pos0 + N], op=MUL)
                P.op("dve", "tensor_tensor", R=[psb[7], Bc], W=[b_t2], out=t2f[:, :N], in0=ps[7][:, :N],
                     in1=ropeT[:, 1, pos0:pos0 + N], op=MUL)
                P.op("dve", "tensor_tensor", R=[b_t1, b_t2], W=Wb, out=out_ap, in0=t1f[:, :N], in1=t2f[:, :N], op=ADD)
            kvp["rms"], kvp["rope"] = rmsnorm_fm, rope_fm

            for c in range(3):
                k = k_of[c]
                cols = slice(512 * c, 512 * c + 512)
                for ft in range(8):
                    P.op("dve", "tensor_scalar", R=[xb(ft, c), Bmod], W=[b_hT], out=hT[:, ft, :], in0=xT[:, ft, cols],
                         scalar1=modv[:, 8 + ft, k:k + 1], scalar2=modv[:, ft, k:k + 1], op0=MUL, op1=ADD)
                if c == 0:
                    dump("hT%d" % l, hT[:, 0, :], [128, 512], [b_hT], BF16)
                for m in range(4):
                    bank = nb()
                    P.mm([dict(out=ps[bank][:], lhsT=wA[:, kt, 128 * m:128 * m + 128], rhs=hT[:, kt, :],
                               start=(kt == 0), stop=(kt == 7)) for kt in range(8)], R=[b_wA, b_hT], W=[psb[bank]])
                    u_dst = ubf[:, m, :] if c == 0 else ubs[:, m, 512 * (c - 1):512 * c]
                    P.op("act", "activation", R=[psb[bank]], W=[b_u], out=u_dst, in_=ps[bank][:], func=AF.Copy)
                if c > 0:
                    P.dma("sp", xu.rearrange("(m p) t -> p m t", p=128)[:, :, 512 * (c - 1):512 * c], ubs[:, :, 512 * (c - 1):512 * c],
                          R=[b_u], W=[b_xu])
                if c == 2:
                    P.coll(xu, xu_g, R=[b_xu], W=[b_xug])
                for br, off in ((0, 512), (1, 640)):
                    bank = nb()
                    P.mm([dict(out=ps[bank][:], lhsT=wA[:, kt, off:off + 128], rhs=hT[:, kt, :],
                               start=(kt == 0), stop=(kt == 7)) for kt in range(8)], R=[b_wA, b_hT], W=[psb[bank]])
                    if c == 0:
                        if br == 0:
                            P.op("act", "activation", R=[psb[bank]], W=[Bkp], out=kpT[:, 0, :], in_=ps[bank][:], func=AF.Copy)
                        else:
                            rmsnorm_fm(bank, 512, qknT[:, 1:2], kpT[:, 1, :], [Bkp])
                    else:
                        p0 = 512 * (c - 1)
                        if br == 0:
                            P.op("act", "activation", R=[psb[bank]], W=[b_x32], out=x32[:], in_=ps[bank][:], func=AF.Copy)
                        else:
                            rmsnorm_fm(bank, 512, qknT[:, 1:2], x32[:], [b_x32])
                        rope_fm(512, p0, ksT[:, br, p0:p0 + 512], [b_ks])
                ncol, coff = (512, 512) if c == 0 else (256, 768)
                for tt in range(4):
                    bank = nb()
                    P.mm([dict(out=ps[bank][:, 0:ncol], lhsT=hT[:, kt, 128 * tt:128 * tt + 128], rhs=wA[:, kt, coff:1024],
                               start=(kt == 0), stop=(kt == 7)) for kt in range(8)], R=[b_wA, b_hT], W=[psb[bank]])
                    pb = ps[bank]
                    if c == 0:
                        s_, t_ = tt // 2, tt % 2
                        ct = ctok[tt % 2]
                        b_ct = P.b("ctok", tt % 2)
                        P.op("act", "activation", R=[psb[bank]], W=[b_ct], out=ct[:, 0, :], in_=pb[:, 0:128], func=AF.Copy)
                        P.op("act", "activation", R=[psb[bank]], W=[b_ct], out=ct[:, 1, :], in_=pb[:, 256:384], func=AF.Copy)
                        P.op("act", "activation", R=[psb[bank]], W=[b_ct], out=ct[:, 3, :], in_=pb[:, 384:512], func=AF.Copy)
                        P.op("act", "activation", R=[psb[bank]], W=[b_sq128], out=sq128[:], in_=pb[:, 128:256], func=AF.Square)
                        P.op("dve", "reduce_sum", R=[b_sq128], W=[b_ssq], out=ssq[:], in_=sq128[:].rearrange("p (g d) -> p g d", g=2),
                             axis=mybir.AxisListType.X)
                        P.op("act", "activation", R=[b_ssq, Bc], W=[b_ssq], out=ssq[:], in_=ssq[:], func=AF.Sqrt,
                             bias=cst[:, 1:2], scale=1.0 / 64)
                        P.op("dve", "reciprocal", R=[b_ssq], W=[b_ssq], out=ssq[:], in_=ssq[:])
                        for g in range(2):
                            P.op("dve", "scalar_tensor_tensor", R=[psb[bank], b_ssq, b_knr], W=[b_ct],
                                 out=ct[:, 2, 64 * g:64 * g + 64], in0=pb[:, 128 + 64 * g:192 + 64 * g], scalar=ssq[:, g:g + 1],
                                 in1=knr[:], op0=MUL, op1=MUL)
                        P.dma("sp", co[s_, l][:, 128 * t_:128 * t_ + 128, :].rearrange("f t c -> t f c"), ct[:], R=[b_ct], is_out=True)
                        P.op("dve", "tensor_copy", R=[psb[bank]], W=[Bvp], out=vp[:, tt, 0, :, 0:64],
                             in_=pb[:, 256:384].rearrange("p (g d) -> p g d", g=2))
                        P.op("dve", "tensor_copy", R=[psb[bank]], W=[Bvp], out=vp[:, tt, 1, :, 0:64],
                             in_=pb[:, 384:512].rearrange("p (g d) -> p g d", g=2))
                    else:
                        P.op("act", "activation", R=[psb[bank]], W=[b_vs], out=vs[:, 4 * (c - 1) + tt, :], in_=pb[:, 0:256], func=AF.Copy)
                if c > 0:
                    p0 = 512 * (c - 1)
                    P.dma("sp", xk.rearrange("(b p) t -> p b t", p=128)[:, :, p0:p0 + 512], ksT[:, :, p0:p0 + 512], R=[b_ks], W=[b_xk])
                    P.dma("sp", xv.rearrange("(t p) c -> p t c", p=128)[:, 4 * (c - 1):4 * c, :], vs[:, 4 * (c - 1):4 * c, :], R=[b_vs], W=[b_xv])
            dump("ubf%d" % l, ubf[:, 0, :], [128, 512], [b_u], BF16)
            dump("ubs%d" % l, ubs[:, 0, :], [128, 1024], [b_u], BF16)
            dump("kpT%d" % l, kpT[:].rearrange("p a t -> p (a t)"), [128, 1024], [Bkp], BF16)
            dump("ksT%d" % l, ksT[:].rearrange("p a t -> p (a t)"), [128, 2048], [b_ks], BF16)
            late_colls = [lambda: P.coll(xk, xk_g, R=[b_xk], W=[b_xkg]), lambda: P.coll(xv, xv_g, R=[b_xv], W=[b_xvg])]
            if stop_after == "A":
                P.barrier()
                A.release(mLay)
                return
            P.barrier()
            A.release(mA)

            tabE = [A.alloc("tabE", [128, 2, 512], F32) for _ in range(4)]
            tabD = [A.alloc("tabD", [128, 2, 512], F32) for _ in range(4)]
            ang = A.alloc("ang", [128, 512], F32)
            g1 = A.alloc("g1", [128, 512], F32)
            g2 = A.alloc("g2", [128, 512], F32)
            kI = nc.alloc_sbuf_tensor_at("kIalias%d" % l, [128, 512], I32, offset=A.last_off)
            q1 = A.alloc("q1", [128, 512], BF16)
            q2 = A.alloc("q2", [128, 512], BF16)
            mre = A.alloc("mre", [128, 512], F32)
            mim = A.alloc("mim", [128, 512], F32)
            zre2 = [A.alloc("zre", [128, 512], F32) for _ in range(2)]
            zim2 = [A.alloc("zim", [128, 512], F32) for _ in range(2)]
            b_zre2 = [P.b("zre", i_) for i_ in range(2)]
            b_zim2 = [P.b("zim", i_) for i_ in range(2)]
            b_q1, b_q2 = P.b("q1"), P.b("q2")
            jobn = [0]
            p1 = A.alloc("p1", [128, 512], F32)
            p2 = A.alloc("p2", [128, 512], F32)
            sre = A.alloc("sre", [128, 512], BF16)
            nsi = A.alloc("nsi", [128, 512], BF16)
            useq = A.alloc("useq", [128, 4096], BF16)
            accp = A.alloc("accp", [128, 4, 512], F32)
            accs = A.alloc("accs", [128, 4096], F32)
            fin = A.alloc("fin", [128, 2, 2, 2, 16], F32)
            finT = A.alloc("finT", [128, 128], F32)
            tt2 = A.alloc("tt2", [128, 2], F32)
            (b_ang, b_g1, b_g2, b_mre, b_mim, b_zreX, b_zimX, b_p1, b_p2, b_sre, b_nsi, b_urp, b_useq, b_urs, b_accp,
             b_accs, b_fin, b_tt2, b_xy, b_xyg, b_ypd) = (P.b(n) for n in (
                 "ang", "g1", "g2", "mre", "mim", "zre", "zim", "p1", "p2", "sre", "nsi", "urp", "useq", "urs", "accp",
                 "accs", "fin", "tt2", "xy", "xyg", "ypd"))
            del b_urp, b_urs
            b_tE = [P.b("tabE", a) for a in range(4)]
            b_tD = [P.b("tabD", a) for a in range(4)]
            P.op("pool", "memset", W=[b_accp], ap=accp[:], constant=0.0)
            P.op("pool", "memset", W=[b_accs], ap=accs[:], constant=0.0)
            for rr_ in range(4):
                dyn_dma("sp", lambda e, rr_=rr_, useq=useq: (useq[:, 1024 * rr_:1024 * rr_ + 1024],
                                                  xu_g[bass.ds(pid4(e) * 128 + 512 * rr_, 128), :]),
                        R=[b_xug], W=[b_useq])
            if stop_after == "B1":
                P.barrier()
                A.release(mAB)
                A.release(mLay)
                return

            def gen_table(a, e_, jsel, part=0):
                E, D = tabE[a], tabD[a]
                W_ = 256 if jsel else 512
                th_ = CS[:, THR, e_:e_ + 1]
                if part in (0, 1):
                    P.op("dve", "tensor_scalar", R=[b_cs, Bc], W=[b_ang], out=ang[:, :W_], in0=jT[:, jsel, :W_], scalar1=th_, scalar2=None, op0=MUL)
                    P.op("dve", "tensor_scalar", R=[b_ang], W=[b_g1], out=g1[:, :W_], in0=ang[:, :W_], scalar1=1.0 / TWO_PI, scalar2=None, op0=MUL)
                    P.op("dve", "tensor_copy", R=[b_g1], W=[b_g2], out=kI[:, :W_], in_=g1[:, :W_])
                    P.op("dve", "tensor_copy", R=[b_g2], W=[b_g1], out=g1[:, :W_], in_=kI[:, :W_])
                    P.op("dve", "scalar_tensor_tensor", R=[b_g1, b_ang], W=[b_ang], out=ang[:, :W_], in0=g1[:, :W_], scalar=-TWO_PI, in1=ang[:, :W_],
                         op0=MUL, op1=ADD)
                    P.op("dve", "tensor_scalar", R=[b_ang], W=[b_ang], out=ang[:, :W_], in0=ang[:, :W_], scalar1=PI_LO, scalar2=-PI_LO, op0=ALU.min, op1=ALU.max)
                    P.op("act", "activation", R=[b_ang], W=[b_tD[a]], out=D[:, 1, :W_], in_=ang[:, :W_], func=AF.Sin)
                    P.op("act", "activation", R=[b_ang], W=[b_g1], out=g1[:, :W_], in_=ang[:, :W_], func=AF.Abs)
                    P.op("act", "activation", R=[b_g1, Bc], W=[b_tD[a]], out=D[:, 0, :W_], in_=g1[:, :W_], func=AF.Sin, scale=-1.0, bias=cst[:, 2:3])
                    if jsel:
                        P.op("act", "activation", R=[b_tD[a]], W=[b_tD[a]], out=D[:, :, 256:512], in_=D[:, :, 0:256], func=AF.Copy)
                if part == 1:
                    return
                cre_, cim_ = CS[:, CFRE, e_:e_ + 1], CS[:, CFIM, e_:e_ + 1]
                P.op("dve", "tensor_scalar", R=[b_tD[a], b_cs], W=[b_g2], out=g2[:, :W_], in0=D[:, 1, :W_], scalar1=cim_, scalar2=None, op0=MUL)
                P.op("dve", "scalar_tensor_tensor", R=[b_tD[a], b_cs, b_g2], W=[b_tE[a]], out=E[:, 0, :W_], in0=D[:, 0, :W_], scalar=cre_,
                     in1=g2[:, :W_], op0=MUL, op1=ADD)
                P.op("dve", "tensor_scalar", R=[b_tD[a], b_cs], W=[b_g2], out=g2[:, :W_], in0=D[:, 1, :W_], scalar1=cre_, scalar2=None, op0=MUL)
                P.op("dve", "scalar_tensor_tensor", R=[b_tD[a], b_cs, b_g2], W=[b_tE[a]], out=E[:, 1, :W_], in0=D[:, 0, :W_], scalar=cim_,
                     in1=g2[:, :W_], op0=MUL, op1=SUB)
                if jsel:
                    P.op("act", "activation", R=[b_tE[a]], W=[b_tE[a]], out=E[:, :, 256:512], in_=E[:, :, 0:256], func=AF.Copy)

            def pair_job(a, e_, usrc, Ru, segs, own_j, bC):
                E, D = tabE[a], tabD[a]
                zi = jobn[0] % 2
                jobn[0] += 1
                zre, zim, b_zre, b_zim = zre2[zi], zim2[zi], b_zre2[zi], b_zim2[zi]
                bA, bB = (4, 5) if zi == 0 else (0, 1)
                for bnk, ri_ in ((bA, 0), (bB, 1)):
                    P.mm([dict(out=ps[bnk][:, c0_:c0_ + src_.shape[1]], lhsT=WB[:, e_, ri_, :], rhs=src_, start=True, stop=True)
                          for (c0_, src_) in usrc], R=[b_WB] + Ru, W=[psb[bnk]])
                RT = [b_tE[a]]
                P.op("dve", "tensor_tensor", R=[psb[bA]] + RT, W=[b_p1], out=p1[:], in0=ps[bA][:], in1=E[:, 0, :], op=MUL)
                P.op("dve", "tensor_tensor", R=[psb[bB]] + RT, W=[b_p2], out=p2[:], in0=ps[bB][:], in1=E[:, 1, :], op=MUL)
                P.op("dve", "tensor_tensor", R=[psb[bB]] + RT, W=[b_mim], out=mim[:], in0=ps[bB][:], in1=E[:, 0, :], op=MUL)
                P.op("dve", "tensor_tensor", R=[b_p1, b_p2], W=[b_mre], out=mre[:], in0=p1[:], in1=p2[:], op=SUB)
                P.op("dve", "tensor_tensor", R=[psb[bA]] + RT, W=[b_p1], out=p1[:], in0=ps[bA][:], in1=E[:, 1, :], op=MUL)
                P.op("dve", "tensor_tensor", R=[b_p1, b_mim], W=[b_mim], out=mim[:], in0=mim[:], in1=p1[:], op=ADD)
                rcol = CS[:, RR, e_:e_ + 1]
                for (c0, c1) in segs:
                    n_ = c1 - c0
                    i_re = 0.0 if own_j is None else car[:, 0, own_j:own_j + 1]
                    i_im = 0.0 if own_j is None else car[:, 1, own_j:own_j + 1]
                    P.op("dve", "tensor_tensor_scan", R=[b_mre, b_cs, b_car], W=[b_zre], out=zre[:, c0:c1],
                         data0=rcol.to_broadcast([128, n_]), data1=mre[:, c0:c1], initial=i_re, op0=MUL, op1=ADD)
                    P.op("dve", "tensor_tensor_scan", R=[b_mim, b_cs, b_car], W=[b_zim], out=zim[:, c0:c1],
                         data0=rcol.to_broadcast([128, n_]), data1=mim[:, c0:c1], initial=i_im, op0=MUL, op1=ADD)
                ID = AF.Identity
                if own_j is not None:
                    c5, s5 = CS[:, C512, e_:e_ + 1], CS[:, S512, e_:e_ + 1]
                    zlr, zli = zre[:, 511:512], zim[:, 511:512]
                    P.op("act", "activation", R=[b_zim, b_cs], W=[b_tt2], out=tt2[:, 0:1], in_=zli, func=ID, scale=s5)
                    P.op("act", "activation", R=[b_tt2], W=[b_tt2], out=tt2[:, 0:1], in_=tt2[:, 0:1], func=ID, scale=-1.0)
                    P.op("act", "activation", R=[b_zre, b_cs, b_tt2], W=[b_car], out=car[:, 0, own_j:own_j + 1], in_=zlr, func=ID,
                         scale=c5, bias=tt2[:, 0:1])
                    P.op("act", "activation", R=[b_zim, b_cs], W=[b_tt2], out=tt2[:, 1:2], in_=zli, func=ID, scale=c5)
                    P.op("act", "activation", R=[b_zre, b_cs, b_tt2], W=[b_car], out=car[:, 1, own_j:own_j + 1], in_=zlr, func=ID,
                         scale=s5, bias=tt2[:, 1:2])
                else:
                    cc, ss = D[:, 0, 255:256], D[:, 1, 255:256]
                    for s_ in range(2):
                        zfr, zfi = zre[:, 256 * s_ + 255:256 * s_ + 256], zim[:, 256 * s_ + 255:256 * s_ + 256]
                        P.op("act", "activation", R=[b_zim, b_tD[a]], W=[b_tt2], out=tt2[:, 0:1], in_=zfi, func=ID, scale=ss)
                        P.op("act", "activation", R=[b_tt2], W=[b_tt2], out=tt2[:, 0:1], in_=tt2[:, 0:1], func=ID, scale=-1.0)
                        P.op("act", "activation", R=[b_zre, b_tD[a], b_tt2], W=[b_fin], out=fin[:, s_:s_ + 1, e_ // 16, 0, e_ % 16],
                             in_=zfr, func=ID, scale=cc, bias=tt2[:, 0:1])
                        P.op("act", "activation", R=[b_zim, b_tD[a]], W=[b_tt2], out=tt2[:, 1:2], in_=zfi, func=ID, scale=cc)
                        P.op("act", "activation", R=[b_zre, b_tD[a], b_tt2], W=[b_fin], out=fin[:, s_:s_ + 1, e_ // 16, 1, e_ % 16],
                             in_=zfr, func=ID, scale=ss, bias=tt2[:, 1:2])
                RD = [b_tD[a]]

                def stage2(a=a, e_=e_, zre=zre, zim=zim, b_zre=b_zre, b_zim=b_zim, D=D, RD=RD, bC=bC):
                    P.op("pool", "tensor_tensor", R=[b_zre] + RD, W=[b_sre], out=sre[:], in0=zre[:], in1=D[:, 0, :], op=MUL)
                    P.op("pool", "tensor_tensor", R=[b_zim] + RD, W=[b_q1], out=q1[:], in0=zim[:], in1=D[:, 1, :], op=MUL)
                    P.op("pool", "tensor_tensor", R=[b_zre] + RD, W=[b_nsi], out=nsi[:], in0=zre[:], in1=D[:, 1, :], op=MUL)
                    P.op("pool", "tensor_tensor", R=[b_zim] + RD, W=[b_q2], out=q2[:], in0=zim[:], in1=D[:, 0, :], op=MUL)
                    oc = ps[bC][32 * a:32 * a + 32, :]
                    tp = (0, 32 * a)
                    P.mm([dict(out=oc, lhsT=WC[:, e_, 0, :], rhs=sre[:], start=True, stop=False, tile_position=tp),
                          dict(out=oc, lhsT=WCn[:, e_, :], rhs=q1[:], start=False, stop=False, tile_position=tp),
                          dict(out=oc, lhsT=WC[:, e_, 1, :], rhs=nsi[:], start=False, stop=False, tile_position=tp),
                          dict(out=oc, lhsT=WC[:, e_, 1, :], rhs=q2[:], start=False, stop=True, tile_position=tp)],
                         R=[b_WC, b_sre, b_nsi, b_q1, b_q2], W=[psb[bC]])
                pend.append(stage2)

            pend = []

            def flush(keep):
                while len(pend) > keep:
                    pend.pop(0)()

            nq = [0]
            pj = [(d_, quad, a) for d_ in range(2) for quad in range(4) for a in range(4)]
            gen_table(0, 0, 1)
            for i_, (d_, quad, a) in enumerate(pj):
                if a == 0:
                    bC = 6 + (nq[0] % 2)
                    nq[0] += 1
                e_ = 16 * d_ + 4 * quad + a
                if i_ + 1 < len(pj):
                    dn, qn, an = pj[i_ + 1]
                    gen_table(an, 16 * dn + 4 * qn + an, 1, part=1)
                if d_ == 0:
                    src = [(0, ubf[:, quad, 0:512])]
                else:
                    src = [(256 * s_, ubf[:, quad, 256 * s_:256 * s_ + 256][:, ::-1]) for s_ in range(2)]
                pair_job(a, e_, src, [b_u], [(0, 256), (256, 512)], None, bC)
                flush(1)
                if i_ + 1 < len(pj):
                    gen_table(an, 16 * dn + 4 * qn + an, 1, part=2)
                if a == 3:
                    def acc_p(d_=d_, quad=quad, bC=bC):
                        for s_ in range(2):
                            av = accp[:, quad, 256 * s_:256 * s_ + 256]
                            if d_:
                                av = av[:, ::-1]
                            P.op("dve", "tensor_tensor", R=[psb[bC], b_accp], W=[b_accp], out=av, in0=av,
                                 in1=ps[bC][:, 256 * s_:256 * s_ + 256], op=ADD)
                    pend.append(acc_p)
                    if nq[0] in (2, 5):
                        pend.append(late_colls.pop(0))
            flush(0)
            if stop_after == "B2":
                P.barrier()
                A.release(mAB)
                A.release(mLay)
                return
            P.dma("sp", yp_d, accp[:], R=[b_accp], W=[b_ypd])
            P.mm([dict(out=ps[4][:, 0:128], lhsT=fin[:].rearrange("p s d r e -> p (s d r e)"), rhs=identF[:], start=True, stop=True)],
                 R=[b_fin, Bc], W=[psb[4]])
            b_finT = P.b("finT")
            P.op("act", "activation", R=[psb[4]], W=[b_finT], out=finT[:], in_=ps[4][:, 0:128], func=AF.Copy)
            for s_ in range(2):
                P.dma("sp", so[s_, l].rearrange("d r (pi gl) p -> (d r pi) (gl p)", gl=2), finT[64 * s_:64 * s_ + 64, :],
                      R=[b_finT], is_out=True)
            dump("accp%d" % l, accp[:].rearrange("p a t -> p (a t)"), [128, 2048], [b_accp])
            if stop_after == "B3":
                P.barrier()
                A.release(mAB)
                A.release(mLay)
                return
            mgen = mod_vectors_gen(l + 1) if l + 1 < nlayers else iter(())
            for d_ in range(2):
                if "nosamp" in DBGF:
                    break
                for a in range(4):
                    gen_table(a, 32 + 4 * d_ + a, 0)
                for ch in range(8):
                    if "1ch" in DBGF and ch:
                        break
                    bC = 6 + (nq[0] % 2)
                    nq[0] += 1
                    for a in range(4):
                        if d_ == 0:
                            src = [(0, useq[:, 512 * ch:512 * ch + 512])]
                        else:
                            src = [(0, useq[:, 4096 - 512 * ch - 512:4096 - 512 * ch][:, ::-1])]
                        pair_job(a, 32 + 4 * d_ + a, src, [b_useq], [(0, 512)], 4 * d_ + a, bC)
                        flush(1)
                        next(mgen, None)

                    def acc_s(d_=d_, ch=ch, bC=bC):
                        if d_ == 0:
                            av = accs[:, 512 * ch:512 * ch + 512]
                        else:
                            av = accs[:, 4096 - 512 * ch - 512:4096 - 512 * ch][:, ::-1]
                        P.op("dve", "tensor_tensor", R=[psb[bC], b_accs], W=[b_accs], out=av, in0=av, in1=ps[bC][:], op=ADD)
                    pend.append(acc_s)
                    if d_ == 1 and ch % 2 == 1:
                        def xfer(tb=3 - ch // 2):
                            P.dma("sp", xy[tb], accs[:, 1024 * tb:1024 * tb + 1024], R=[b_accs], W=[b_xy])
                            P.coll(xy[tb], xy_g[512 * tb:512 * tb + 512, :], R=[b_xy], W=[b_xyg])
                        pend.append(xfer)
            flush(0)
            for _ in mgen:
                pass
            dump("accs%d" % l, accs[:], [128, 4096], [b_accs])
            P.barrier()
            A.release(mAB)
            if stop_after == "B":
                A.release(mLay)
                return

            mC = A.mark()
            ytokL = [A.alloc("ytokL", [128, 256], F32) for _ in range(2)] if l == nlayers - 1 else None
            kgT = A.alloc("kgT", [128, 4352], BF16)
            vgA = A.alloc("vgA", [128, 34, 2, 65], BF16)
            kwT = A.alloc("kwT", [128, 1536], BF16)
            vwA = A.alloc("vwA", [128, 12, 2, 65], BF16)
            wglu = A.alloc("wglu", [128, 4, 512], BF16)
            ctb = A.alloc("ctb", [128, 2, 2, 128], BF16)
            b_kg, b_vg, b_kw, b_vw, b_wglu, b_ctb = (P.b(n) for n in ("kgT", "vgA", "kwT", "vwA", "wglu", "ctb"))
            P.dma("pool", wglu[:], w_glu[l].rearrange("(kt p) c -> p kt c", p=128), W=[b_wglu])

            def kv_assembly():
                P.op("pool", "memset", W=[b_vg], ap=vgA[:], constant=1.0)
                P.op("pool", "memset", W=[b_vw], ap=vwA[:], constant=1.0)
                P.dma("sp", kgT[:, 0:4096].rearrange("p (r t) -> p r t", r=4),
                      xk_g.rearrange("(r two p) t -> p r two t", two=2, p=128)[:, :, 1, :], R=[b_xkg], W=[b_kg])
                for g_ in range(2):
                    P.dma("sp", vgA[:, 0:32, g_, 0:64], xv_g[:, 128 + 64 * g_:192 + 64 * g_].rearrange("(t p) d -> p t d", p=128),
                          R=[b_xvg], W=[b_vg])
                dyn_dma("sp", lambda e, kwT=kwT: (kwT[:, 128:1152], xk_g[bass.ds(pid4(e) * 256, 128), :]), R=[b_xkg], W=[b_kw])
                dyn_dma("sp", lambda e, kwT=kwT: (kwT[:, 0:128], xk_g[bass.ds(((pid4(e) + 3) % 4) * 256, 128), 896:1024]), R=[b_xkg], W=[b_kw])
                dyn_dma("sp", lambda e, kwT=kwT: (kwT[:, 1152:1280], xk_g[bass.ds(((pid4(e) + 1) % 4) * 256, 128), 0:128]), R=[b_xkg], W=[b_kw])
                for g_ in range(2):
                    dyn_dma("sp", lambda e, g_=g_, vwA=vwA: (vwA[:, 1:9, g_, 0:64],
                                                    xv_g[bass.ds(pid4(e) * 1024, 1024), 64 * g_:64 * g_ + 64].rearrange("(t p) d -> p t d", p=128)),
                            R=[b_xvg], W=[b_vw])
                dyn_dma("sp", lambda e, vwA=vwA: (vwA[:, 0, :, 0:64],
                                         xv_g[bass.ds(((pid4(e) + 3) % 4) * 1024 + 896, 128), 0:128].rearrange("p (g d) -> p g d", g=2)),
                        R=[b_xvg], W=[b_vw])
                dyn_dma("sp", lambda e, vwA=vwA: (vwA[:, 9, :, 0:64],
                                         xv_g[bass.ds(((pid4(e) + 1) % 4) * 1024, 128), 0:128].rearrange("p (g d) -> p g d", g=2)),
                        R=[b_xvg], W=[b_vw])
                for g_ in range(2):
                    P.dma("pool", vwA[:, 10:12, g_, 0:64], cv[l, 0][:, 64 * g_:64 * g_ + 64].rearrange("(t p) d -> p t d", p=128), W=[b_vw])
                    P.dma("pool", vgA[:, 32:34, g_, 0:64], cv[l, 1][:, 64 * g_:64 * g_ + 64].rearrange("(t p) d -> p t d", p=128), W=[b_vg])
                P.dma("pool", ctb[:], ck[l].rearrange("b (t p) c -> p b t c", p=128), W=[b_ctb])
                for br in range(2):
                    for t_ in range(2):
                        P.mm([dict(out=ps[0][:, 0:128], lhsT=ctb[:, br, t_, :], rhs=cm[:, IDB, :], start=True, stop=True)],
                             R=[b_ctb, Bc], W=[psb[0]])
                        if br == 0:
                            P.op("act", "activation", R=[psb[0]], W=[b_kw], out=kwT[:, 1280 + 128 * t_:1408 + 128 * t_], in_=ps[0][:, 0:128], func=AF.Copy)
                        else:
                            P.op("act", "activation", R=[psb[0]], W=[b_kg], out=kgT[:, 4096 + 128 * t_:4224 + 128 * t_], in_=ps[0][:, 0:128], func=AF.Copy)


            _b2 = [0, 2]

            def nb2():
                _b2[0] = (_b2[0] + 1) % _b2[1]
                return _b2[0]

            wvd = w_down[l].rearrange("(kt p) c -> p kt c", p=128)
            wvu = w_up[l].rearrange("(kt p) c -> p kt c", p=128)
            wvo = w_out[l].rearrange("(kt p) c -> p kt c", p=128)
            wvb0 = w_br[l, 0].rearrange("(kt p) c -> p kt c", p=128)
            wvb1 = w_br[l, 1].rearrange("(h d) c -> d h c", d=64)
            wvb2 = w_br[l, 2].rearrange("(h d) c -> d h c", d=64)

            for c in range(3):
                k = k_of[c]
                cols = slice(512 * c, 512 * c + 512)
                p0 = 512 * (c - 1)
                if c == 1:
                    kv_assembly()
                mCh = A.mark()
                hT = A.alloc("hT", [128, 8, 512], BF16)
                for ft in range(8):
                    P.op("dve", "tensor_scalar", R=[xb(ft, c), Bmod], W=[b_hT], out=hT[:, ft, :], in0=xT[:, ft, cols],
                         scalar1=modv[:, 8 + ft, k:k + 1], scalar2=modv[:, ft, k:k + 1], op0=MUL, op1=ADD)
                mC12 = A.mark()
                ya = A.alloc("ya", [128, 4, 512], BF16)
                ywT = A.alloc("ywT", [64, 8, 512], BF16)
                ygT = A.alloc("ygT", [64, 8, 512], BF16)
                b_ya, b_yw, b_yg = P.b("ya"), P.b("ywT"), P.b("ygT")
                _b2[1] = 6
                cpend = []

                def cflush(keep):
                    while len(cpend) > keep:
                        cpend.pop(0)()
                mC1 = A.mark()
                wblk = [A.alloc("wblk", [128, 8, 512], BF16) for _ in range(2)]
                b_wblk = [P.b("wblk", i_) for i_ in range(2)]
                ssm_in = A.alloc("ssm_in", [128, 4, 512], F32)
                gb = A.alloc("gb", [128, 4, 512], BF16)
                qwT = A.alloc("qwT", [128, 4, 512], BF16)
                qgT = A.alloc("qgT", [128, 4, 512], BF16)
                x32 = A.alloc("x32", [128, 512], F32)
                sqb = A.alloc("sqb", [128, 512], BF16)
                rsf = A.alloc("rsf", [128, 512], F32)
                xbb = A.alloc("xbb", [128, 512], BF16)
                t1f = A.alloc("t1f", [128, 512], F32)
                t2f = A.alloc("t2f", [128, 512], F32)
                sgf = A.alloc("sgf", [128, 512], F32)
                pT = [A.alloc("pT", [128, 512], BF16) for _ in range(4)]
                drow = A.alloc("drow", [128, 512], F32)
                bcs = A.alloc("bcs", [64, 512], F32)
                b_ssm, b_gb, b_qw, b_qg, b_sgf, b_drow, b_bcs = (P.b(n) for n in ("ssm_in", "gb", "qwT", "qgT", "sgf", "drow", "bcs"))
                b_pT = [P.b("pT", i_) for i_ in range(4)]
                if c == 0:
                    P.dma("sp", ssm_in[:], yp_d, R=[b_ypd], W=[b_ssm])
                else:
                    dyn_dma("sp", lambda e, p0=p0, ssm_in=ssm_in: (ssm_in[:], xy_g[bass.ds(pid4(e) * 512, 512), p0:p0 + 512].rearrange("(q p) t -> p q t", p=128)),
                            R=[b_xyg], W=[b_ssm])
                for bi, kind in enumerate(("u", "qw", "qg")):
                    w = wblk[bi % 2]
                    bw = b_wblk[bi % 2]
                    if kind == "u":
                        P.dma("pool", w[:], wv[:, :, 0:512], W=[bw])
                    else:
                        c0 = 512 if kind == "qw" else 1280
                        for j_ in range(4):
                            for hf in range(2):
                                P.dma("pool", w[:, :, 128 * j_ + 64 * hf:128 * j_ + 64 * hf + 64],
                                      wv[:, :, c0 + 256 * hf + 64 * j_:c0 + 256 * hf + 64 * j_ + 64], W=[bw])
                    for m in range(4):
                        bank = nb2()
                        P.mm([dict(out=ps[bank][:], lhsT=w[:, kt, 128 * m:128 * m + 128], rhs=hT[:, kt, :],
                                   start=(kt == 0), stop=(kt == 7)) for kt in range(8)], R=[bw, b_hT], W=[psb[bank]])
                        def chain(kind=kind, m=m, bank=bank):
                            if kind == "u":
                                P.op("dve", "scalar_tensor_tensor", R=[psb[bank], Blay, b_ssm], W=[b_x32], out=x32[:], in0=ps[bank][:],
                                     scalar=dskT[:, m:m + 1], in1=ssm_in[:, m, :], op0=MUL, op1=ADD)
                                P.op("act", "activation", R=[b_x32], W=[b_t1], out=t1f[:], in_=x32[:], func=AF.Square)
                                P.op("dve", "tensor_scalar", R=[b_t1], W=[b_t1], out=t1f[:], in0=t1f[:], scalar1=0.044715, scalar2=1.0, op0=MUL, op1=ADD)
                                P.op("dve", "tensor_tensor", R=[b_t1, b_x32], W=[b_t1], out=t1f[:], in0=t1f[:], in1=x32[:], op=MUL)
                                P.op("act", "activation", R=[b_t1], W=[b_t2], out=t2f[:], in_=t1f[:], func=AF.Sigmoid, scale=1.5957691216057308)
                                P.op("dve", "tensor_tensor", R=[b_t2, b_x32], W=[b_gb], out=gb[:, m, :], in0=x32[:], in1=t2f[:], op=MUL)
                            elif kind == "qw":
                                if c == 0:
                                    P.op("act", "activation", R=[psb[bank]], W=[b_qw], out=qwT[:, m, :], in_=ps[bank][:], func=AF.Copy)
                                else:
                                    P.op("act", "activation", R=[psb[bank]], W=[b_x32], out=x32[:], in_=ps[bank][:], func=AF.Copy)
                                    rope_fm(512, p0, qwT[:, m, :], [b_qw])
                            else:
                                if c == 0:
                                    rmsnorm_fm(bank, 512, qknT[:, 0:1], qgT[:, m, :], [b_qg])
                                else:
                                    rmsnorm_fm(bank, 512, qknT[:, 0:1], x32[:], [b_x32])
                                    rope_fm(512, p0, qgT[:, m, :], [b_qg])
                        cpend.append(chain)
                        cflush(1)
                cflush(0)
                _b2[1] = 2
                for m in range(4):
                    bank = nb2()
                    P.mm([dict(out=ps[bank][:], lhsT=wglu[:, kt, 128 * m:128 * m + 128], rhs=gb[:, kt, :],
                               start=(kt == 0), stop=(kt == 3)) for kt in range(4)], R=[b_wglu, b_gb], W=[psb[bank]])
                    P.op("act", "activation", R=[psb[bank]], W=[b_sgf], out=sgf[:], in_=ps[bank][:], func=AF.Sigmoid)
                    P.op("dve", "tensor_tensor", R=[b_sgf, b_gb], W=[b_ya], out=ya[:, m, :], in0=gb[:, m, :], in1=sgf[:], op=MUL)
                if c == 1:
                    dump("ya%d" % l, ya[:, 0, :], [128, 512], [b_ya], BF16)
                    dump("qgT%d" % l, qgT[:, 0, :], [128, 512], [b_qg], BF16)

                acnt = [0]
                fin_pend = []

                def attn_core(N, qsrc, Rq, ktiles, bO, c0):
                    nt = len(ktiles)
                    LA = 2
                    for ti in range(nt + LA):
                        if ti < nt:
                            kT_ap, Rk, v_ap, Rv, mask = ktiles[ti]
                            bS = ti % 4
                            pt, bp = pT[ti % 4], b_pT[ti % 4]
                            P.mm([dict(out=ps[bS][:, :N], lhsT=kT_ap, rhs=qsrc, start=True, stop=True)], R=Rk + Rq, W=[psb[bS]])
                            P.op("act", "activation", R=[psb[bS]], W=[bp], out=pt[:, :N], in_=ps[bS][:, :N], func=AF.Exp, scale=0.125)
                            if mask is not None:
                                P.op("dve", "tensor_tensor", R=[bp, Bc], W=[bp], out=pt[:, :N], in0=pt[:, :N], in1=mask, op=MUL)
                        if ti >= LA:
                            tj = ti - LA
                            kT_ap, Rk, v_ap, Rv, mask = ktiles[tj]
                            P.mm([dict(out=ps[bO][0:65, c0:c0 + N], lhsT=v_ap, rhs=pT[tj % 4][:, :N], start=(tj == 0), stop=(tj == nt - 1))],
                                 R=Rv + [b_pT[tj % 4]], W=[psb[bO]])

                def attn_fin(N, h, sink, bO, out_ap, Wb):
                    def f():
                        if sink:
                            P.op("dve", "tensor_scalar", R=[psb[bO], Blay], W=[b_drow], out=drow[64:65, :N], in0=ps[bO][64:65, :N],
                                 scalar1=esink[64:65, h:h + 1], scalar2=None, op0=ADD)
                        else:
                            P.op("dve", "tensor_copy", R=[psb[bO]], W=[b_drow], out=drow[64:65, :N], in_=ps[bO][64:65, :N])
                        P.op("act", "activation", R=[b_drow], W=[b_drow], out=drow[64:65, :N], in_=drow[64:65, :N], func=AF.Ln)
                        P.op("act", "activation", R=[b_drow], W=[b_drow], out=drow[64:65, :N], in_=drow[64:65, :N], func=AF.Exp, scale=-1.0)
                        P.mm([dict(out=ps[6][0:64, :N], lhsT=onesF[64:65, 0:64], rhs=drow[64:65, :N], start=True, stop=True)],
                             R=[b_drow, Bc], W=[psb[6]])
                        P.op("act", "activation", R=[psb[6]], W=[b_bcs], out=bcs[:, :N], in_=ps[6][0:64, :N], func=AF.Copy)
                        P.op("dve", "tensor_tensor", R=[psb[bO], b_bcs], W=Wb, out=out_ap, in0=ps[bO][0:64, :N], in1=bcs[:, :N], op=MUL)
                    fin_pend.append(f)
                    while len(fin_pend) > 1:
                        fin_pend.pop(0)()

                def next_bO():
                    acnt[0] += 1
                    return 4 + (acnt[0] % 2)

                for h in range(8):
                    g, j = h // 4, h % 4
                    pr = slice(64 * g, 64 * g + 64)
                    if c == 0:
                        for br, (qT_, bq_, oT_, bo_) in enumerate(((qwT, b_qw, ywT, b_yw), (qgT, b_qg, ygT, b_yg))):
                            bO = next_bO()
                            for s_ in range(2):
                                sc_ = slice(256 * s_, 256 * s_ + 256)
                                kts = [(kpT[pr, br, 256 * s_ + 128 * t_:256 * s_ + 128 * t_ + 128], [Bkp],
                                        vp[:, 2 * s_ + t_, br, g, :], [Bvp], None) for t_ in range(2)]
                                attn_core(256, qT_[pr, j, sc_], [bq_], kts, bO, 256 * s_)
                            attn_fin(512, h, br == 0, bO, oT_[:, h, :], [bo_])
                    else:
                        bO = next_bO()
                        for qb in range(4):
                            nl = 4 * (c - 1) + qb
                            qc = slice(128 * qb, 128 * qb + 128)
                            kts = [(kwT[pr, 128 * nl:128 * nl + 128], [b_kw], vwA[:, nl, g, :], [b_vw], cm[:, MLF if nl == 0 else ML, :]),
                                   (kwT[pr, 128 * nl + 128:128 * nl + 256], [b_kw], vwA[:, nl + 1, g, :], [b_vw], None),
                                   (kwT[pr, 128 * nl + 256:128 * nl + 384], [b_kw], vwA[:, nl + 2, g, :], [b_vw], cm[:, MRL if nl == 7 else MR, :]),
                                   (kwT[pr, 1280:1408], [b_kw], vwA[:, 10, g, :], [b_vw], None),
                                   (kwT[pr, 1408:1536], [b_kw], vwA[:, 11, g, :], [b_vw], None)]
                            attn_core(128, qwT[pr, j, qc], [b_qw], kts, bO, 128 * qb)
                        attn_fin(512, h, True, bO, ywT[:, h, :], [b_yw])
                        bO = next_bO()
                        kts = [(kgT[pr, 128 * t_:128 * t_ + 128], [b_kg], vgA[:, t_, g, :], [b_vg], None) for t_ in range(34)]
                        attn_core(512, qgT[pr, j, :], [b_qg], kts, bO, 0)
                        attn_fin(512, h, False, bO, ygT[:, h, :], [b_yg])
                while fin_pend:
                    fin_pend.pop(0)()
                if c == 1:
                    dump("ywT%d" % l, ywT[:, 0, :], [64, 512], [b_yw], BF16)
                    dump("ygT%d" % l, ygT[:, 0, :], [64, 512], [b_yg], BF16)
                if c == 0:
                    dump("ywTp%d" % l, ywT[:, 0, :], [64, 512], [b_yw], BF16)
                    dump("ygTp%d" % l, ygT[:, 0, :], [64, 512], [b_yg], BF16)
                P.barrier()
                A.release(mC1)

                _b2[1] = 4
                wg = [A.alloc("wg", [128, 8, 3, 256], BF16) for _ in range(2)]
                wbs = [A.alloc("wbs", [128, 4, 256], BF16) for _ in range(2)]
                wbw = [A.alloc("wbw", [64, 8, 256], BF16) for _ in range(2)]
                wbg = [A.alloc("wbg", [64, 8, 256], BF16) for _ in range(2)]
                b_wm = [P.b("wm2", i_) for i_ in range(2)]
                mT = A.alloc("mT", [128, 8, 512], BF16)
                sgf2 = [A.alloc("sgf", [128, 512], F32) for _ in range(2)]
                b_sgf2 = [P.b("sgf2", i_) for i_ in range(2)]
                tmp2 = [A.alloc("tmp2", [128, 512], F32) for _ in range(2)]
                b_tmp2 = [P.b("tmp2", i_) for i_ in range(2)]
                nsg = [0]
                accf = A.alloc("accf", [128, 512], F32)
                tmpf = A.alloc("tmpf", [128, 512], F32)
                wo = [A.alloc("wo", [128, 8, 256], BF16) for _ in range(2)]
                b_wo = [P.b("wo", i_) for i_ in range(2)]
                sqs = [A.alloc("sqs", [128, 512], BF16) for _ in range(2)]
                zbs = [A.alloc("zbs", [128, 512], BF16) for _ in range(2)]
                b_sq = [P.b("sqs", i_) for i_ in range(2)]
                b_zb = [P.b("zbs", i_) for i_ in range(2)]
                m2 = A.alloc("m2", [128, 512], F32)
                lt = A.alloc("lt", [128, 512], F32)
                b_mT, b_acc, b_tmp, b_m2, b_lt = (P.b(n) for n in ("mT", "accf", "tmpf", "m2", "lt"))

                def layer_norm(gsel):
                    for ft in range(8):
                        sq, zb = sqs[ft % 2], zbs[ft % 2]
                        P.op("act", "activation", R=[xb(ft, c)], W=[b_sq[ft % 2]], out=sq[:], in_=xT[:, ft, cols], func=AF.Square)
                        P.op("act", "activation", R=[xb(ft, c)], W=[b_zb[ft % 2]], out=zb[:], in_=xT[:, ft, cols], func=AF.Copy)
                        P.mm([dict(out=ps[6][:], lhsT=lnmean[:], rhs=zb[:], start=(ft == 0), stop=(ft == 7))],
                             R=[b_zb[ft % 2], Bc], W=[psb[6]])
                        P.mm([dict(out=ps[7][:], lhsT=lnmean[:], rhs=sq[:], start=(ft == 0), stop=(ft == 7))],
                             R=[b_sq[ft % 2], Bc], W=[psb[7]])
                    P.op("act", "activation", R=[psb[6]], W=[b_m2], out=m2[:], in_=ps[6][:], func=AF.Square)
                    P.op("dve", "tensor_tensor", R=[psb[7], b_m2], W=[b_m2], out=m2[:], in0=ps[7][:], in1=m2[:], op=SUB)
                    P.op("act", "activation", R=[b_m2, Bc], W=[b_m2], out=m2[:], in_=m2[:], func=AF.Ln, bias=cst[:, 0:1], scale=1.0)
                    P.op("act", "activation", R=[b_m2], W=[b_m2], out=m2[:], in_=m2[:], func=AF.Exp, scale=-0.5)
                    for ft in range(8):
                        P.op("dve", "tensor_tensor", R=[xb(ft, c), psb[6]], W=[b_lt], out=lt[:], in0=xT[:, ft, cols], in1=ps[6][:], op=SUB)
                        P.op("dve", "tensor_tensor", R=[b_lt, b_m2], W=[b_lt], out=lt[:], in0=lt[:], in1=m2[:], op=MUL)
                        P.op("dve", "tensor_scalar", R=[b_lt, Blay], W=[xb(ft, c)], out=xT[:, ft, cols], in0=lt[:],
                             scalar1=lnpT[:, gsel, ft:ft + 1], scalar2=lnpT[:, gsel + 1, ft:ft + 1], op0=MUL, op1=ADD)

                for m in range(8):
                    i2 = (m // 2) % 2
                    mc = 128 * (m % 2)
                    bw = b_wm[i2]
                    if m % 2 == 0:
                        c2_ = 256 * (m // 2)
                        for kk in range(3):
                            P.dma("pool", wg[i2][:, :, kk, :], wv[:, :, 2048 + 1024 * kk + c2_:2048 + 1024 * kk + c2_ + 256], W=[bw])
                        P.dma("pool", wbs[i2][:], wvb0[:, :, c2_:c2_ + 256], W=[bw])
                        P.dma("pool", wbw[i2][:], wvb1[:, :, c2_:c2_ + 256], W=[bw])
                        P.dma("pool", wbg[i2][:], wvb2[:, :, c2_:c2_ + 256], W=[bw])
                    for kk in range(3):
                        bg_ = nb2()
                        P.mm([dict(out=ps[bg_][:], lhsT=wg[i2][:, kt, kk, mc:mc + 128], rhs=hT[:, kt, :], start=(kt == 0), stop=(kt == 7))
                              for kt in range(8)], R=[bw, b_hT], W=[psb[bg_]])
                        sgf, b_sgfc = sgf2[nsg[0] % 2], b_sgf2[nsg[0] % 2]
                        nsg[0] += 1
                        P.op("act", "activation", R=[psb[bg_]], W=[b_sgfc], out=sgf[:], in_=ps[bg_][:], func=AF.Sigmoid)
                        bp_ = 4 + (kk % 2)
                        if kk == 0:
                            P.mm([dict(out=ps[bp_][:], lhsT=wbs[i2][:, kt, mc:mc + 128], rhs=ya[:, kt, :], start=(kt == 0), stop=(kt == 3))
                                  for kt in range(4)], R=[bw, b_ya], W=[psb[bp_]])
                        else:
                            wsel, ysel, by_ = (wbw, ywT, b_yw) if kk == 1 else (wbg, ygT, b_yg)
                            P.mm([dict(out=ps[bp_][:], lhsT=wsel[i2][:, hh, mc:mc + 128], rhs=ysel[:, hh, :], start=(hh == 0), stop=(hh == 7))
                                  for hh in range(8)], R=[bw, by_], W=[psb[bp_]])
                        if kk == 0:
                            P.op("dve", "tensor_tensor", R=[psb[bp_], b_sgfc], W=[b_acc], out=accf[:], in0=ps[bp_][:], in1=sgf[:], op=MUL)
                        else:
                            P.op("dve", "tensor_tensor", R=[psb[bp_], b_sgfc], W=[b_tmp], out=tmpf[:], in0=ps[bp_][:], in1=sgf[:], op=MUL)
                            if kk == 1:
                                P.op("dve", "tensor_tensor", R=[b_acc, b_tmp], W=[b_acc], out=accf[:], in0=accf[:], in1=tmpf[:], op=ADD)
                            else:
                                P.op("dve", "tensor_tensor", R=[b_acc, b_tmp], W=[b_mT], out=mT[:, m, :], in0=accf[:], in1=tmpf[:], op=ADD)
                if c == 1:
                    dump("mT%d" % l, mT[:, 0, :], [128, 512], [b_mT], BF16)
                for ob in range(4):
                    wo_, bwo_ = wo[ob % 2], b_wo[ob % 2]
                    P.dma("pool", wo_[:], wvo[:, :, 256 * ob:256 * ob + 256], W=[bwo_])
                    for mm_ in range(2):
                        mo = 2 * ob + mm_
                        bank = nb2()
                        P.mm([dict(out=ps[bank][:], lhsT=wo_[:, kt, 128 * mm_:128 * mm_ + 128], rhs=mT[:, kt, :],
                                   start=(kt == 0), stop=(kt == 7)) for kt in range(8)], R=[bwo_, b_mT], W=[psb[bank]])
                        tq, btq = tmp2[mo % 2], b_tmp2[mo % 2]
                        P.op("act", "activation", R=[psb[bank], Bmod], W=[btq], out=tq[:], in_=ps[bank][:], func=AF.Copy,
                             scale=modv[:, 16 + mo, k:k + 1])
                        P.op("dve", "scalar_tensor_tensor", R=[xb(mo, c), btq], W=[xb(mo, c)], out=xT[:, mo, cols], in0=xT[:, mo, cols],
                             scalar=ALPHA, in1=tq[:], op0=MUL, op1=ADD)
                layer_norm(0)
                if c == 1:
                    dump("xln1_%d" % l, xT[:, 0, cols], [128, 512], [xb(0, c)])
                P.barrier()
                A.release(mC12)

                hid = A.alloc("hid", [128, 32, 512], BF16)
                wup = [A.alloc("wup", [128, 8, 512], BF16) for _ in range(2)]
                wdn = [A.alloc("wdn", [128, 4, 1024], BF16) for _ in range(2)]
                rl = [A.alloc("rl", [128, 512], F32) for _ in range(2)]
                tmp2 = [A.alloc("tmp2", [128, 512], F32) for _ in range(2)]
                sqs = [A.alloc("sqs", [128, 512], BF16) for _ in range(2)]
                zbs = [A.alloc("zbs", [128, 512], BF16) for _ in range(2)]
                m2 = A.alloc("m2", [128, 512], F32)
                lt = A.alloc("lt", [128, 512], F32)
                b_hid = P.b("hid")
                b_wup = [P.b("wup", i_) for i_ in range(2)]
                b_wdn = [P.b("wdn", i_) for i_ in range(2)]
                b_rl = [P.b("rl", i_) for i_ in range(2)]
                for ft in range(8):
                    P.op("dve", "tensor_scalar", R=[xb(ft, c), Bmod], W=[b_hT], out=hT[:, ft, :], in0=xT[:, ft, cols],
                         scalar1=modv[:, 32 + ft, k:k + 1], scalar2=modv[:, 24 + ft, k:k + 1], op0=MUL, op1=ADD)
                nr = 0
                for jb in range(8):
                    w, bw = wup[jb % 2], b_wup[jb % 2]
                    P.dma("pool", w[:], wvu[:, :, 512 * jb:512 * jb + 512], W=[bw])
                    for t_ in range(4):
                        bank = nb2()
                        P.mm([dict(out=ps[bank][:], lhsT=w[:, kt, 128 * t_:128 * t_ + 128], rhs=hT[:, kt, :],
                                   start=(kt == 0), stop=(kt == 7)) for kt in range(8)], R=[bw, b_hT], W=[psb[bank]])
                        r_, br_ = rl[nr % 2], b_rl[nr % 2]
                        nr += 1
                        P.op("act", "activation", R=[psb[bank]], W=[br_], out=r_[:], in_=ps[bank][:], func=AF.Relu)
                        P.op("dve" if nr % 2 else "pool", "tensor_tensor", R=[br_], W=[b_hid], out=hid[:, 4 * jb + t_, :], in0=r_[:], in1=r_[:], op=MUL)
                for kb in range(8):
                    w, bw = wdn[kb % 2], b_wdn[kb % 2]
                    P.dma("pool", w[:], wvd[:, 4 * kb:4 * kb + 4, :], W=[bw])
                    for mo in range(8):
                        P.mm([dict(out=ps[mo][:], lhsT=w[:, kq, 128 * mo:128 * mo + 128], rhs=hid[:, 4 * kb + kq, :],
                                   start=(kb == 0 and kq == 0), stop=(kb == 7 and kq == 3)) for kq in range(4)],
                             R=[bw, b_hid], W=[psb[mo]])
                for mo in range(8):
                    tq, btq = tmp2[mo % 2], b_tmp2[mo % 2]
                    P.op("act", "activation", R=[psb[mo], Bmod], W=[btq], out=tq[:], in_=ps[mo][:], func=AF.Copy,
                         scale=modv[:, 40 + mo, k:k + 1])
                    P.op("dve", "scalar_tensor_tensor", R=[xb(mo, c), btq], W=[xb(mo, c)], out=xT[:, mo, cols], in0=xT[:, mo, cols],
                         scalar=ALPHA, in1=tq[:], op0=MUL, op1=ADD)
                layer_norm(2)
                P.barrier()
                A.release(mCh)
                if l == nlayers - 1:
                    out_chunk(c, ytokL)
            P.barrier()
            A.release(mLay)

        for l in range(nlayers):
            layer(l)

        m0 = A.mark()
        ytok = [A.alloc("ytok", [128, 1024], F32) for _ in range(2)]
        for tt in (range(12) if nlayers == 0 else ()):
            yt = ytok[tt % 2]
            by = P.b("ytok", tt % 2)
            c = tt // 4
            for hh in range(2):
                bank = (2 * tt + hh) % 2
                P.mm([dict(out=ps[bank][:, 128 * q:128 * q + 128],
                           lhsT=xT[:, 4 * hh + q, 128 * tt:128 * tt + 128],
                           rhs=identF[:], start=True, stop=True) for q in range(4)],
                     R=[xb(ft, c) for ft in range(4 * hh, 4 * hh + 4)] + [Bc], W=[psb[bank]])
                if hh:
                    P.op("act", "activation", R=[psb[bank]], W=[by], out=yt[:, 512:1024], in_=ps[bank][:], func=AF.Copy)
                else:
                    P.op("dve", "tensor_copy", R=[psb[bank]], W=[by], out=yt[:, 0:512], in_=ps[bank][:])
            P.dma("sp", yo[128 * tt:128 * tt + 128, :], yt[:], R=[by], is_out=True)
        A.release(m0)
        P.finish(block)
    return nc, A.peak


def _consts():
    ident = np.eye(128, dtype=np.float32)
    rotm = np.zeros((128, 128), np.float32)
    for hb in range(2):
        for j in range(32):
            rotm[64 * hb + j + 32, 64 * hb + j] = -1.0
            rotm[64 * hb + j, 64 * hb + j + 32] = 1.0
    hmean = np.zeros((128, 128), np.float32)
    hmean[:64, :64] = 1.0 / 64
    hmean[64:, 64:] = 1.0 / 64
    jj = np.arange(128)
    mL = (jj[:, None] >= jj[None, :]).astype(np.float32)
    mR = (jj[:, None] <= jj[None, :]).astype(np.float32)
    return ident, rotm, hmean, mL, mR


def _rope_tables(i):
    t = 1024 * i + np.arange(1024)
    row = (t // 64).astype(np.float32)
    col = (t % 64).astype(np.float32)
    inv = (10000.0 ** (-np.arange(16, dtype=np.float32) / 16)).astype(np.float32)
    ang = np.concatenate([row[:, None] * inv, col[:, None] * inv], axis=-1).astype(np.float32)
    c, s = np.cos(ang).astype(np.float32), np.sin(ang).astype(np.float32)
    p = np.arange(128) % 32
    return np.stack([c[:, p].T, s[:, p].T]).astype(np.float32)


def _pair_list(i):
    es = [(e // 16, e % 16) for e in range(32)]
    es += [(j // 4, 4 * i + j % 4) for j in range(8)]
    return es


def _host_inputs(inp, r, shared):
    f = np.float32
    b, i = r // 4, r % 4
    d = dict(shared)
    d["xin"] = np.ascontiguousarray(np.concatenate(
        [inp["x_prompt"][2 * r:2 * r + 2].reshape(512, 1024), inp["x_sample"][b, 1024 * i:1024 * (i + 1)]], 0), f)
    cond = np.stack([inp["c_ctx"], inp["c"][b]])
    d["condT"] = np.ascontiguousarray(cond.reshape(2, 8, 128).transpose(2, 1, 0), f)
    pl = _pair_list(i)
    lamh = np.zeros((2, 128, 3, 40), f)
    wbc = np.zeros((2, 4, 32, 10, 2, 128), f)
    wcc = np.zeros((2, 128, 40, 2, 32), f)
    s0 = np.zeros((2, 128, 2, 8), f)
    for l in range(2):
        for e, (dd, pi) in enumerate(pl):
            for gl in range(2):
                g = 2 * pi + gl
                n0 = 64 * gl
                lamh[l, n0:n0 + 64, 0, e] = inp["ssm_lam_re"][l, dd, g]
                lamh[l, n0:n0 + 64, 1, e] = inp["ssm_lam_im"][l, dd, g]
                lamh[l, n0:n0 + 64, 2, e] = inp["ssm_log_step"][l, dd, g]
                wbc[l, e % 4, 16 * gl:16 * gl + 16, e // 4, 0, n0:n0 + 64] = inp["ssm_b_re"][l, dd, g].T
                wbc[l, e % 4, 16 * gl:16 * gl + 16, e // 4, 1, n0:n0 + 64] = inp["ssm_b_im"][l, dd, g].T
                wcc[l, n0:n0 + 64, e, 0, 16 * gl:16 * gl + 16] = inp["ssm_c_re"][l, dd, g].T
                wcc[l, n0:n0 + 64, e, 1, 16 * gl:16 * gl + 16] = inp["ssm_c_im"][l, dd, g].T
                if e >= 32:
                    s0[l, n0:n0 + 64, 0, e - 32] = inp["state_ssm"][b, l, dd, 0, g]
                    s0[l, n0:n0 + 64, 1, e - 32] = inp["state_ssm"][b, l, dd, 1, g]
    d["lam"], d["wbc"], d["wcc"], d["s0h"] = lamh, wbc, wcc, s0
    d["ck"] = np.ascontiguousarray(np.stack([inp["cache_k_win"][b].reshape(2, 256, 128), inp["cache_k_glb"][b].reshape(2, 256, 128)], 1), f)
    d["cv"] = np.ascontiguousarray(np.stack([inp["cache_v_win"][b].reshape(2, 256, 128), inp["cache_v_glb"][b].reshape(2, 256, 128)], 1), f)
    ident, rotm, hmean, mL, mR = _consts()
    z = np.zeros_like(mL)
    d["cmat"] = np.stack([ident, rotm, hmean, mL, mR, z if i == 0 else mL, z if i == 3 else mR]).astype(f)
    d["rope"] = _rope_tables(i)
    return d


def _shared_inputs(inp):
    f = np.float32
    d = {}
    d["w_mod"] = np.ascontiguousarray(inp["w_mod"], f)
    d["b_modT"] = np.ascontiguousarray(inp["b_mod"].reshape(2, 48, 128).transpose(0, 2, 1), f)
    d["w_in"] = np.ascontiguousarray(inp["w_in"], f)
    d["dskipT"] = np.ascontiguousarray(inp["ssm_d"].reshape(2, 4, 128).transpose(0, 2, 1), f)
    d["w_glu"] = np.ascontiguousarray(inp["w_glu"], f)
    d["w_br"] = np.ascontiguousarray(np.stack([inp["w_br_ssm"], inp["w_br_win"], inp["w_br_glb"]], 1), f)
    d["w_out"] = np.ascontiguousarray(inp["w_out"], f)
    d["w_up"] = np.ascontiguousarray(inp["w_up"], f)
    d["w_down"] = np.ascontiguousarray(inp["w_down"], f)
    lnp = np.stack([inp["ln1_g"], inp["ln1_b"], inp["ln2_g"], inp["ln2_b"]], 1)
    d["lnp"] = np.ascontiguousarray(lnp.reshape(2, 4, 8, 128).transpose(0, 3, 1, 2), f)
    qn = np.tile(inp["q_norm_glb"], (1, 2))
    kn = np.tile(inp["k_norm_glb"], (1, 2))
    d["qkn"] = np.ascontiguousarray(np.stack([qn, kn], -1), f)
    d["knrow"] = np.ascontiguousarray(np.broadcast_to(inp["k_norm_glb"][:, None, :], (2, 128, 64)), f)
    d["sinkb"] = np.ascontiguousarray(np.broadcast_to(inp["sink_win"][:, None, :], (2, 128, 8)), f)
    jj = np.arange(512, dtype=f)
    d["jidx"] = np.ascontiguousarray(np.stack([np.broadcast_to(jj, (128, 512)), np.broadcast_to(jj % 256, (128, 512))]), f)
    return d


_CACHE = {}


def _run(inputs, debug=(), nlayers=2, stop_after=None):
    inp = {k: np.asarray(v) for k, v in inputs.items()}
    key = (tuple(debug), nlayers, stop_after)
    if key not in _CACHE:
        _CACHE[key] = build(debug, nlayers, stop_after)
    nc, peak = _CACHE[key]
    shared = _shared_inputs(inp)
    in_maps = [_host_inputs(inp, r, shared) for r in range(8)]
    res = run_bass_kernel_spmd(nc, in_maps, core_ids=list(range(8)))
    return res.results


def kernel(**inputs):
    R = _run(inputs)
    f = np.float32
    y_prompt = np.zeros((16, 256, 1024), f)
    y_sample = np.zeros((2, 4096, 1024), f)
    st = np.zeros((16, 2, 2, 2, 32, 64), f)
    kw = np.zeros((16, 2, 256, 2, 64), f)
    vw = np.zeros_like(kw)
    kg = np.zeros_like(kw)
    vg = np.zeros_like(kw)
    for r in range(8):
        o = R[r]
        b, i = r // 4, r % 4
        yo = np.asarray(o["yo"])
        y_prompt[2 * r:2 * r + 2] = yo[:512].reshape(2, 256, 1024)
        y_sample[b, 1024 * i:1024 * (i + 1)] = yo[512:]
        co = np.asarray(o["co"])
        for s in range(2):
            kw[2 * r + s] = co[s, :, 0].reshape(2, 256, 2, 64)
            vw[2 * r + s] = co[s, :, 1].reshape(2, 256, 2, 64)
            kg[2 * r + s] = co[s, :, 2].reshape(2, 256, 2, 64)
            vg[2 * r + s] = co[s, :, 3].reshape(2, 256, 2, 64)
        st[2 * r:2 * r + 2] = np.asarray(o["so"])
    return (y_prompt, y_sample, st, kw, vw, kg, vg)
```

```python
import numpy as np
from contextlib import ExitStack
import concourse.bass as bass
import concourse.mybir as mybir
from concourse.bass_utils import run_bass_kernel_spmd

F32 = mybir.dt.float32
BF16 = mybir.dt.bfloat16
I32 = mybir.dt.int32
ALU = mybir.AluOpType
AF = mybir.ActivationFunctionType
GROUPS = [[0, 1, 2, 3], [4, 5, 6, 7]]
NDMA = 40
EPOCH = 6000
ALPHA = float((2.0 * 2) ** 0.25)
TWO_PI = float(2 * np.pi)
PI_LO = 3.1415925
import os
DBGF = set(os.environ.get('KDBG', '').split(','))
SKIP_SELF = False
SKIP_OLD = False


class Buf:
    __slots__ = ("w", "r")

    def __init__(self):
        self.w = None
        self.r = {}


class Prog:
    ENG = ("pe", "act", "dve", "pool", "sp")

    def __init__(self, nc, es):
        self.nc, self.es = nc, es
        self.q = {e: [] for e in self.ENG}
        self.n = {e: 0 for e in self.ENG}
        self.ep = {e: None for e in self.ENG}
        self.last = {e: None for e in self.ENG}
        self.waited = {e: {} for e in self.ENG}
        self.bufs = {}
        self.nsem = 0
        self.dma_sems = [es.enter_context(nc.semaphore(f"dq{i}")) for i in range(NDMA)]
        self.dma_val = [0] * NDMA
        self.dma_tok = [None] * NDMA
        self.dma_rr = 0
        self.dma_rrq = {"sp": 0, "pool": 0, "act": 0}
        self.out_toks = []

    def b(self, *key):
        v = self.bufs.get(key)
        if v is None:
            v = self.bufs[key] = Buf()
        return v

    def _sem(self, e):
        s = self.ep[e]
        if s is None or s[1] >= EPOCH:
            h = self.es.enter_context(self.nc.semaphore(f"pg{e}{self.nsem}"))
            self.nsem += 1
            s = self.ep[e] = [h, 0]
        return s

    def _need(self, eng, tok, cur_big=False):
        if tok is None:
            return
        h, val, src, idx = tok[:4]
        if src == eng:
            if eng in ("pe", "sp"):
                return
            if SKIP_OLD and self.n[eng] - idx > 3:
                return
            if SKIP_SELF and cur_big and len(tok) > 4 and tok[4]:
                return
        w = self.waited[eng]
        if w.get(h, 0) >= val:
            return
        w[h] = val
        self.q[eng].append(lambda e, h=h, val=val: e.wait_ge(h, val))

    def _deps(self, eng, R, W, cur_big=False):
        for b in R:
            self._need(eng, b.w, cur_big)
        for b in W:
            self._need(eng, b.w, cur_big)
            for t in b.r.values():
                self._need(eng, t, cur_big)

    def _upd(self, tok, R, W):
        for b in R:
            b.r[tok[2]] = tok
        for b in W:
            b.w = tok
            b.r = {}

    def op(self, eng, meth, R=(), W=(), **kw):
        big = False
        if not isinstance(meth, list):
            o_ = kw.get("out", kw.get("ap"))
            if o_ is not None:
                fs = 1
                for d_ in o_.shape[1:]:
                    fs *= d_
                big = fs >= 256
        self._deps(eng, R, W, big)
        s = self._sem(eng)
        s[1] += 1
        h, val = s[0], s[1]
        idx = self.n[eng]
        self.n[eng] += 1
        calls = meth if isinstance(meth, list) else [(meth, kw)]

        def thunk(e, calls=calls, h=h):
            for m, k in calls[:-1]:
                getattr(e, m)(**k)
            m, k = calls[-1]
            getattr(e, m)(**k).then_inc(h, 1)
        self.q[eng].append(thunk)
        tok = (h, val, eng, idx, big)
        self.last[eng] = tok
        self._upd(tok, R, W)
        return tok

    def mm(self, calls, R=(), W=()):
        return self.op("pe", [("matmul", c) for c in calls], R=R, W=W)

    def dma_slot(self, qeng):
        half = NDMA // 2
        i = self.dma_rrq[qeng]
        self.dma_rrq[qeng] = (i + 1) % half
        return i + (half if qeng == "pool" else 0)

    def dma(self, qeng, out, in_, R=(), W=(), is_out=False):
        self._deps(qeng, R, W)
        k = self.dma_slot(qeng)
        self._need(qeng, self.dma_tok[k])
        self.dma_val[k] += 16
        val, h = self.dma_val[k], self.dma_sems[k]
        self.q[qeng].append(lambda e, out=out, in_=in_, h=h: e.dma_start(out=out, in_=in_).then_inc(h, 16))
        self.n[qeng] += 1
        tok = (h, val, "dma%d" % k, 0)
        self.dma_tok[k] = tok
        self._upd(tok, R, W)
        if is_out:
            self.out_toks.append(tok)
        return tok

    def coll(self, in_ap, out_ap, R=(), W=()):
        self._deps("pool", R, W)
        h = self.es.enter_context(self.nc.semaphore(f"cc{self.nsem}"))
        self.nsem += 1
        self.q["pool"].append(lambda e, h=h: e.collective_compute(
            "AllGather", ALU.bypass, replica_groups=GROUPS, ins=[in_ap], outs=[out_ap]).then_inc(h, 1))
        self.n["pool"] += 1
        tok = (h, 1, "cc%d" % self.nsem, 0)
        self._upd(tok, R, W)
        return tok

    def barrier(self):
        toks = [self.last[e] for e in self.ENG] + list(self.dma_tok)
        for e in self.ENG:
            for t in toks:
                if t is not None and t[2] != e:
                    self._need(e, t)

    def finish(self, block):
        for t in self.out_toks + [self.last[e] for e in self.ENG] + list(self.dma_tok):
            self._need("sp", t)
        reg = {"pe": block.tensor, "act": block.scalar, "dve": block.vector, "pool": block.gpsimd, "sp": block.sync}
        for e in self.ENG:
            lst = self.q[e]

            def run(eng, lst=lst):
                for f in lst:
                    f(eng)
            reg[e](run)


class Arena:
    def __init__(self, nc):
        self.nc = nc
        self.p = 16512
        self.hi = 229344
        self.k = 0
        self.peak = 0

    def alloc(self, name, shape, dt):
        n = 1
        for s in shape[1:]:
            n *= s
        sz = {F32: 4, BF16: 2, I32: 4}[dt]
        off = (self.p + 63) // 64 * 64
        self.last_off = off
        self.p = off + n * sz
        self.peak = max(self.peak, self.p)
        assert self.p <= self.hi, f"SBUF overflow at {name}: {self.p}"
        self.k += 1
        return self.nc.alloc_sbuf_tensor_at(f"{name}{self.k}", list(shape), dt, offset=off)

    def mark(self):
        return self.p

    def release(self, m):
        self.p = m


def build(debug=(), nlayers=2, stop_after=None):
    nc = bass.Bass("TRN2", target_bir_lowering=False)
    es = ExitStack()
    with es:
        def din(name, shape, dt=F32):
            return nc.dram_tensor(name, list(shape), dt, kind="ExternalInput").ap()

        def dout(name, shape, dt=F32):
            return nc.dram_tensor(name, list(shape), dt, kind="ExternalOutput").ap()

        def dint(name, shape, dt):
            return nc.dram_tensor(name, list(shape), dt, kind="Internal").ap()

        xin = din("xin", [1536, 1024])
        condT = din("condT", [128, 8, 2])
        w_mod = din("w_mod", [2, 1024, 6144])
        b_modT = din("b_modT", [2, 128, 48])
        w_in = din("w_in", [2, 1024, 5120])
        lam = din("lam", [2, 128, 3, 40])
        wbc = din("wbc", [2, 4, 32, 10, 2, 128])
        wcc = din("wcc", [2, 128, 40, 2, 32])
        dskipT = din("dskipT", [2, 128, 4])
        s0h = din("s0h", [2, 128, 2, 8])
        w_glu = din("w_glu", [2, 512, 512])
        w_br = din("w_br", [2, 3, 512, 1024])
        w_out = din("w_out", [2, 1024, 1024])
        w_up = din("w_up", [2, 1024, 4096])
        w_down = din("w_down", [2, 4096, 1024])
        lnp = din("lnp", [2, 128, 4, 8])
        qkn = din("qkn", [2, 128, 2])
        knrow = din("knrow", [2, 128, 64])
        sinkb = din("sinkb", [2, 128, 8])
        ck = din("ck", [2, 2, 256, 128])
        cv = din("cv", [2, 2, 256, 128])
        cmat = din("cmat", [7, 128, 128])
        jidx = din("jidx", [2, 128, 512])
        rope = din("rope", [2, 128, 1024])
        yo = dout("yo", [1536, 1024])
        co = dout("co", [2, 2, 4, 256, 128])
        so = dout("so", [2, 2, 2, 2, 32, 64])
        xu = dint("xu", [512, 1024], BF16)
        xu_g = dint("xu_g", [2048, 1024], BF16)
        xk = dint("xk", [256, 1024], BF16)
        xk_g = dint("xk_g", [1024, 1024], BF16)
        xv = dint("xv", [1024, 256], BF16)
        xv_g = dint("xv_g", [4096, 256], BF16)
        xy = dint("xy", [4, 128, 1024], F32)
        xy_g = dint("xy_g", [2048, 1024], F32)
        yp_d = dint("yp_d", [128, 4, 512], F32)
        dbg_out = {}

        P = Prog(nc, es)
        A = Arena(nc)
        ps = [es.enter_context(nc.psum_tensor(f"ps{i}", [128, 512], F32)) for i in range(8)]
        psb = [P.b("ps", i) for i in range(8)]
        block = es.enter_context(nc.Block())
        pid = None

        def dump(name, ap, shape, bufs, dt=F32):
            if name not in debug:
                return
            d = dout("dbg_" + name, shape, dt)
            P.dma("sp", d, ap, R=bufs, is_out=True)

        xT = A.alloc("xT", [128, 8, 1536], F32)
        cm = A.alloc("cm", [128, 7, 128], BF16)
        identF = A.alloc("identF", [128, 128], F32)
        lnmean = A.alloc("lnmean", [128, 128], BF16)
        onesF = A.alloc("onesF", [128, 64], F32)
        cst = A.alloc("cst", [128, 8], F32)
        ropeT = A.alloc("ropeT", [128, 2, 1024], F32)
        jT = A.alloc("jT", [128, 2, 512], F32)
        modvs = [A.alloc("modv", [128, 48, 2], F32) for _ in range(2)]
        lnpT = A.alloc("lnpT", [128, 4, 8], F32)
        qknT = A.alloc("qknT", [128, 2], F32)
        esink = A.alloc("esink", [128, 8], F32)
        dskT = A.alloc("dskT", [128, 4], F32)
        Bc = P.b("const")
        Blay = P.b("laycst")

        def xb(ft, c):
            return P.b("xT", ft, c)

        P.dma("pool", cm[:], cmat.rearrange("k p c -> p k c"), W=[Bc])
        P.dma("sp", identF[:], cmat[0], W=[Bc])
        P.dma("sp", ropeT[:], rope.rearrange("k p c -> p k c"), W=[Bc])
        P.dma("sp", jT[:], jidx.rearrange("k p c -> p k c"), W=[Bc])
        P.op("pool", "memset", W=[Bc], ap=lnmean[:], constant=1.0 / 1024)
        P.op("pool", "memset", W=[Bc], ap=onesF[:], constant=1.0)
        P.op("pool", "memset", W=[Bc], ap=cst[:, 0:1], constant=1e-5)
        P.op("pool", "memset", W=[Bc], ap=cst[:, 1:2], constant=1e-6)
        P.op("pool", "memset", W=[Bc], ap=cst[:, 2:3], constant=float(np.pi / 2))
        P.op("pool", "memset", W=[Bc], ap=cst[:, 3:4], constant=0.0)
        IDB, ROT, HMEAN, ML, MR, MLF, MRL = range(7)

        m0 = A.mark()
        xtok = [A.alloc("xtok", [128, 1024], F32) for _ in range(2)]
        for tt in range(12):
            xt = xtok[tt % 2]
            bx = P.b("xtok", tt % 2)
            P.dma("sp", xt[:], xin[128 * tt:128 * tt + 128, :], W=[bx])
            for hh in range(2):
                bank = (2 * tt + hh) % 2
                P.mm([dict(out=ps[bank][:, 128 * q:128 * q + 128],
                           lhsT=xt[:, 128 * (4 * hh + q):128 * (4 * hh + q) + 128],
                           rhs=identF[:], start=True, stop=True) for q in range(4)],
                     R=[bx, Bc], W=[psb[bank]])
                c = tt // 4
                o_ap = xT[:, 4 * hh:4 * hh + 4, 128 * tt:128 * tt + 128]
                i_ap = ps[bank][:].rearrange("p (q t) -> p q t", q=4)
                Wx = [xb(ft, c) for ft in range(4 * hh, 4 * hh + 4)]
                if hh:
                    P.op("act", "activation", R=[psb[bank]], W=Wx, out=o_ap, in_=i_ap, func=AF.Copy)
                else:
                    P.op("dve", "tensor_copy", R=[psb[bank]], W=Wx, out=o_ap, in_=i_ap)
        P.barrier()
        A.release(m0)
        dump("xT0", xT[:, 0, :], [128, 1536], [xb(0, c) for c in range(3)])

        _bank = [0]

        def nb():
            _bank[0] = (_bank[0] + 1) % 4
            return _bank[0]

        MUL, ADD, SUB = ALU.mult, ALU.add, ALU.subtract
        (LRE, LIM, LST, STEP, LR, TH, RR, T1, T2, T3, THR, SIN, COS, ARE, AIM, AM1, NRE, NIM, DEN, CFRE, CFIM,
         C512, S512, T4) = range(24)
        kvp = {}

        _pidc = {}

        def pid4(e):
            if "v" not in _pidc:
                _pidc["v"] = e.partition_id()
            return _pidc["v"] % 4

        def dyn_dma(qeng, fn, R=(), W=(), is_out=False):
            P._deps(qeng, R, W)
            k = P.dma_slot(qeng)
            P._need(qeng, P.dma_tok[k])
            P.dma_val[k] += 16
            val, h = P.dma_val[k], P.dma_sems[k]

            def thunk(e, fn=fn, h=h):
                o_, i_ = fn(e)
                e.dma_start(out=o_, in_=i_).then_inc(h, 16)
            P.q[qeng].append(thunk)
            P.n[qeng] += 1
            tok = (h, val, "dma%d" % k, 0)
            P.dma_tok[k] = tok
            P._upd(tok, R, W)
            return tok

        def mod_vectors_gen(l_):
            mv = modvs[l_ % 2]
            bmv = P.b("modv", l_ % 2)
            mM = A.mark()
            sc = A.alloc("scond", [128, 8, 2], F32)
            bm = A.alloc("bmod", [128, 48], F32)
            b_sc, b_bm = P.b("sc"), P.b("bm")
            P.dma("sp", sc[:], condT, W=[b_sc])
            P.op("act", "activation", R=[b_sc], W=[b_sc], out=sc[:], in_=sc[:], func=AF.Silu)
            P.dma("sp", bm[:], b_modT[l_], W=[b_bm])
            wm = [A.alloc("wm", [128, 8, 128], F32) for _ in range(3)]
            wmv = w_mod[l_].rearrange("(kt p) c -> p kt c", p=128)

            def load(T):
                P.dma("sp", wm[T % 3][:], wmv[:, :, 128 * T:128 * T + 128], W=[P.b("wm", T % 3)])
            load(0)
            load(1)
            for T in range(48):
                if T + 2 < 48:
                    load(T + 2)
                w = wm[T % 3]
                P.mm([dict(out=ps[3][:, 2 * T:2 * T + 2], lhsT=w[:, kt, :], rhs=sc[:, kt, :], start=(kt == 0), stop=(kt == 7))
                      for kt in range(8)], R=[P.b("wm", T % 3), b_sc], W=[psb[3]])
                yield T
            pm3 = ps[3][:, 0:96].rearrange("p (t k) -> p t k", k=2)
            for k in range(2):
                P.op("dve", "tensor_tensor", R=[psb[3], b_bm], W=[bmv], out=mv[:, :, k], in0=pm3[:, :, k], in1=bm[:, :], op=ADD)
            for lo in (8, 32):
                P.op("dve", "tensor_scalar", R=[bmv], W=[bmv], out=mv[:, lo:lo + 8, :], in0=mv[:, lo:lo + 8, :],
                     scalar1=1.0, scalar2=None, op0=ADD)
            if l_ == 0:
                P.barrier()
            A.release(mM)

        def mod_vectors(l_):
            for _ in mod_vectors_gen(l_):
                pass

        def layer(l):
            wv = w_in[l].rearrange("(kt p) c -> p kt c", p=128)
            k_of = (0, 1, 1)
            P.dma("sp", lnpT[:], lnp[l], W=[Blay])
            P.dma("sp", qknT[:], qkn[l], W=[Blay])
            P.dma("sp", dskT[:], dskipT[l], W=[Blay])
            mLay = A.mark()
            kpT = A.alloc("kpT", [128, 2, 512], BF16)
            vp = A.alloc("vp", [128, 4, 2, 2, 65], BF16)
            Bkp, Bvp = P.b("kpT"), P.b("vp")
            P.op("pool", "memset", W=[Bvp], ap=vp[:], constant=1.0)

            modv = modvs[l % 2]
            Bmod = P.b("modv", l % 2)
            if l == 0:
                mod_vectors(0)
            sk = A.alloc("sk", [128, 8], F32)
            b_sk = P.b("sk")
            P.dma("sp", sk[:], sinkb[l], W=[b_sk])
            P.op("act", "activation", R=[b_sk], W=[Blay], out=esink[:], in_=sk[:], func=AF.Exp)

            mAB = A.mark()
            ubf = A.alloc("ubf", [128, 4, 512], BF16)
            CS = A.alloc("CS", [128, 24, 40], F32)
            CI = A.alloc("CI", [128, 40], I32)
            S0 = A.alloc("S0", [128, 2, 8], F32)
            car = A.alloc("car", [128, 2, 8], F32)
            tt8 = A.alloc("tt8", [128, 8], F32)
            WB = A.alloc("WB", [128, 40, 2, 128], BF16)
            WC = A.alloc("WC", [128, 40, 2, 32], BF16)
            b_cs, b_S0, b_car, b_tt8, b_WB, b_WC = (P.b(n) for n in ("CS", "S0", "car", "tt8", "WB", "WC"))

            def cs(i_, lo=0, hi=40):
                return CS[:, i_, lo:hi]

            def ctt(o_, a_, b_, op_):
                P.op("dve", "tensor_tensor", R=[b_cs], W=[b_cs], out=cs(o_), in0=cs(a_), in1=cs(b_), op=op_)

            P.dma("sp", CS[:, 0:3, :], lam[l], W=[b_cs])
            P.dma("sp", S0[:], s0h[l], W=[b_S0])
            P.op("pool", "memset", W=[b_WB], ap=WB[:], constant=0.0)
            WB5 = WB[:].rearrange("p (q a) r n -> p q a r n", a=4)
            for a in range(4):
                P.dma("pool", WB5[32 * a:32 * a + 32, :, a, :, :], wbc[l, a], W=[b_WB])
            P.dma("pool", WC[:], wcc[l], W=[b_WC])
            P.op("dve", "tensor_scalar", R=[b_WC], W=[b_WC], out=WC[:, :, 1, :], in0=WC[:, :, 1, :], scalar1=-1.0, scalar2=None, op0=MUL)
            WCn = A.alloc("WCn", [128, 40, 32], BF16)
            P.op("dve", "tensor_scalar", R=[b_WC], W=[b_WC], out=WCn[:], in0=WC[:, :, 0, :], scalar1=-1.0, scalar2=None, op0=MUL)
            P.op("act", "activation", R=[b_cs], W=[b_cs], out=cs(STEP), in_=cs(LST), func=AF.Exp)
            ctt(LR, LRE, STEP, MUL)
            ctt(TH, LIM, STEP, MUL)
            P.op("act", "activation", R=[b_cs], W=[b_cs], out=cs(RR), in_=cs(LR), func=AF.Exp)
            P.op("dve", "tensor_scalar", R=[b_cs], W=[b_cs], out=cs(T1), in0=cs(TH), scalar1=1.0 / TWO_PI, scalar2=None, op0=MUL)
            P.op("dve", "tensor_copy", R=[b_cs], W=[b_cs], out=CI[:], in_=cs(T1))
            P.op("dve", "tensor_copy", R=[b_cs], W=[b_cs], out=cs(T1), in_=CI[:])
            P.op("dve", "scalar_tensor_tensor", R=[b_cs], W=[b_cs], out=cs(THR), in0=cs(T1), scalar=-TWO_PI, in1=cs(TH), op0=MUL, op1=ADD)
            P.op("dve", "tensor_scalar", R=[b_cs], W=[b_cs], out=cs(THR), in0=cs(THR), scalar1=PI_LO, scalar2=-PI_LO, op0=ALU.min, op1=ALU.max)
            P.op("act", "activation", R=[b_cs], W=[b_cs], out=cs(SIN), in_=cs(THR), func=AF.Sin)
            P.op("act", "activation", R=[b_cs], W=[b_cs], out=cs(T2), in_=cs(THR), func=AF.Abs)
            P.op("act", "activation", R=[b_cs, Bc], W=[b_cs], out=cs(COS), in_=cs(T2), func=AF.Sin, scale=-1.0, bias=cst[:, 2:3])
            ctt(ARE, RR, COS, MUL)
            ctt(AIM, RR, SIN, MUL)
            P.op("dve", "tensor_scalar", R=[b_cs], W=[b_cs], out=cs(AM1), in0=cs(ARE), scalar1=-1.0, scalar2=None, op0=ADD)
            ctt(T2, AM1, LRE, MUL)
            ctt(T3, AIM, LIM, MUL)
            ctt(NRE, T2, T3, ADD)
            ctt(T2, AIM, LRE, MUL)
            ctt(T3, AM1, LIM, MUL)
            ctt(NIM, T2, T3, SUB)
            ctt(T2, LRE, LRE, MUL)
            ctt(T3, LIM, LIM, MUL)
            ctt(DEN, T2, T3, ADD)
            P.op("dve", "reciprocal", R=[b_cs], W=[b_cs], out=cs(DEN), in_=cs(DEN))
            ctt(CFRE, NRE, DEN, MUL)
            ctt(CFIM, NIM, DEN, MUL)
            P.op("dve", "tensor_copy", R=[b_cs], W=[b_cs], out=cs(C512), in_=cs(COS))
            P.op("dve", "tensor_copy", R=[b_cs], W=[b_cs], out=cs(S512), in_=cs(SIN))
            for _ in range(9):
                ctt(T2, C512, C512, MUL)
                ctt(T3, S512, S512, MUL)
                ctt(T4, C512, S512, MUL)
                ctt(C512, T2, T3, SUB)
                P.op("dve", "tensor_scalar", R=[b_cs], W=[b_cs], out=cs(S512), in0=cs(T4), scalar1=2.0, scalar2=None, op0=MUL)
            P.op("dve", "tensor_tensor", R=[b_cs, b_S0], W=[b_tt8], out=tt8[:], in0=S0[:, 1, :], in1=cs(SIN, 32, 40), op=MUL)
            P.op("dve", "tensor_tensor", R=[b_cs, b_S0], W=[b_car], out=car[:, 0, :], in0=S0[:, 0, :], in1=cs(COS, 32, 40), op=MUL)
            P.op("dve", "tensor_tensor", R=[b_car, b_tt8], W=[b_car], out=car[:, 0, :], in0=car[:, 0, :], in1=tt8[:], op=SUB)
            P.op("dve", "tensor_tensor", R=[b_cs, b_S0], W=[b_tt8], out=tt8[:], in0=S0[:, 1, :], in1=cs(COS, 32, 40), op=MUL)
            P.op("dve", "tensor_tensor", R=[b_cs, b_S0], W=[b_car], out=car[:, 1, :], in0=S0[:, 0, :], in1=cs(SIN, 32, 40), op=MUL)
            P.op("dve", "tensor_tensor", R=[b_car, b_tt8], W=[b_car], out=car[:, 1, :], in0=car[:, 1, :], in1=tt8[:], op=ADD)
            dump("CS%d" % l, CS[:].rearrange("p a e -> p (a e)"), [128, 960], [b_cs])

            mA = A.mark()
            ubs = A.alloc("ubs", [128, 4, 1024], BF16)
            wA = A.alloc("wA", [128, 8, 1024], BF16)
            hT = A.alloc("hT", [128, 8, 512], BF16)
            ksT = A.alloc("ksT", [128, 2, 1024], BF16)
            vs = A.alloc("vs", [128, 8, 256], BF16)
            x32 = A.alloc("x32", [128, 512], F32)
            sqb = A.alloc("sqb", [128, 512], BF16)
            rsf = A.alloc("rsf", [128, 512], F32)
            xbb = A.alloc("xbb", [128, 512], BF16)
            t1f = A.alloc("t1f", [128, 512], F32)
            t2f = A.alloc("t2f", [128, 512], F32)
            ctok = [A.alloc("ctok", [128, 4, 128], F32) for _ in range(2)]
            sq128 = A.alloc("sq128", [128, 128], F32)
            ssq = A.alloc("ssq", [128, 2], F32)
            knr = A.alloc("knr", [128, 64], F32)
            b_wA, b_hT, b_u, b_ks, b_vs = P.b("wA"), P.b("hT"), P.b("ubf"), P.b("ksT"), P.b("vs")
            b_x32, b_sqb, b_rsf, b_xbb, b_t1, b_t2 = (P.b(n) for n in ("x32", "sqb", "rsf", "xbb", "t1f", "t2f"))
            b_sq128, b_ssq, b_knr = P.b("sq128"), P.b("ssq"), P.b("knr")
            b_xu, b_xk, b_xv, b_xug, b_xkg, b_xvg = (P.b(n) for n in ("xu", "xk", "xv", "xug", "xkg", "xvg"))
            P.dma("sp", knr[:], knrow[l], W=[b_knr])
            for (d0, s0_, n_) in ((0, 0, 512), (512, 1024, 128), (640, 1792, 128), (768, 1152, 128), (896, 1920, 128)):
                P.dma("pool", wA[:, :, d0:d0 + n_], wv[:, :, s0_:s0_ + n_], W=[b_wA])

            def rmsnorm_fm(bank, N, gcol, out_ap, Wb):
                P.op("act", "activation", R=[psb[bank]], W=[b_x32], out=x32[:, :N], in_=ps[bank][:, :N], func=AF.Copy)
                P.op("act", "activation", R=[b_x32], W=[b_sqb], out=sqb[:, :N], in_=x32[:, :N], func=AF.Square)
                P.mm([dict(out=ps[7][:, :N], lhsT=cm[:, HMEAN, :], rhs=sqb[:, :N], start=True, stop=True)],
                     R=[b_sqb, Bc], W=[psb[7]])
                P.op("act", "activation", R=[psb[7], Bc], W=[b_rsf], out=rsf[:, :N], in_=ps[7][:, :N], func=AF.Ln,
                     bias=cst[:, 1:2], scale=1.0)
                P.op("act", "activation", R=[b_rsf], W=[b_rsf], out=rsf[:, :N], in_=rsf[:, :N], func=AF.Exp, scale=-0.5)
                P.op("dve", "scalar_tensor_tensor", R=[b_x32, b_rsf, Blay], W=Wb, out=out_ap, in0=x32[:, :N], scalar=gcol,
                     in1=rsf[:, :N], op0=MUL, op1=MUL)

            def rope_fm(N, pos0, out_ap, Wb):
                P.op("act", "activation", R=[b_x32], W=[b_xbb], out=xbb[:, :N], in_=x32[:, :N], func=AF.Copy)
                P.mm([dict(out=ps[7][:, :N], lhsT=cm[:, ROT, :], rhs=xbb[:, :N], start=True, stop=True)],
                     R=[b_xbb, Bc], W=[psb[7]])
                P.op("dve", "tensor_tensor", R=[b_x32, Bc], W=[b_t1], out=t1f[:, :N], in0=x32[:, :N],
                     in1=ropeT[:, 0, pos0:pos0 + N], op=MUL)
                P.op("dve", "tensor_tensor", R=[psb[7], Bc], W=[b_t2], out=t2f[:, :N], in0=ps[7][:, :N],
                     in1=ropeT[:, 1, pos0:pos0 + N], op=MUL)
                P.op("dve", "tensor_tensor", R=[b_t1, b_t2], W=Wb, out=out_ap, in0=t1f[:, :N], in1=t2f[:, :N], op=ADD)
            kvp["rms"], kvp["rope"] = rmsnorm_fm, rope_fm

            for c in range(3):
                k = k_of[c]
                cols = slice(512 * c, 512 * c + 512)
                for ft in range(8):
                    P.op("dve", "tensor_scalar", R=[xb(ft, c), Bmod], W=[b_hT], out=hT[:, ft, :], in0=xT[:, ft, cols],
                         scalar1=modv[:, 8 + ft, k:k + 1], scalar2=modv[:, ft, k:k + 1], op0=MUL, op1=ADD)
                if c == 0:
                    dump("hT%d" % l, hT[:, 0, :], [128, 512], [b_hT], BF16)
                for m in range(4):
                    bank = nb()
                    P.mm([dict(out=ps[bank][:], lhsT=wA[:, kt, 128 * m:128 * m + 128], rhs=hT[:, kt, :],
                               start=(kt == 0), stop=(kt == 7)) for kt in range(8)], R=[b_wA, b_hT], W=[psb[bank]])
                    u_dst = ubf[:, m, :] if c == 0 else ubs[:, m, 512 * (c - 1):512 * c]
                    P.op("act", "activation", R=[psb[bank]], W=[b_u], out=u_dst, in_=ps[bank][:], func=AF.Copy)
                if c > 0:
                    P.dma("sp", xu.rearrange("(m p) t -> p m t", p=128)[:, :, 512 * (c - 1):512 * c], ubs[:, :, 512 * (c - 1):512 * c],
                          R=[b_u], W=[b_xu])
                if c == 2:
                    P.coll(xu, xu_g, R=[b_xu], W=[b_xug])
                for br, off in ((0, 512), (1, 640)):
                    bank = nb()
                    P.mm([dict(out=ps[bank][:], lhsT=wA[:, kt, off:off + 128], rhs=hT[:, kt, :],
                               start=(kt == 0), stop=(kt == 7)) for kt in range(8)], R=[b_wA, b_hT], W=[psb[bank]])
                    if c == 0:
                        if br == 0:
                            P.op("act", "activation", R=[psb[bank]], W=[Bkp], out=kpT[:, 0, :], in_=ps[bank][:], func=AF.Copy)
                        else:
                            rmsnorm_fm(bank, 512, qknT[:, 1:2], kpT[:, 1, :], [Bkp])
                    else:
                        p0 = 512 * (c - 1)
                        if br == 0:
                            P.op("act", "activation", R=[psb[bank]], W=[b_x32], out=x32[:], in_=ps[bank][:], func=AF.Copy)
                        else:
                            rmsnorm_fm(bank, 512, qknT[:, 1:2], x32[:], [b_x32])
                        rope_fm(512, p0, ksT[:, br, p0:p0 + 512], [b_ks])
                ncol, coff = (512, 512) if c == 0 else (256, 768)
                for tt in range(4):
                    bank = nb()
                    P.mm([dict(out=ps[bank][:, 0:ncol], lhsT=hT[:, kt, 128 * tt:128 * tt + 128], rhs=wA[:, kt, coff:1024],
                               start=(kt == 0), stop=(kt == 7)) for kt in range(8)], R=[b_wA, b_hT], W=[psb[bank]])
                    pb = ps[bank]
                    if c == 0:
                        s_, t_ = tt // 2, tt % 2
                        ct = ctok[tt % 2]
                        b_ct = P.b("ctok", tt % 2)
                        P.op("act", "activation", R=[psb[bank]], W=[b_ct], out=ct[:, 0, :], in_=pb[:, 0:128], func=AF.Copy)
                        P.op("act", "activation", R=[psb[bank]], W=[b_ct], out=ct[:, 1, :], in_=pb[:, 256:384], func=AF.Copy)
                        P.op("act", "activation", R=[psb[bank]], W=[b_ct], out=ct[:, 3, :], in_=pb[:, 384:512], func=AF.Copy)
                        P.op("act", "activation", R=[psb[bank]], W=[b_sq128], out=sq128[:], in_=pb[:, 128:256], func=AF.Square)
                        P.op("dve", "reduce_sum", R=[b_sq128], W=[b_ssq], out=ssq[:], in_=sq128[:].rearrange("p (g d) -> p g d", g=2),
                             axis=mybir.AxisListType.X)
                        P.op("act", "activation", R=[b_ssq, Bc], W=[b_ssq], out=ssq[:], in_=ssq[:], func=AF.Sqrt,
                             bias=cst[:, 1:2], scale=1.0 / 64)
                        P.op("dve", "reciprocal", R=[b_ssq], W=[b_ssq], out=ssq[:], in_=ssq[:])
                        for g in range(2):
                            P.op("dve", "scalar_tensor_tensor", R=[psb[bank], b_ssq, b_knr], W=[b_ct],
                                 out=ct[:, 2, 64 * g:64 * g + 64], in0=pb[:, 128 + 64 * g:192 + 64 * g], scalar=ssq[:, g:g + 1],
                                 in1=knr[:], op0=MUL, op1=MUL)
                        P.dma("sp", co[s_, l][:, 128 * t_:128 * t_ + 128, :].rearrange("f t c -> t f c"), ct[:], R=[b_ct], is_out=True)
                        P.op("dve", "tensor_copy", R=[psb[bank]], W=[Bvp], out=vp[:, tt, 0, :, 0:64],
                             in_=pb[:, 256:384].rearrange("p (g d) -> p g d", g=2))
                        P.op("dve", "tensor_copy", R=[psb[bank]], W=[Bvp], out=vp[:, tt, 1, :, 0:64],
                             in_=pb[:, 384:512].rearrange("p (g d) -> p g d", g=2))
                    else:
                        P.op("act", "activation", R=[psb[bank]], W=[b_vs], out=vs[:, 4 * (c - 1) + tt, :], in_=pb[:, 0:256], func=AF.Copy)
                if c > 0:
                    p0 = 512 * (c - 1)
                    P.dma("sp", xk.rearrange("(b p) t -> p b t", p=128)[:, :, p0:p0 + 512], ksT[:, :, p0:p0 + 512], R=[b_ks], W=[b_xk])
                    P.dma("sp", xv.rearrange("(t p) c -> p t c", p=128)[:, 4 * (c - 1):4 * c, :], vs[:, 4 * (c - 1):4 * c, :], R=[b_vs], W=[b_xv])
            dump("ubf%d" % l, ubf[:, 0, :], [128, 512], [b_u], BF16)
            dump("ubs%d" % l, ubs[:, 0, :], [128, 1024], [b_u], BF16)
            dump("kpT%d" % l, kpT[:].rearrange("p a t -> p (a t)"), [128, 1024], [Bkp], BF16)
            dump("ksT%d" % l, ksT[:].rearrange("p a t -> p (a t)"), [128, 2048], [b_ks], BF16)
            late_colls = [lambda: P.coll(xk, xk_g, R=[b_xk], W=[b_xkg]), lambda: P.coll(xv, xv_g, R=[b_xv], W=[b_xvg])]
            if stop_after == "A":
                P.barrier()
                A.release(mLay)
                return
            P.barrier()
            A.release(mA)

            tabE = [A.alloc("tabE", [128, 2, 512], F32) for _ in range(4)]
            tabD = [A.alloc("tabD", [128, 2, 512], F32) for _ in range(4)]
            ang = A.alloc("ang", [128, 512], F32)
            g1 = A.alloc("g1", [128, 512], F32)
            g2 = A.alloc("g2", [128, 512], F32)
            kI = nc.alloc_sbuf_tensor_at("kIalias%d" % l, [128, 512], I32, offset=A.last_off)
            q1 = A.alloc("q1", [128, 512], BF16)
            q2 = A.alloc("q2", [128, 512], BF16)
            mre = A.alloc("mre", [128, 512], F32)
            mim = A.alloc("mim", [128, 512], F32)
            zre2 = [A.alloc("zre", [128, 512], F32) for _ in range(2)]
            zim2 = [A.alloc("zim", [128, 512], F32) for _ in range(2)]
            b_zre2 = [P.b("zre", i_) for i_ in range(2)]
            b_zim2 = [P.b("zim", i_) for i_ in range(2)]
            b_q1, b_q2 = P.b("q1"), P.b("q2")
            jobn = [0]
            p1 = A.alloc("p1", [128, 512], F32)
            p2 = A.alloc("p2", [128, 512], F32)
            sre = A.alloc("sre", [128, 512], BF16)
            nsi = A.alloc("nsi", [128, 512], BF16)
            useq = A.alloc("useq", [128, 4096], BF16)
            accp = A.alloc("accp", [128, 4, 512], F32)
            accs = A.alloc("accs", [128, 4096], F32)
            fin = A.alloc("fin", [128, 2, 2, 2, 16], F32)
            finT = A.alloc("finT", [128, 128], F32)
            tt2 = A.alloc("tt2", [128, 2], F32)
            (b_ang, b_g1, b_g2, b_mre, b_mim, b_zreX, b_zimX, b_p1, b_p2, b_sre, b_nsi, b_urp, b_useq, b_urs, b_accp,
             b_accs, b_fin, b_tt2, b_xy, b_xyg, b_ypd) = (P.b(n) for n in (
                 "ang", "g1", "g2", "mre", "mim", "zre", "zim", "p1", "p2", "sre", "nsi", "urp", "useq", "urs", "accp",
                 "accs", "fin", "tt2", "xy", "xyg", "ypd"))
            del b_urp, b_urs
            b_tE = [P.b("tabE", a) for a in range(4)]
            b_tD = [P.b("tabD", a) for a in range(4)]
            P.op("pool", "memset", W=[b_accp], ap=accp[:], constant=0.0)
            P.op("pool", "memset", W=[b_accs], ap=accs[:], constant=0.0)
            for rr_ in range(4):
                dyn_dma("sp", lambda e, rr_=rr_, useq=useq: (useq[:, 1024 * rr_:1024 * rr_ + 1024],
                                                  xu_g[bass.ds(pid4(e) * 128 + 512 * rr_, 128), :]),
                        R=[b_xug], W=[b_useq])
            if stop_after == "B1":
                P.barrier()
                A.release(mAB)
                A.release(mLay)
                return

            def gen_table(a, e_, jsel, part=0):
                E, D = tabE[a], tabD[a]
                W_ = 256 if jsel else 512
                th_ = CS[:, THR, e_:e_ + 1]
                if part in (0, 1):
                    P.op("dve", "tensor_scalar", R=[b_cs, Bc], W=[b_ang], out=ang[:, :W_], in0=jT[:, jsel, :W_], scalar1=th_, scalar2=None, op0=MUL)
                    P.op("dve", "tensor_scalar", R=[b_ang], W=[b_g1], out=g1[:, :W_], in0=ang[:, :W_], scalar1=1.0 / TWO_PI, scalar2=None, op0=MUL)
                    P.op("dve", "tensor_copy", R=[b_g1], W=[b_g2], out=kI[:, :W_], in_=g1[:, :W_])
                    P.op("dve", "tensor_copy", R=[b_g2], W=[b_g1], out=g1[:, :W_], in_=kI[:, :W_])
                    P.op("dve", "scalar_tensor_tensor", R=[b_g1, b_ang], W=[b_ang], out=ang[:, :W_], in0=g1[:, :W_], scalar=-TWO_PI, in1=ang[:, :W_],
                         op0=MUL, op1=ADD)
                    P.op("dve", "tensor_scalar", R=[b_ang], W=[b_ang], out=ang[:, :W_], in0=ang[:, :W_], scalar1=PI_LO, scalar2=-PI_LO, op0=ALU.min, op1=ALU.max)
                    P.op("act", "activation", R=[b_ang], W=[b_tD[a]], out=D[:, 1, :W_], in_=ang[:, :W_], func=AF.Sin)
                    P.op("act", "activation", R=[b_ang], W=[b_g1], out=g1[:, :W_], in_=ang[:, :W_], func=AF.Abs)
                    P.op("act", "activation", R=[b_g1, Bc], W=[b_tD[a]], out=D[:, 0, :W_], in_=g1[:, :W_], func=AF.Sin, scale=-1.0, bias=cst[:, 2:3])
                    P.op("act", "activation", R=[b_tD[a], b_cs], W=[b_g2], out=g2[:, :W_], in_=D[:, 1, :W_], func=AF.Identity,
                         scale=CS[:, CFIM, e_:e_ + 1])
                    P.op("act", "activation", R=[b_tD[a], b_cs], W=[b_g1], out=g1[:, :W_], in_=D[:, 1, :W_], func=AF.Identity,
                         scale=CS[:, CFRE, e_:e_ + 1])
                    if jsel:
                        P.op("act", "activation", R=[b_tD[a]], W=[b_tD[a]], out=D[:, :, 256:512], in_=D[:, :, 0:256], func=AF.Copy)
                if part == 1:
                    return
                cre_, cim_ = CS[:, CFRE, e_:e_ + 1], CS[:, CFIM, e_:e_ + 1]
                P.op("dve", "scalar_tensor_tensor", R=[b_tD[a], b_cs, b_g2], W=[b_tE[a]], out=E[:, 0, :W_], in0=D[:, 0, :W_], scalar=cre_,
                     in1=g2[:, :W_], op0=MUL, op1=ADD)
                P.op("dve", "scalar_tensor_tensor", R=[b_tD[a], b_cs, b_g1], W=[b_tE[a]], out=E[:, 1, :W_], in0=D[:, 0, :W_], scalar=cim_,
                     in1=g1[:, :W_], op0=MUL, op1=SUB)
                if jsel:
                    P.op("act", "activation", R=[b_tE[a]], W=[b_tE[a]], out=E[:, :, 256:512], in_=E[:, :, 0:256], func=AF.Copy)

            def pair_job(a, e_, usrc, Ru, segs, own_j, bC):
                E, D = tabE[a], tabD[a]
                zi = jobn[0] % 2
                jobn[0] += 1
                zre, zim, b_zre, b_zim = zre2[zi], zim2[zi], b_zre2[zi], b_zim2[zi]
                bA, bB = (4, 5) if zi == 0 else (0, 1)
                for bnk, ri_ in ((bA, 0), (bB, 1)):
                    P.mm([dict(out=ps[bnk][:, c0_:c0_ + src_.shape[1]], lhsT=WB[:, e_, ri_, :], rhs=src_, start=True, stop=True)
                          for (c0_, src_) in usrc], R=[b_WB] + Ru, W=[psb[bnk]])
                RT = [b_tE[a]]
                P.op("dve", "tensor_tensor", R=[psb[bA]] + RT, W=[b_p1], out=p1[:], in0=ps[bA][:], in1=E[:, 0, :], op=MUL)
                P.op("dve", "tensor_tensor", R=[psb[bB]] + RT, W=[b_p2], out=p2[:], in0=ps[bB][:], in1=E[:, 1, :], op=MUL)
                P.op("dve", "tensor_tensor", R=[psb[bB]] + RT, W=[b_mim], out=mim[:], in0=ps[bB][:], in1=E[:, 0, :], op=MUL)
                P.op("dve", "tensor_tensor", R=[b_p1, b_p2], W=[b_mre], out=mre[:], in0=p1[:], in1=p2[:], op=SUB)
                P.op("dve", "tensor_tensor", R=[psb[bA]] + RT, W=[b_p1], out=p1[:], in0=ps[bA][:], in1=E[:, 1, :], op=MUL)
                P.op("dve", "tensor_tensor", R=[b_p1, b_mim], W=[b_mim], out=mim[:], in0=mim[:], in1=p1[:], op=ADD)
                rcol = CS[:, RR, e_:e_ + 1]
                for (c0, c1) in segs:
                    n_ = c1 - c0
                    i_re = 0.0 if own_j is None else car[:, 0, own_j:own_j + 1]
                    i_im = 0.0 if own_j is None else car[:, 1, own_j:own_j + 1]
                    P.op("dve", "tensor_tensor_scan", R=[b_mre, b_cs, b_car], W=[b_zre], out=zre[:, c0:c1],
                         data0=rcol.to_broadcast([128, n_]), data1=mre[:, c0:c1], initial=i_re, op0=MUL, op1=ADD)
                    P.op("dve", "tensor_tensor_scan", R=[b_mim, b_cs, b_car], W=[b_zim], out=zim[:, c0:c1],
                         data0=rcol.to_broadcast([128, n_]), data1=mim[:, c0:c1], initial=i_im, op0=MUL, op1=ADD)
                ID = AF.Identity
                if own_j is not None:
                    c5, s5 = CS[:, C512, e_:e_ + 1], CS[:, S512, e_:e_ + 1]
                    zlr, zli = zre[:, 511:512], zim[:, 511:512]
                    P.op("act", "activation", R=[b_zim, b_cs], W=[b_tt2], out=tt2[:, 0:1], in_=zli, func=ID, scale=s5)
                    P.op("act", "activation", R=[b_tt2], W=[b_tt2], out=tt2[:, 0:1], in_=tt2[:, 0:1], func=ID, scale=-1.0)
                    P.op("act", "activation", R=[b_zre, b_cs, b_tt2], W=[b_car], out=car[:, 0, own_j:own_j + 1], in_=zlr, func=ID,
                         scale=c5, bias=tt2[:, 0:1])
                    P.op("act", "activation", R=[b_zim, b_cs], W=[b_tt2], out=tt2[:, 1:2], in_=zli, func=ID, scale=c5)
                    P.op("act", "activation", R=[b_zre, b_cs, b_tt2], W=[b_car], out=car[:, 1, own_j:own_j + 1], in_=zlr, func=ID,
                         scale=s5, bias=tt2[:, 1:2])
                else:
                    cc, ss = D[:, 0, 255:256], D[:, 1, 255:256]
                    for s_ in range(2):
                        zfr, zfi = zre[:, 256 * s_ + 255:256 * s_ + 256], zim[:, 256 * s_ + 255:256 * s_ + 256]
                        P.op("act", "activation", R=[b_zim, b_tD[a]], W=[b_tt2], out=tt2[:, 0:1], in_=zfi, func=ID, scale=ss)
                        P.op("act", "activation", R=[b_tt2], W=[b_tt2], out=tt2[:, 0:1], in_=tt2[:, 0:1], func=ID, scale=-1.0)
                        P.op("act", "activation", R=[b_zre, b_tD[a], b_tt2], W=[b_fin], out=fin[:, s_:s_ + 1, e_ // 16, 0, e_ % 16],
                             in_=zfr, func=ID, scale=cc, bias=tt2[:, 0:1])
                        P.op("act", "activation", R=[b_zim, b_tD[a]], W=[b_tt2], out=tt2[:, 1:2], in_=zfi, func=ID, scale=cc)
                        P.op("act", "activation", R=[b_zre, b_tD[a], b_tt2], W=[b_fin], out=fin[:, s_:s_ + 1, e_ // 16, 1, e_ % 16],
                             in_=zfr, func=ID, scale=ss, bias=tt2[:, 1:2])
                RD = [b_tD[a]]

                def stage2(a=a, e_=e_, zre=zre, zim=zim, b_zre=b_zre, b_zim=b_zim, D=D, RD=RD, bC=bC):
                    P.op("pool", "tensor_tensor", R=[b_zre] + RD, W=[b_sre], out=sre[:], in0=zre[:], in1=D[:, 0, :], op=MUL)
                    P.op("pool", "tensor_tensor", R=[b_zim] + RD, W=[b_q1], out=q1[:], in0=zim[:], in1=D[:, 1, :], op=MUL)
                    P.op("pool", "tensor_tensor", R=[b_zre] + RD, W=[b_nsi], out=nsi[:], in0=zre[:], in1=D[:, 1, :], op=MUL)
                    P.op("pool", "tensor_tensor", R=[b_zim] + RD, W=[b_q2], out=q2[:], in0=zim[:], in1=D[:, 0, :], op=MUL)
                    oc = ps[bC][32 * a:32 * a + 32, :]
                    tp = (0, 32 * a)
                    P.mm([dict(out=oc, lhsT=WC[:, e_, 0, :], rhs=sre[:], start=True, stop=False, tile_position=tp),
                          dict(out=oc, lhsT=WCn[:, e_, :], rhs=q1[:], start=False, stop=False, tile_position=tp),
                          dict(out=oc, lhsT=WC[:, e_, 1, :], rhs=nsi[:], start=False, stop=False, tile_position=tp),
                          dict(out=oc, lhsT=WC[:, e_, 1, :], rhs=q2[:], start=False, stop=True, tile_position=tp)],
                         R=[b_WC, b_sre, b_nsi, b_q1, b_q2], W=[psb[bC]])
                pend.append(stage2)

            pend = []

            def flush(keep):
                while len(pend) > keep:
                    pend.pop(0)()

            nq = [0]
            pj = [(d_, quad, a) for d_ in range(2) for quad in range(4) for a in range(4)]
            gen_table(0, 0, 1)
            for i_, (d_, quad, a) in enumerate(pj):
                if a == 0:
                    bC = 6 + (nq[0] % 2)
                    nq[0] += 1
                e_ = 16 * d_ + 4 * quad + a
                if i_ + 1 < len(pj):
                    dn, qn, an = pj[i_ + 1]
                    gen_table(an, 16 * dn + 4 * qn + an, 1, part=1)
                if d_ == 0:
                    src = [(0, ubf[:, quad, 0:512])]
                else:
                    src = [(256 * s_, ubf[:, quad, 256 * s_:256 * s_ + 256][:, ::-1]) for s_ in range(2)]
                pair_job(a, e_, src, [b_u], [(0, 256), (256, 512)], None, bC)
                flush(1)
                if i_ + 1 < len(pj):
                    gen_table(an, 16 * dn + 4 * qn + an, 1, part=2)
                if a == 3:
                    def acc_p(d_=d_, quad=quad, bC=bC):
                        for s_ in range(2):
                            av = accp[:, quad, 256 * s_:256 * s_ + 256]
                            if d_:
                                av = av[:, ::-1]
                            P.op("dve", "tensor_tensor", R=[psb[bC], b_accp], W=[b_accp], out=av, in0=av,
                                 in1=ps[bC][:, 256 * s_:256 * s_ + 256], op=ADD)
                    pend.append(acc_p)
                    if nq[0] in (2, 5):
                        pend.append(late_colls.pop(0))
            flush(0)
            if stop_after == "B2":
                P.barrier()
                A.release(mAB)
                A.release(mLay)
                return
            P.dma("sp", yp_d, accp[:], R=[b_accp], W=[b_ypd])
            P.mm([dict(out=ps[4][:, 0:128], lhsT=fin[:].rearrange("p s d r e -> p (s d r e)"), rhs=identF[:], start=True, stop=True)],
                 R=[b_fin, Bc], W=[psb[4]])
            b_finT = P.b("finT")
            P.op("act", "activation", R=[psb[4]], W=[b_finT], out=finT[:], in_=ps[4][:, 0:128], func=AF.Copy)
            for s_ in range(2):
                P.dma("sp", so[s_, l].rearrange("d r (pi gl) p -> (d r pi) (gl p)", gl=2), finT[64 * s_:64 * s_ + 64, :],
                      R=[b_finT], is_out=True)
            dump("accp%d" % l, accp[:].rearrange("p a t -> p (a t)"), [128, 2048], [b_accp])
            if stop_after == "B3":
                P.barrier()
                A.release(mAB)
                A.release(mLay)
                return
            mgen = mod_vectors_gen(l + 1) if l + 1 < nlayers else iter(())
            for d_ in range(2):
                if "nosamp" in DBGF:
                    break
                for a in range(4):
                    gen_table(a, 32 + 4 * d_ + a, 0)
                for ch in range(8):
                    if "1ch" in DBGF and ch:
                        break
                    bC = 6 + (nq[0] % 2)
                    nq[0] += 1
                    for a in range(4):
                        if d_ == 0:
                            src = [(0, useq[:, 512 * ch:512 * ch + 512])]
                        else:
                            src = [(0, useq[:, 4096 - 512 * ch - 512:4096 - 512 * ch][:, ::-1])]
                        pair_job(a, 32 + 4 * d_ + a, src, [b_useq], [(0, 512)], 4 * d_ + a, bC)
                        flush(1)
                        next(mgen, None)

                    def acc_s(d_=d_, ch=ch, bC=bC):
                        if d_ == 0:
                            av = accs[:, 512 * ch:512 * ch + 512]
                        else:
                            av = accs[:, 4096 - 512 * ch - 512:4096 - 512 * ch][:, ::-1]
                        P.op("dve", "tensor_tensor", R=[psb[bC], b_accs], W=[b_accs], out=av, in0=av, in1=ps[bC][:], op=ADD)
                    pend.append(acc_s)
                    if d_ == 1 and ch % 2 == 1:
                        def xfer(tb=3 - ch // 2):
                            P.dma("sp", xy[tb], accs[:, 1024 * tb:1024 * tb + 1024], R=[b_accs], W=[b_xy])
                            P.coll(xy[tb], xy_g[512 * tb:512 * tb + 512, :], R=[b_xy], W=[b_xyg])
                        pend.append(xfer)
            flush(0)
            for _ in mgen:
                pass
            dump("accs%d" % l, accs[:], [128, 4096], [b_accs])
            P.barrier()
            A.release(mAB)
            if stop_after == "B":
                A.release(mLay)
                return

            mC = A.mark()
            kgT = A.alloc("kgT", [128, 4352], BF16)
            vgA = A.alloc("vgA", [128, 34, 2, 65], BF16)
            kwT = A.alloc("kwT", [128, 1536], BF16)
            vwA = A.alloc("vwA", [128, 12, 2, 65], BF16)
            wglu = A.alloc("wglu", [128, 4, 512], BF16)
            ctb = A.alloc("ctb", [128, 2, 2, 128], BF16)
            b_kg, b_vg, b_kw, b_vw, b_wglu, b_ctb = (P.b(n) for n in ("kgT", "vgA", "kwT", "vwA", "wglu", "ctb"))
            P.dma("pool", wglu[:], w_glu[l].rearrange("(kt p) c -> p kt c", p=128), W=[b_wglu])

            def kv_assembly():
                P.op("pool", "memset", W=[b_vg], ap=vgA[:], constant=1.0)
                P.op("pool", "memset", W=[b_vw], ap=vwA[:], constant=1.0)
                P.dma("sp", kgT[:, 0:4096].rearrange("p (r t) -> p r t", r=4),
                      xk_g.rearrange("(r two p) t -> p r two t", two=2, p=128)[:, :, 1, :], R=[b_xkg], W=[b_kg])
                for g_ in range(2):
                    P.dma("sp", vgA[:, 0:32, g_, 0:64], xv_g[:, 128 + 64 * g_:192 + 64 * g_].rearrange("(t p) d -> p t d", p=128),
                          R=[b_xvg], W=[b_vg])
                dyn_dma("sp", lambda e, kwT=kwT: (kwT[:, 128:1152], xk_g[bass.ds(pid4(e) * 256, 128), :]), R=[b_xkg], W=[b_kw])
                dyn_dma("sp", lambda e, kwT=kwT: (kwT[:, 0:128], xk_g[bass.ds(((pid4(e) + 3) % 4) * 256, 128), 896:1024]), R=[b_xkg], W=[b_kw])
                dyn_dma("sp", lambda e, kwT=kwT: (kwT[:, 1152:1280], xk_g[bass.ds(((pid4(e) + 1) % 4) * 256, 128), 0:128]), R=[b_xkg], W=[b_kw])
                for g_ in range(2):
                    dyn_dma("sp", lambda e, g_=g_, vwA=vwA: (vwA[:, 1:9, g_, 0:64],
                                                    xv_g[bass.ds(pid4(e) * 1024, 1024), 64 * g_:64 * g_ + 64].rearrange("(t p) d -> p t d", p=128)),
                            R=[b_xvg], W=[b_vw])
                dyn_dma("sp", lambda e, vwA=vwA: (vwA[:, 0, :, 0:64],
                                         xv_g[bass.ds(((pid4(e) + 3) % 4) * 1024 + 896, 128), 0:128].rearrange("p (g d) -> p g d", g=2)),
                        R=[b_xvg], W=[b_vw])
                dyn_dma("sp", lambda e, vwA=vwA: (vwA[:, 9, :, 0:64],
                                         xv_g[bass.ds(((pid4(e) + 1) % 4) * 1024, 128), 0:128].rearrange("p (g d) -> p g d", g=2)),
                        R=[b_xvg], W=[b_vw])
                for g_ in range(2):
                    P.dma("pool", vwA[:, 10:12, g_, 0:64], cv[l, 0][:, 64 * g_:64 * g_ + 64].rearrange("(t p) d -> p t d", p=128), W=[b_vw])
                    P.dma("pool", vgA[:, 32:34, g_, 0:64], cv[l, 1][:, 64 * g_:64 * g_ + 64].rearrange("(t p) d -> p t d", p=128), W=[b_vg])
                P.dma("pool", ctb[:], ck[l].rearrange("b (t p) c -> p b t c", p=128), W=[b_ctb])
                for br in range(2):
                    for t_ in range(2):
                        P.mm([dict(out=ps[0][:, 0:128], lhsT=ctb[:, br, t_, :], rhs=cm[:, IDB, :], start=True, stop=True)],
                             R=[b_ctb, Bc], W=[psb[0]])
                        if br == 0:
                            P.op("act", "activation", R=[psb[0]], W=[b_kw], out=kwT[:, 1280 + 128 * t_:1408 + 128 * t_], in_=ps[0][:, 0:128], func=AF.Copy)
                        else:
                            P.op("act", "activation", R=[psb[0]], W=[b_kg], out=kgT[:, 4096 + 128 * t_:4224 + 128 * t_], in_=ps[0][:, 0:128], func=AF.Copy)


            _b2 = [0, 2]

            def nb2():
                _b2[0] = (_b2[0] + 1) % _b2[1]
                return _b2[0]

            wvd = w_down[l].rearrange("(kt p) c -> p kt c", p=128)
            wvu = w_up[l].rearrange("(kt p) c -> p kt c", p=128)
            wvo = w_out[l].rearrange("(kt p) c -> p kt c", p=128)
            wvb0 = w_br[l, 0].rearrange("(kt p) c -> p kt c", p=128)
            wvb1 = w_br[l, 1].rearrange("(h d) c -> d h c", d=64)
            wvb2 = w_br[l, 2].rearrange("(h d) c -> d h c", d=64)

            for c in range(3):
                k = k_of[c]
                cols = slice(512 * c, 512 * c + 512)
                p0 = 512 * (c - 1)
                if c == 1:
                    kv_assembly()
                mCh = A.mark()
                hT = A.alloc("hT", [128, 8, 512], BF16)
                for ft in range(8):
                    P.op("dve", "tensor_scalar", R=[xb(ft, c), Bmod], W=[b_hT], out=hT[:, ft, :], in0=xT[:, ft, cols],
                         scalar1=modv[:, 8 + ft, k:k + 1], scalar2=modv[:, ft, k:k + 1], op0=MUL, op1=ADD)
                mC12 = A.mark()
                ya = A.alloc("ya", [128, 4, 512], BF16)
                ywT = A.alloc("ywT", [64, 8, 512], BF16)
                ygT = A.alloc("ygT", [64, 8, 512], BF16)
                b_ya, b_yw, b_yg = P.b("ya"), P.b("ywT"), P.b("ygT")
                _b2[1] = 6
                cpend = []

                def cflush(keep):
                    while len(cpend) > keep:
                        cpend.pop(0)()
                mC1 = A.mark()
                wblk = [A.alloc("wblk", [128, 8, 512], BF16) for _ in range(2)]
                b_wblk = [P.b("wblk", i_) for i_ in range(2)]
                ssm_in = A.alloc("ssm_in", [128, 4, 512], F32)
                gb = A.alloc("gb", [128, 4, 512], BF16)
                qwT = A.alloc("qwT", [128, 4, 512], BF16)
                qgT = A.alloc("qgT", [128, 4, 512], BF16)
                x32 = A.alloc("x32", [128, 512], F32)
                sqb = A.alloc("sqb", [128, 512], BF16)
                rsf = A.alloc("rsf", [128, 512], F32)
                xbb = A.alloc("xbb", [128, 512], BF16)
                t1f = A.alloc("t1f", [128, 512], F32)
                t2f = A.alloc("t2f", [128, 512], F32)
                sgf = A.alloc("sgf", [128, 512], F32)
                pT = [A.alloc("pT", [128, 512], BF16) for _ in range(4)]
                drow = A.alloc("drow", [128, 512], F32)
                bcs = A.alloc("bcs", [64, 512], F32)
                b_ssm, b_gb, b_qw, b_qg, b_sgf, b_drow, b_bcs = (P.b(n) for n in ("ssm_in", "gb", "qwT", "qgT", "sgf", "drow", "bcs"))
                b_pT = [P.b("pT", i_) for i_ in range(4)]
                if c == 0:
                    P.dma("sp", ssm_in[:], yp_d, R=[b_ypd], W=[b_ssm])
                else:
                    dyn_dma("sp", lambda e, p0=p0, ssm_in=ssm_in: (ssm_in[:], xy_g[bass.ds(pid4(e) * 512, 512), p0:p0 + 512].rearrange("(q p) t -> p q t", p=128)),
                            R=[b_xyg], W=[b_ssm])
                for bi, kind in enumerate(("u", "qw", "qg")):
                    w = wblk[bi % 2]
                    bw = b_wblk[bi % 2]
                    if kind == "u":
                        P.dma("pool", w[:], wv[:, :, 0:512], W=[bw])
                    else:
                        c0 = 512 if kind == "qw" else 1280
                        for j_ in range(4):
                            for hf in range(2):
                                P.dma("pool", w[:, :, 128 * j_ + 64 * hf:128 * j_ + 64 * hf + 64],
                                      wv[:, :, c0 + 256 * hf + 64 * j_:c0 + 256 * hf + 64 * j_ + 64], W=[bw])
                    for m in range(4):
                        bank = nb2()
                        P.mm([dict(out=ps[bank][:], lhsT=w[:, kt, 128 * m:128 * m + 128], rhs=hT[:, kt, :],
                                   start=(kt == 0), stop=(kt == 7)) for kt in range(8)], R=[bw, b_hT], W=[psb[bank]])
                        def chain(kind=kind, m=m, bank=bank):
                            if kind == "u":
                                P.op("dve", "scalar_tensor_tensor", R=[psb[bank], Blay, b_ssm], W=[b_x32], out=x32[:], in0=ps[bank][:],
                                     scalar=dskT[:, m:m + 1], in1=ssm_in[:, m, :], op0=MUL, op1=ADD)
                                P.op("act", "activation", R=[b_x32], W=[b_t1], out=t1f[:], in_=x32[:], func=AF.Square)
                                P.op("dve", "tensor_scalar", R=[b_t1], W=[b_t1], out=t1f[:], in0=t1f[:], scalar1=0.044715, scalar2=1.0, op0=MUL, op1=ADD)
                                P.op("dve", "tensor_tensor", R=[b_t1, b_x32], W=[b_t1], out=t1f[:], in0=t1f[:], in1=x32[:], op=MUL)
                                P.op("act", "activation", R=[b_t1], W=[b_t2], out=t2f[:], in_=t1f[:], func=AF.Sigmoid, scale=1.5957691216057308)
                                P.op("dve", "tensor_tensor", R=[b_t2, b_x32], W=[b_gb], out=gb[:, m, :], in0=x32[:], in1=t2f[:], op=MUL)
                            elif kind == "qw":
                                if c == 0:
                                    P.op("act", "activation", R=[psb[bank]], W=[b_qw], out=qwT[:, m, :], in_=ps[bank][:], func=AF.Copy)
                                else:
                                    P.op("act", "activation", R=[psb[bank]], W=[b_x32], out=x32[:], in_=ps[bank][:], func=AF.Copy)
                                    rope_fm(512, p0, qwT[:, m, :], [b_qw])
                            else:
                                if c == 0:
                                    rmsnorm_fm(bank, 512, qknT[:, 0:1], qgT[:, m, :], [b_qg])
                                else:
                                    rmsnorm_fm(bank, 512, qknT[:, 0:1], x32[:], [b_x32])
                                    rope_fm(512, p0, qgT[:, m, :], [b_qg])
                        cpend.append(chain)
                        cflush(1)
                cflush(0)
                _b2[1] = 2
                for m in range(4):
                    bank = nb2()
                    P.mm([dict(out=ps[bank][:], lhsT=wglu[:, kt, 128 * m:128 * m + 128], rhs=gb[:, kt, :],
                               start=(kt == 0), stop=(kt == 3)) for kt in range(4)], R=[b_wglu, b_gb], W=[psb[bank]])
                    P.op("act", "activation", R=[psb[bank]], W=[b_sgf], out=sgf[:], in_=ps[bank][:], func=AF.Sigmoid)
                    P.op("dve", "tensor_tensor", R=[b_sgf, b_gb], W=[b_ya], out=ya[:, m, :], in0=gb[:, m, :], in1=sgf[:], op=MUL)
                if c == 1:
                    dump("ya%d" % l, ya[:, 0, :], [128, 512], [b_ya], BF16)
                    dump("qgT%d" % l, qgT[:, 0, :], [128, 512], [b_qg], BF16)

                acnt = [0]
                fin_pend = []

                def attn_core(N, qsrc, Rq, ktiles, bO, c0):
                    nt = len(ktiles)
                    LA = 2
                    for ti in range(nt + LA):
                        if ti < nt:
                            kT_ap, Rk, v_ap, Rv, mask = ktiles[ti]
                            bS = ti % 4
                            pt, bp = pT[ti % 4], b_pT[ti % 4]
                            P.mm([dict(out=ps[bS][:, :N], lhsT=kT_ap, rhs=qsrc, start=True, stop=True)], R=Rk + Rq, W=[psb[bS]])
                            P.op("act", "activation", R=[psb[bS]], W=[bp], out=pt[:, :N], in_=ps[bS][:, :N], func=AF.Exp, scale=0.125)
                            if mask is not None:
                                P.op("dve", "tensor_tensor", R=[bp, Bc], W=[bp], out=pt[:, :N], in0=pt[:, :N], in1=mask, op=MUL)
                        if ti >= LA:
                            tj = ti - LA
                            kT_ap, Rk, v_ap, Rv, mask = ktiles[tj]
                            P.mm([dict(out=ps[bO][0:65, c0:c0 + N], lhsT=v_ap, rhs=pT[tj % 4][:, :N], start=(tj == 0), stop=(tj == nt - 1))],
                                 R=Rv + [b_pT[tj % 4]], W=[psb[bO]])

                def attn_fin(N, h, sink, bO, out_ap, Wb):
                    def f():
                        if sink:
                            P.op("dve", "tensor_scalar", R=[psb[bO], Blay], W=[b_drow], out=drow[64:65, :N], in0=ps[bO][64:65, :N],
                                 scalar1=esink[64:65, h:h + 1], scalar2=None, op0=ADD)
                        else:
                            P.op("dve", "tensor_copy", R=[psb[bO]], W=[b_drow], out=drow[64:65, :N], in_=ps[bO][64:65, :N])
                        P.op("act", "activation", R=[b_drow], W=[b_drow], out=drow[64:65, :N], in_=drow[64:65, :N], func=AF.Ln)
                        P.op("act", "activation", R=[b_drow], W=[b_drow], out=drow[64:65, :N], in_=drow[64:65, :N], func=AF.Exp, scale=-1.0)
                        P.mm([dict(out=ps[6][0:64, :N], lhsT=onesF[64:65, 0:64], rhs=drow[64:65, :N], start=True, stop=True)],
                             R=[b_drow, Bc], W=[psb[6]])
                        P.op("act", "activation", R=[psb[6]], W=[b_bcs], out=bcs[:, :N], in_=ps[6][0:64, :N], func=AF.Copy)
                        P.op("dve", "tensor_tensor", R=[psb[bO], b_bcs], W=Wb, out=out_ap, in0=ps[bO][0:64, :N], in1=bcs[:, :N], op=MUL)
                    fin_pend.append(f)
                    while len(fin_pend) > 1:
                        fin_pend.pop(0)()

                def next_bO():
                    acnt[0] += 1
                    return 4 + (acnt[0] % 2)

                for h in range(8):
                    g, j = h // 4, h % 4
                    pr = slice(64 * g, 64 * g + 64)
                    if c == 0:
                        for br, (qT_, bq_, oT_, bo_) in enumerate(((qwT, b_qw, ywT, b_yw), (qgT, b_qg, ygT, b_yg))):
                            bO = next_bO()
                            for s_ in range(2):
                                sc_ = slice(256 * s_, 256 * s_ + 256)
                                kts = [(kpT[pr, br, 256 * s_ + 128 * t_:256 * s_ + 128 * t_ + 128], [Bkp],
                                        vp[:, 2 * s_ + t_, br, g, :], [Bvp], None) for t_ in range(2)]
                                attn_core(256, qT_[pr, j, sc_], [bq_], kts, bO, 256 * s_)
                            attn_fin(512, h, br == 0, bO, oT_[:, h, :], [bo_])
                    else:
                        bO = next_bO()
                        for qb in range(4):
                            nl = 4 * (c - 1) + qb
                            qc = slice(128 * qb, 128 * qb + 128)
                            kts = [(kwT[pr, 128 * nl:128 * nl + 128], [b_kw], vwA[:, nl, g, :], [b_vw], cm[:, MLF if nl == 0 else ML, :]),
                                   (kwT[pr, 128 * nl + 128:128 * nl + 256], [b_kw], vwA[:, nl + 1, g, :], [b_vw], None),
                                   (kwT[pr, 128 * nl + 256:128 * nl + 384], [b_kw], vwA[:, nl + 2, g, :], [b_vw], cm[:, MRL if nl == 7 else MR, :]),
                                   (kwT[pr, 1280:1408], [b_kw], vwA[:, 10, g, :], [b_vw], None),
                                   (kwT[pr, 1408:1536], [b_kw], vwA[:, 11, g, :], [b_vw], None)]
                            attn_core(128, qwT[pr, j, qc], [b_qw], kts, bO, 128 * qb)
                        attn_fin(512, h, True, bO, ywT[:, h, :], [b_yw])
                        bO = next_bO()
                        kts = [(kgT[pr, 128 * t_:128 * t_ + 128], [b_kg], vgA[:, t_, g, :], [b_vg], None) for t_ in range(34)]
                        attn_core(512, qgT[pr, j, :], [b_qg], kts, bO, 0)
                        attn_fin(512, h, False, bO, ygT[:, h, :], [b_yg])
                while fin_pend:
                    fin_pend.pop(0)()
                if c == 1:
                    dump("ywT%d" % l, ywT[:, 0, :], [64, 512], [b_yw], BF16)
                    dump("ygT%d" % l, ygT[:, 0, :], [64, 512], [b_yg], BF16)
                if c == 0:
                    dump("ywTp%d" % l, ywT[:, 0, :], [64, 512], [b_yw], BF16)
                    dump("ygTp%d" % l, ygT[:, 0, :], [64, 512], [b_yg], BF16)
                P.barrier()
                A.release(mC1)

                _b2[1] = 4
                wg = [A.alloc("wg", [128, 8, 3, 256], BF16) for _ in range(2)]
                wbs = [A.alloc("wbs", [128, 4, 256], BF16) for _ in range(2)]
                wbw = [A.alloc("wbw", [64, 8, 256], BF16) for _ in range(2)]
                wbg = [A.alloc("wbg", [64, 8, 256], BF16) for _ in range(2)]
                b_wm = [P.b("wm2", i_) for i_ in range(2)]
                mT = A.alloc("mT", [128, 8, 512], BF16)
                sgf2 = [A.alloc("sgf", [128, 512], F32) for _ in range(2)]
                b_sgf2 = [P.b("sgf2", i_) for i_ in range(2)]
                tmp2 = [A.alloc("tmp2", [128, 512], F32) for _ in range(2)]
                b_tmp2 = [P.b("tmp2", i_) for i_ in range(2)]
                nsg = [0]
                accf = A.alloc("accf", [128, 512], F32)
                tmpf = A.alloc("tmpf", [128, 512], F32)
                wo = [A.alloc("wo", [128, 8, 256], BF16) for _ in range(2)]
                b_wo = [P.b("wo", i_) for i_ in range(2)]
                sqs = [A.alloc("sqs", [128, 512], BF16) for _ in range(2)]
                zbs = [A.alloc("zbs", [128, 512], BF16) for _ in range(2)]
                b_sq = [P.b("sqs", i_) for i_ in range(2)]
                b_zb = [P.b("zbs", i_) for i_ in range(2)]
                m2 = A.alloc("m2", [128, 512], F32)
                lt = A.alloc("lt", [128, 512], F32)
                b_mT, b_acc, b_tmp, b_m2, b_lt = (P.b(n) for n in ("mT", "accf", "tmpf", "m2", "lt"))

                def layer_norm(gsel):
                    for ft in range(8):
                        sq, zb = sqs[ft % 2], zbs[ft % 2]
                        P.op("act", "activation", R=[xb(ft, c)], W=[b_sq[ft % 2]], out=sq[:], in_=xT[:, ft, cols], func=AF.Square)
                        P.op("act", "activation", R=[xb(ft, c)], W=[b_zb[ft % 2]], out=zb[:], in_=xT[:, ft, cols], func=AF.Copy)
                        P.mm([dict(out=ps[6][:], lhsT=lnmean[:], rhs=zb[:], start=(ft == 0), stop=(ft == 7))],
                             R=[b_zb[ft % 2], Bc], W=[psb[6]])
                        P.mm([dict(out=ps[7][:], lhsT=lnmean[:], rhs=sq[:], start=(ft == 0), stop=(ft == 7))],
                             R=[b_sq[ft % 2], Bc], W=[psb[7]])
                    P.op("act", "activation", R=[psb[6]], W=[b_m2], out=m2[:], in_=ps[6][:], func=AF.Square)
                    P.op("dve", "tensor_tensor", R=[psb[7], b_m2], W=[b_m2], out=m2[:], in0=ps[7][:], in1=m2[:], op=SUB)
                    P.op("act", "activation", R=[b_m2, Bc], W=[b_m2], out=m2[:], in_=m2[:], func=AF.Ln, bias=cst[:, 0:1], scale=1.0)
                    P.op("act", "activation", R=[b_m2], W=[b_m2], out=m2[:], in_=m2[:], func=AF.Exp, scale=-0.5)
                    for ft in range(8):
                        P.op("dve", "tensor_tensor", R=[xb(ft, c), psb[6]], W=[b_lt], out=lt[:], in0=xT[:, ft, cols], in1=ps[6][:], op=SUB)
                        P.op("dve", "tensor_tensor", R=[b_lt, b_m2], W=[b_lt], out=lt[:], in0=lt[:], in1=m2[:], op=MUL)
                        P.op("dve", "tensor_scalar", R=[b_lt, Blay], W=[xb(ft, c)], out=xT[:, ft, cols], in0=lt[:],
                             scalar1=lnpT[:, gsel, ft:ft + 1], scalar2=lnpT[:, gsel + 1, ft:ft + 1], op0=MUL, op1=ADD)

                for m in range(8):
                    i2 = (m // 2) % 2
                    mc = 128 * (m % 2)
                    bw = b_wm[i2]
                    if m % 2 == 0:
                        c2_ = 256 * (m // 2)
                        for kk in range(3):
                            P.dma("pool", wg[i2][:, :, kk, :], wv[:, :, 2048 + 1024 * kk + c2_:2048 + 1024 * kk + c2_ + 256], W=[bw])
                        P.dma("pool", wbs[i2][:], wvb0[:, :, c2_:c2_ + 256], W=[bw])
                        P.dma("pool", wbw[i2][:], wvb1[:, :, c2_:c2_ + 256], W=[bw])
                        P.dma("pool", wbg[i2][:], wvb2[:, :, c2_:c2_ + 256], W=[bw])
                    for kk in range(3):
                        bg_ = nb2()
                        P.mm([dict(out=ps[bg_][:], lhsT=wg[i2][:, kt, kk, mc:mc + 128], rhs=hT[:, kt, :], start=(kt == 0), stop=(kt == 7))
                              for kt in range(8)], R=[bw, b_hT], W=[psb[bg_]])
                        sgf, b_sgfc = sgf2[nsg[0] % 2], b_sgf2[nsg[0] % 2]
                        nsg[0] += 1
                        P.op("act", "activation", R=[psb[bg_]], W=[b_sgfc], out=sgf[:], in_=ps[bg_][:], func=AF.Sigmoid)
                        bp_ = 4 + (kk % 2)
                        if kk == 0:
                            P.mm([dict(out=ps[bp_][:], lhsT=wbs[i2][:, kt, mc:mc + 128], rhs=ya[:, kt, :], start=(kt == 0), stop=(kt == 3))
                                  for kt in range(4)], R=[bw, b_ya], W=[psb[bp_]])
                        else:
                            wsel, ysel, by_ = (wbw, ywT, b_yw) if kk == 1 else (wbg, ygT, b_yg)
                            P.mm([dict(out=ps[bp_][:], lhsT=wsel[i2][:, hh, mc:mc + 128], rhs=ysel[:, hh, :], start=(hh == 0), stop=(hh == 7))
                                  for hh in range(8)], R=[bw, by_], W=[psb[bp_]])
                        if kk == 0:
                            P.op("dve", "tensor_tensor", R=[psb[bp_], b_sgfc], W=[b_acc], out=accf[:], in0=ps[bp_][:], in1=sgf[:], op=MUL)
                        else:
                            P.op("dve", "tensor_tensor", R=[psb[bp_], b_sgfc], W=[b_tmp], out=tmpf[:], in0=ps[bp_][:], in1=sgf[:], op=MUL)
                            if kk == 1:
                                P.op("dve", "tensor_tensor", R=[b_acc, b_tmp], W=[b_acc], out=accf[:], in0=accf[:], in1=tmpf[:], op=ADD)
                            else:
                                P.op("dve", "tensor_tensor", R=[b_acc, b_tmp], W=[b_mT], out=mT[:, m, :], in0=accf[:], in1=tmpf[:], op=ADD)
                if c == 1:
                    dump("mT%d" % l, mT[:, 0, :], [128, 512], [b_mT], BF16)
                for ob in range(4):
                    wo_, bwo_ = wo[ob % 2], b_wo[ob % 2]
                    P.dma("pool", wo_[:], wvo[:, :, 256 * ob:256 * ob + 256], W=[bwo_])
                    for mm_ in range(2):
                        mo = 2 * ob + mm_
                        bank = nb2()
                        P.mm([dict(out=ps[bank][:], lhsT=wo_[:, kt, 128 * mm_:128 * mm_ + 128], rhs=mT[:, kt, :],
                                   start=(kt == 0), stop=(kt == 7)) for kt in range(8)], R=[bwo_, b_mT], W=[psb[bank]])
                        tq, btq = tmp2[mo % 2], b_tmp2[mo % 2]
                        P.op("act", "activation", R=[psb[bank], Bmod], W=[btq], out=tq[:], in_=ps[bank][:], func=AF.Copy,
                             scale=modv[:, 16 + mo, k:k + 1])
                        P.op("dve", "scalar_tensor_tensor", R=[xb(mo, c), btq], W=[xb(mo, c)], out=xT[:, mo, cols], in0=xT[:, mo, cols],
                             scalar=ALPHA, in1=tq[:], op0=MUL, op1=ADD)
                layer_norm(0)
                if c == 1:
                    dump("xln1_%d" % l, xT[:, 0, cols], [128, 512], [xb(0, c)])
                P.barrier()
                A.release(mC12)

                hid = A.alloc("hid", [128, 32, 512], BF16)
                wup = [A.alloc("wup", [128, 8, 512], BF16) for _ in range(2)]
                wdn = [A.alloc("wdn", [128, 4, 1024], BF16) for _ in range(2)]
                rl = [A.alloc("rl", [128, 512], F32) for _ in range(2)]
                tmp2 = [A.alloc("tmp2", [128, 512], F32) for _ in range(2)]
                sqs = [A.alloc("sqs", [128, 512], BF16) for _ in range(2)]
                zbs = [A.alloc("zbs", [128, 512], BF16) for _ in range(2)]
                m2 = A.alloc("m2", [128, 512], F32)
                lt = A.alloc("lt", [128, 512], F32)
                b_hid = P.b("hid")
                b_wup = [P.b("wup", i_) for i_ in range(2)]
                b_wdn = [P.b("wdn", i_) for i_ in range(2)]
                b_rl = [P.b("rl", i_) for i_ in range(2)]
                for ft in range(8):
                    P.op("dve", "tensor_scalar", R=[xb(ft, c), Bmod], W=[b_hT], out=hT[:, ft, :], in0=xT[:, ft, cols],
                         scalar1=modv[:, 32 + ft, k:k + 1], scalar2=modv[:, 24 + ft, k:k + 1], op0=MUL, op1=ADD)
                nr = 0
                for jb in range(8):
                    w, bw = wup[jb % 2], b_wup[jb % 2]
                    P.dma("pool", w[:], wvu[:, :, 512 * jb:512 * jb + 512], W=[bw])
                    for t_ in range(4):
                        bank = nb2()
                        P.mm([dict(out=ps[bank][:], lhsT=w[:, kt, 128 * t_:128 * t_ + 128], rhs=hT[:, kt, :],
                                   start=(kt == 0), stop=(kt == 7)) for kt in range(8)], R=[bw, b_hT], W=[psb[bank]])
                        r_, br_ = rl[nr % 2], b_rl[nr % 2]
                        nr += 1
                        P.op("act", "activation", R=[psb[bank]], W=[br_], out=r_[:], in_=ps[bank][:], func=AF.Relu)
                        P.op("dve" if nr % 2 else "pool", "tensor_tensor", R=[br_], W=[b_hid], out=hid[:, 4 * jb + t_, :], in0=r_[:], in1=r_[:], op=MUL)
                for kb in range(8):
                    w, bw = wdn[kb % 2], b_wdn[kb % 2]
                    P.dma("pool", w[:], wvd[:, 4 * kb:4 * kb + 4, :], W=[bw])
                    for mo in range(8):
                        P.mm([dict(out=ps[mo][:], lhsT=w[:, kq, 128 * mo:128 * mo + 128], rhs=hid[:, 4 * kb + kq, :],
                                   start=(kb == 0 and kq == 0), stop=(kb == 7 and kq == 3)) for kq in range(4)],
                             R=[bw, b_hid], W=[psb[mo]])
                for mo in range(8):
                    tq, btq = tmp2[mo % 2], b_tmp2[mo % 2]
                    P.op("act", "activation", R=[psb[mo], Bmod], W=[btq], out=tq[:], in_=ps[mo][:], func=AF.Copy,
                         scale=modv[:, 40 + mo, k:k + 1])
                    P.op("dve", "scalar_tensor_tensor", R=[xb(mo, c), btq], W=[xb(mo, c)], out=xT[:, mo, cols], in0=xT[:, mo, cols],
                         scalar=ALPHA, in1=tq[:], op0=MUL, op1=ADD)
                layer_norm(2)
                P.barrier()
                A.release(mCh)
            P.barrier()
            A.release(mLay)

        for l in range(nlayers):
            layer(l)

        m0 = A.mark()
        ytok = [A.alloc("ytok", [128, 1024], F32) for _ in range(2)]
        for tt in range(12):
            yt = ytok[tt % 2]
            by = P.b("ytok", tt % 2)
            c = tt // 4
            for hh in range(2):
                bank = (2 * tt + hh) % 2
                P.mm([dict(out=ps[bank][:, 128 * q:128 * q + 128],
                           lhsT=xT[:, 4 * hh + q, 128 * tt:128 * tt + 128],
                           rhs=identF[:], start=True, stop=True) for q in range(4)],
                     R=[xb(ft, c) for ft in range(4 * hh, 4 * hh + 4)] + [Bc], W=[psb[bank]])
                if hh:
                    P.op("act", "activation", R=[psb[bank]], W=[by], out=yt[:, 512:1024], in_=ps[bank][:], func=AF.Copy)
                else:
                    P.op("dve", "tensor_copy", R=[psb[bank]], W=[by], out=yt[:, 0:512], in_=ps[bank][:])
            P.dma("sp", yo[128 * tt:128 * tt + 128, :], yt[:], R=[by], is_out=True)
        A.release(m0)
        P.finish(block)
    return nc, A.peak


def _consts():
    ident = np.eye(128, dtype=np.float32)
    rotm = np.zeros((128, 128), np.float32)
    for hb in range(2):
        for j in range(32):
            rotm[64 * hb + j + 32, 64 * hb + j] = -1.0
            rotm[64 * hb + j, 64 * hb + j + 32] = 1.0
    hmean = np.zeros((128, 128), np.float32)
    hmean[:64, :64] = 1.0 / 64
    hmean[64:, 64:] = 1.0 / 64
    jj = np.arange(128)
    mL = (jj[:, None] >= jj[None, :]).astype(np.float32)
    mR = (jj[:, None] <= jj[None, :]).astype(np.float32)
    return ident, rotm, hmean, mL, mR


def _rope_tables(i):
    t = 1024 * i + np.arange(1024)
    row = (t // 64).astype(np.float32)
    col = (t % 64).astype(np.float32)
    inv = (10000.0 ** (-np.arange(16, dtype=np.float32) / 16)).astype(np.float32)
    ang = np.concatenate([row[:, None] * inv, col[:, None] * inv], axis=-1).astype(np.float32)
    c, s = np.cos(ang).astype(np.float32), np.sin(ang).astype(np.float32)
    p = np.arange(128) % 32
    return np.stack([c[:, p].T, s[:, p].T]).astype(np.float32)


def _pair_list(i):
    es = [(e // 16, e % 16) for e in range(32)]
    es += [(j // 4, 4 * i + j % 4) for j in range(8)]
    return es


def _host_inputs(inp, r, shared):
    f = np.float32
    b, i = r // 4, r % 4
    d = dict(shared)
    d["xin"] = np.ascontiguousarray(np.concatenate(
        [inp["x_prompt"][2 * r:2 * r + 2].reshape(512, 1024), inp["x_sample"][b, 1024 * i:1024 * (i + 1)]], 0), f)
    cond = np.stack([inp["c_ctx"], inp["c"][b]])
    d["condT"] = np.ascontiguousarray(cond.reshape(2, 8, 128).transpose(2, 1, 0), f)
    pl = _pair_list(i)
    lamh = np.zeros((2, 128, 3, 40), f)
    wbc = np.zeros((2, 4, 32, 10, 2, 128), f)
    wcc = np.zeros((2, 128, 40, 2, 32), f)
    s0 = np.zeros((2, 128, 2, 8), f)
    for l in range(2):
        for e, (dd, pi) in enumerate(pl):
            for gl in range(2):
                g = 2 * pi + gl
                n0 = 64 * gl
                lamh[l, n0:n0 + 64, 0, e] = inp["ssm_lam_re"][l, dd, g]
                lamh[l, n0:n0 + 64, 1, e] = inp["ssm_lam_im"][l, dd, g]
                lamh[l, n0:n0 + 64, 2, e] = inp["ssm_log_step"][l, dd, g]
                wbc[l, e % 4, 16 * gl:16 * gl + 16, e // 4, 0, n0:n0 + 64] = inp["ssm_b_re"][l, dd, g].T
                wbc[l, e % 4, 16 * gl:16 * gl + 16, e // 4, 1, n0:n0 + 64] = inp["ssm_b_im"][l, dd, g].T
                wcc[l, n0:n0 + 64, e, 0, 16 * gl:16 * gl + 16] = inp["ssm_c_re"][l, dd, g].T
                wcc[l, n0:n0 + 64, e, 1, 16 * gl:16 * gl + 16] = inp["ssm_c_im"][l, dd, g].T
                if e >= 32:
                    s0[l, n0:n0 + 64, 0, e - 32] = inp["state_ssm"][b, l, dd, 0, g]
                    s0[l, n0:n0 + 64, 1, e - 32] = inp["state_ssm"][b, l, dd, 1, g]
    d["lam"], d["wbc"], d["wcc"], d["s0h"] = lamh, wbc, wcc, s0
    d["ck"] = np.ascontiguousarray(np.stack([inp["cache_k_win"][b].reshape(2, 256, 128), inp["cache_k_glb"][b].reshape(2, 256, 128)], 1), f)
    d["cv"] = np.ascontiguousarray(np.stack([inp["cache_v_win"][b].reshape(2, 256, 128), inp["cache_v_glb"][b].reshape(2, 256, 128)], 1), f)
    ident, rotm, hmean, mL, mR = _consts()
    z = np.zeros_like(mL)
    d["cmat"] = np.stack([ident, rotm, hmean, mL, mR, z if i == 0 else mL, z if i == 3 else mR]).astype(f)
    d["rope"] = _rope_tables(i)
    return d


def _shared_inputs(inp):
    f = np.float32
    d = {}
    d["w_mod"] = np.ascontiguousarray(inp["w_mod"], f)
    d["b_modT"] = np.ascontiguousarray(inp["b_mod"].reshape(2, 48, 128).transpose(0, 2, 1), f)
    d["w_in"] = np.ascontiguousarray(inp["w_in"], f)
    d["dskipT"] = np.ascontiguousarray(inp["ssm_d"].reshape(2, 4, 128).transpose(0, 2, 1), f)
    d["w_glu"] = np.ascontiguousarray(inp["w_glu"], f)
    d["w_br"] = np.ascontiguousarray(np.stack([inp["w_br_ssm"], inp["w_br_win"], inp["w_br_glb"]], 1), f)
    d["w_out"] = np.ascontiguousarray(inp["w_out"], f)
    d["w_up"] = np.ascontiguousarray(inp["w_up"], f)
    d["w_down"] = np.ascontiguousarray(inp["w_down"], f)
    lnp = np.stack([inp["ln1_g"], inp["ln1_b"], inp["ln2_g"], inp["ln2_b"]], 1)
    d["lnp"] = np.ascontiguousarray(lnp.reshape(2, 4, 8, 128).transpose(0, 3, 1, 2), f)
    qn = np.tile(inp["q_norm_glb"], (1, 2))
    kn = np.tile(inp["k_norm_glb"], (1, 2))
    d["qkn"] = np.ascontiguousarray(np.stack([qn, kn], -1), f)
    d["knrow"] = np.ascontiguousarray(np.broadcast_to(inp["k_norm_glb"][:, None, :], (2, 128, 64)), f)
    d["sinkb"] = np.ascontiguousarray(np.broadcast_to(inp["sink_win"][:, None, :], (2, 128, 8)), f)
    jj = np.arange(512, dtype=f)
    d["jidx"] = np.ascontiguousarray(np.stack([np.broadcast_to(jj, (128, 512)), np.broadcast_to(jj % 256, (128, 512))]), f)
    return d


_CACHE = {}


def _run(inputs, debug=(), nlayers=2, stop_after=None):
    inp = {k: np.asarray(v) for k, v in inputs.items()}
    key = (tuple(debug), nlayers, stop_after)
    if key not in _CACHE:
        _CACHE[key] = build(debug, nlayers, stop_after)
    nc, peak = _CACHE[key]
    shared = _shared_inputs(inp)
    in_maps = [_host_inputs(inp, r, shared) for r in range(8)]
    res = run_bass_kernel_spmd(nc, in_maps, core_ids=list(range(8)))
    return res.results


def kernel(**inputs):
    R = _run(inputs)
    f = np.float32
    y_prompt = np.zeros((16, 256, 1024), f)
    y_sample = np.zeros((2, 4096, 1024), f)
    st = np.zeros((16, 2, 2, 2, 32, 64), f)
    kw = np.zeros((16, 2, 256, 2, 64), f)
    vw = np.zeros_like(kw)
    kg = np.zeros_like(kw)
    vg = np.zeros_like(kw)
    for r in range(8):
        o = R[r]
        b, i = r // 4, r % 4
        yo = np.asarray(o["yo"])
        y_prompt[2 * r:2 * r + 2] = yo[:512].reshape(2, 256, 1024)
        y_sample[b, 1024 * i:1024 * (i + 1)] = yo[512:]
        co = np.asarray(o["co"])
        for s in range(2):
            kw[2 * r + s] = co[s, :, 0].reshape(2, 256, 2, 64)
            vw[2 * r + s] = co[s, :, 1].reshape(2, 256, 2, 64)
            kg[2 * r + s] = co[s, :, 2].reshape(2, 256, 2, 64)
            vg[2 * r + s] = co[s, :, 3].reshape(2, 256, 2, 64)
        st[2 * r:2 * r + 2] = np.asarray(o["so"])
    return (y_prompt, y_sample, st, kw, vw, kg, vg)
```

```python
import numpy as np
from contextlib import ExitStack
import concourse.bass as bass
import concourse.mybir as mybir
from concourse.bass_utils import run_bass_kernel_spmd

F32 = mybir.dt.float32
BF16 = mybir.dt.bfloat16
I32 = mybir.dt.int32
ALU = mybir.AluOpType
AF = mybir.ActivationFunctionType
GROUPS = [[0, 1, 2, 3], [4, 5, 6, 7]]
NDMA = 40
EPOCH = 6000
ALPHA = float((2.0 * 2) ** 0.25)
TWO_PI = float(2 * np.pi)
PI_LO = 3.1415925
import os
DBGF = set(os.environ.get('KDBG', '').split(','))
SKIP_SELF = False
SKIP_OLD = False


class Buf:
    __slots__ = ("w", "r")

    def __init__(self):
        self.w = None
        self.r = {}


class Prog:
    ENG = ("pe", "act", "dve", "pool", "sp")

    def __init__(self, nc, es):
        self.nc, self.es = nc, es
        self.q = {e: [] for e in self.ENG}
        self.n = {e: 0 for e in self.ENG}
        self.ep = {e: None for e in self.ENG}
        self.last = {e: None for e in self.ENG}
        self.waited = {e: {} for e in self.ENG}
        self.bufs = {}
        self.nsem = 0
        self.dma_sems = [es.enter_context(nc.semaphore(f"dq{i}")) for i in range(NDMA)]
        self.dma_val = [0] * NDMA
        self.dma_tok = [None] * NDMA
        self.dma_rr = 0
        self.dma_rrq = {"sp": 0, "pool": 0, "act": 0}
        self.out_toks = []

    def b(self, *key):
        v = self.bufs.get(key)
        if v is None:
            v = self.bufs[key] = Buf()
        return v

    def _sem(self, e):
        s = self.ep[e]
        if s is None or s[1] >= EPOCH:
            h = self.es.enter_context(self.nc.semaphore(f"pg{e}{self.nsem}"))
            self.nsem += 1
            s = self.ep[e] = [h, 0]
        return s

    def _need(self, eng, tok, cur_big=False):
        if tok is None:
            return
        h, val, src, idx = tok[:4]
        if src == eng:
            if eng in ("pe", "sp"):
                return
            if SKIP_OLD and self.n[eng] - idx > 3:
                return
            if SKIP_SELF and cur_big and len(tok) > 4 and tok[4]:
                return
        w = self.waited[eng]
        if w.get(h, 0) >= val:
            return
        w[h] = val
        self.q[eng].append(lambda e, h=h, val=val: e.wait_ge(h, val))

    def _deps(self, eng, R, W, cur_big=False):
        for b in R:
            self._need(eng, b.w, cur_big)
        for b in W:
            self._need(eng, b.w, cur_big)
            for t in b.r.values():
                self._need(eng, t, cur_big)

    def _upd(self, tok, R, W):
        for b in R:
            b.r[tok[2]] = tok
        for b in W:
            b.w = tok
            b.r = {}

    def op(self, eng, meth, R=(), W=(), **kw):
        big = False
        if not isinstance(meth, list):
            o_ = kw.get("out", kw.get("ap"))
            if o_ is not None:
                fs = 1
                for d_ in o_.shape[1:]:
                    fs *= d_
                big = fs >= 256
        self._deps(eng, R, W, big)
        s = self._sem(eng)
        s[1] += 1
        h, val = s[0], s[1]
        idx = self.n[eng]
        self.n[eng] += 1
        calls = meth if isinstance(meth, list) else [(meth, kw)]

        def thunk(e, calls=calls, h=h):
            for m, k in calls[:-1]:
                getattr(e, m)(**k)
            m, k = calls[-1]
            getattr(e, m)(**k).then_inc(h, 1)
        self.q[eng].append(thunk)
        tok = (h, val, eng, idx, big)
        self.last[eng] = tok
        self._upd(tok, R, W)
        return tok

    def mm(self, calls, R=(), W=()):
        return self.op("pe", [("matmul", c) for c in calls], R=R, W=W)

    def dma_slot(self, qeng):
        half = NDMA // 2
        i = self.dma_rrq[qeng]
        self.dma_rrq[qeng] = (i + 1) % half
        return i + (half if qeng == "pool" else 0)

    def dma(self, qeng, out, in_, R=(), W=(), is_out=False):
        self._deps(qeng, R, W)
        k = self.dma_slot(qeng)
        self._need(qeng, self.dma_tok[k])
        self.dma_val[k] += 16
        val, h = self.dma_val[k], self.dma_sems[k]
        self.q[qeng].append(lambda e, out=out, in_=in_, h=h: e.dma_start(out=out, in_=in_).then_inc(h, 16))
        self.n[qeng] += 1
        tok = (h, val, "dma%d" % k, 0)
        self.dma_tok[k] = tok
        self._upd(tok, R, W)
        if is_out:
            self.out_toks.append(tok)
        return tok

    def coll(self, in_ap, out_ap, R=(), W=()):
        self._deps("pool", R, W)
        h = self.es.enter_context(self.nc.semaphore(f"cc{self.nsem}"))
        self.nsem += 1
        self.q["pool"].append(lambda e, h=h: e.collective_compute(
            "AllGather", ALU.bypass, replica_groups=GROUPS, ins=[in_ap], outs=[out_ap]).then_inc(h, 1))
        self.n["pool"] += 1
        tok = (h, 1, "cc%d" % self.nsem, 0)
        self._upd(tok, R, W)
        return tok

    def barrier(self):
        toks = [self.last[e] for e in self.ENG] + list(self.dma_tok)
        for e in self.ENG:
            for t in toks:
                if t is not None and t[2] != e:
                    self._need(e, t)

    def finish(self, block):
        for t in self.out_toks + [self.last[e] for e in self.ENG] + list(self.dma_tok):
            self._need("sp", t)
        reg = {"pe": block.tensor, "act": block.scalar, "dve": block.vector, "pool": block.gpsimd, "sp": block.sync}
        for e in self.ENG:
            lst = self.q[e]

            def run(eng, lst=lst):
                for f in lst:
                    f(eng)
            reg[e](run)


class Arena:
    def __init__(self, nc):
        self.nc = nc
        self.p = 16512
        self.hi = 229344
        self.k = 0
        self.peak = 0

    def alloc(self, name, shape, dt):
        n = 1
        for s in shape[1:]:
            n *= s
        sz = {F32: 4, BF16: 2, I32: 4}[dt]
        off = (self.p + 63) // 64 * 64
        self.last_off = off
        self.p = off + n * sz
        self.peak = max(self.peak, self.p)
        assert self.p <= self.hi, f"SBUF overflow at {name}: {self.p}"
        self.k += 1
        return self.nc.alloc_sbuf_tensor_at(f"{name}{self.k}", list(shape), dt, offset=off)

    def mark(self):
        return self.p

    def release(self, m):
        self.p = m


def build(debug=(), nlayers=2, stop_after=None):
    nc = bass.Bass("TRN2", target_bir_lowering=False)
    es = ExitStack()
    with es:
        def din(name, shape, dt=F32):
            return nc.dram_tensor(name, list(shape), dt, kind="ExternalInput").ap()

        def dout(name, shape, dt=F32):
            return nc.dram_tensor(name, list(shape), dt, kind="ExternalOutput").ap()

        def dint(name, shape, dt):
            return nc.dram_tensor(name, list(shape), dt, kind="Internal").ap()

        xin = din("xin", [1536, 1024])
        condT = din("condT", [128, 8, 2])
        w_mod = din("w_mod", [2, 1024, 6144])
        b_modT = din("b_modT", [2, 128, 48])
        w_in = din("w_in", [2, 1024, 5120])
        lam = din("lam", [2, 128, 3, 40])
        wbc = din("wbc", [2, 4, 32, 10, 2, 128])
        wcc = din("wcc", [2, 128, 40, 2, 32])
        dskipT = din("dskipT", [2, 128, 4])
        s0h = din("s0h", [2, 128, 2, 8])
        w_glu = din("w_glu", [2, 512, 512])
        w_br = din("w_br", [2, 3, 512, 1024])
        w_out = din("w_out", [2, 1024, 1024])
        w_up = din("w_up", [2, 1024, 4096])
        w_down = din("w_down", [2, 4096, 1024])
        lnp = din("lnp", [2, 128, 4, 8])
        qkn = din("qkn", [2, 128, 2])
        knrow = din("knrow", [2, 128, 64])
        sinkb = din("sinkb", [2, 128, 8])
        ck = din("ck", [2, 2, 256, 128])
        cv = din("cv", [2, 2, 256, 128])
        cmat = din("cmat", [7, 128, 128])
        jidx = din("jidx", [2, 128, 512])
        rope = din("rope", [2, 128, 1024])
        yo = dout("yo", [1536, 1024])
        co = dout("co", [2, 2, 4, 256, 128])
        so = dout("so", [2, 2, 2, 2, 32, 64])
        xu = dint("xu", [512, 1024], BF16)
        xu_g = dint("xu_g", [2048, 1024], BF16)
        xk = dint("xk", [256, 1024], BF16)
        xk_g = dint("xk_g", [1024, 1024], BF16)
        xv = dint("xv", [1024, 256], BF16)
        xv_g = dint("xv_g", [4096, 256], BF16)
        xy = dint("xy", [4, 128, 1024], F32)
        xy_g = dint("xy_g", [2048, 1024], F32)
        yp_d = dint("yp_d", [128, 4, 512], F32)
        dbg_out = {}

        P = Prog(nc, es)
        A = Arena(nc)
        ps = [es.enter_context(nc.psum_tensor(f"ps{i}", [128, 512], F32)) for i in range(8)]
        psb = [P.b("ps", i) for i in range(8)]
        block = es.enter_context(nc.Block())
        pid = None

        def dump(name, ap, shape, bufs, dt=F32):
            if name not in debug:
                return
            d = dout("dbg_" + name, shape, dt)
            P.dma("sp", d, ap, R=bufs, is_out=True)

        xT = A.alloc("xT", [128, 8, 1536], F32)
        cm = A.alloc("cm", [128, 7, 128], BF16)
        identF = A.alloc("identF", [128, 128], F32)
        lnmean = A.alloc("lnmean", [128, 128], BF16)
        onesF = A.alloc("onesF", [128, 64], F32)
        cst = A.alloc("cst", [128, 8], F32)
        ropeT = A.alloc("ropeT", [128, 2, 1024], F32)
        jT = A.alloc("jT", [128, 2, 512], F32)
        modvs = [A.alloc("modv", [128, 48, 2], F32) for _ in range(2)]
        lnpT = A.alloc("lnpT", [128, 4, 8], F32)
        qknT = A.alloc("qknT", [128, 2], F32)
        esink = A.alloc("esink", [128, 8], F32)
        dskT = A.alloc("dskT", [128, 4], F32)
        Bc = P.b("const")
        Blay = P.b("laycst")

        def xb(ft, c):
            return P.b("xT", ft, c)

        P.dma("pool", cm[:], cmat.rearrange("k p c -> p k c"), W=[Bc])
        P.dma("sp", identF[:], cmat[0], W=[Bc])
        P.dma("sp", ropeT[:], rope.rearrange("k p c -> p k c"), W=[Bc])
        P.dma("sp", jT[:], jidx.rearrange("k p c -> p k c"), W=[Bc])
        P.op("pool", "memset", W=[Bc], ap=lnmean[:], constant=1.0 / 1024)
        P.op("pool", "memset", W=[Bc], ap=onesF[:], constant=1.0)
        P.op("pool", "memset", W=[Bc], ap=cst[:, 0:1], constant=1e-5)
        P.op("pool", "memset", W=[Bc], ap=cst[:, 1:2], constant=1e-6)
        P.op("pool", "memset", W=[Bc], ap=cst[:, 2:3], constant=float(np.pi / 2))
        P.op("pool", "memset", W=[Bc], ap=cst[:, 3:4], constant=0.0)
        IDB, ROT, HMEAN, ML, MR, MLF, MRL = range(7)

        m0 = A.mark()
        xtok = [A.alloc("xtok", [128, 1024], F32) for _ in range(2)]
        for tt in range(12):
            xt = xtok[tt % 2]
            bx = P.b("xtok", tt % 2)
            P.dma("sp", xt[:], xin[128 * tt:128 * tt + 128, :], W=[bx])
            for hh in range(2):
                bank = (2 * tt + hh) % 2
                P.mm([dict(out=ps[bank][:, 128 * q:128 * q + 128],
                           lhsT=xt[:, 128 * (4 * hh + q):128 * (4 * hh + q) + 128],
                           rhs=identF[:], start=True, stop=True) for q in range(4)],
                     R=[bx, Bc], W=[psb[bank]])
                c = tt // 4
                o_ap = xT[:, 4 * hh:4 * hh + 4, 128 * tt:128 * tt + 128]
                i_ap = ps[bank][:].rearrange("p (q t) -> p q t", q=4)
                Wx = [xb(ft, c) for ft in range(4 * hh, 4 * hh + 4)]
                if hh:
                    P.op("act", "activation", R=[psb[bank]], W=Wx, out=o_ap, in_=i_ap, func=AF.Copy)
                else:
                    P.op("dve", "tensor_copy", R=[psb[bank]], W=Wx, out=o_ap, in_=i_ap)
        P.barrier()
        A.release(m0)
        dump("xT0", xT[:, 0, :], [128, 1536], [xb(0, c) for c in range(3)])

        _bank = [0]

        def nb():
            _bank[0] = (_bank[0] + 1) % 4
            return _bank[0]

        MUL, ADD, SUB = ALU.mult, ALU.add, ALU.subtract
        (LRE, LIM, LST, STEP, LR, TH, RR, T1, T2, T3, THR, SIN, COS, ARE, AIM, AM1, NRE, NIM, DEN, CFRE, CFIM,
         C512, S512, T4) = range(24)
        kvp = {}

        _pidc = {}

        def pid4(e):
            if "v" not in _pidc:
                _pidc["v"] = e.partition_id()
            return _pidc["v"] % 4

        def dyn_dma(qeng, fn, R=(), W=(), is_out=False):
            P._deps(qeng, R, W)
            k = P.dma_slot(qeng)
            P._need(qeng, P.dma_tok[k])
            P.dma_val[k] += 16
            val, h = P.dma_val[k], P.dma_sems[k]

            def thunk(e, fn=fn, h=h):
                o_, i_ = fn(e)
                e.dma_start(out=o_, in_=i_).then_inc(h, 16)
            P.q[qeng].append(thunk)
            P.n[qeng] += 1
            tok = (h, val, "dma%d" % k, 0)
            P.dma_tok[k] = tok
            P._upd(tok, R, W)
            return tok

        def mod_vectors_gen(l_):
            mv = modvs[l_ % 2]
            bmv = P.b("modv", l_ % 2)
            mM = A.mark()
            sc = A.alloc("scond", [128, 8, 2], F32)
            bm = A.alloc("bmod", [128, 48], F32)
            b_sc, b_bm = P.b("sc"), P.b("bm")
            P.dma("sp", sc[:], condT, W=[b_sc])
            P.op("act", "activation", R=[b_sc], W=[b_sc], out=sc[:], in_=sc[:], func=AF.Silu)
            P.dma("sp", bm[:], b_modT[l_], W=[b_bm])
            wm = [A.alloc("wm", [128, 8, 128], F32) for _ in range(3)]
            wmv = w_mod[l_].rearrange("(kt p) c -> p kt c", p=128)

            def load(T):
                P.dma("sp", wm[T % 3][:], wmv[:, :, 128 * T:128 * T + 128], W=[P.b("wm", T % 3)])
            load(0)
            load(1)
            for T in range(48):
                if T + 2 < 48:
                    load(T + 2)
                w = wm[T % 3]
                P.mm([dict(out=ps[3][:, 2 * T:2 * T + 2], lhsT=w[:, kt, :], rhs=sc[:, kt, :], start=(kt == 0), stop=(kt == 7))
                      for kt in range(8)], R=[P.b("wm", T % 3), b_sc], W=[psb[3]])
                yield T
            pm3 = ps[3][:, 0:96].rearrange("p (t k) -> p t k", k=2)
            for k in range(2):
                P.op("dve", "tensor_tensor", R=[psb[3], b_bm], W=[bmv], out=mv[:, :, k], in0=pm3[:, :, k], in1=bm[:, :], op=ADD)
            for lo in (8, 32):
                P.op("dve", "tensor_scalar", R=[bmv], W=[bmv], out=mv[:, lo:lo + 8, :], in0=mv[:, lo:lo + 8, :],
                     scalar1=1.0, scalar2=None, op0=ADD)
            if l_ == 0:
                P.barrier()
            A.release(mM)

        def mod_vectors(l_):
            for _ in mod_vectors_gen(l_):
                pass

        def layer(l):
            wv = w_in[l].rearrange("(kt p) c -> p kt c", p=128)
            k_of = (0, 1, 1)
            P.dma("sp", lnpT[:], lnp[l], W=[Blay])
            P.dma("sp", qknT[:], qkn[l], W=[Blay])
            P.dma("sp", dskT[:], dskipT[l], W=[Blay])
            mLay = A.mark()
            kpT = A.alloc("kpT", [128, 2, 512], BF16)
            vp = A.alloc("vp", [128, 4, 2, 2, 65], BF16)
            Bkp, Bvp = P.b("kpT"), P.b("vp")
            P.op("pool", "memset", W=[Bvp], ap=vp[:], constant=1.0)

            modv = modvs[l % 2]
            Bmod = P.b("modv", l % 2)
            if l == 0:
                mod_vectors(0)
            sk = A.alloc("sk", [128, 8], F32)
            b_sk = P.b("sk")
            P.dma("sp", sk[:], sinkb[l], W=[b_sk])
            P.op("act", "activation", R=[b_sk], W=[Blay], out=esink[:], in_=sk[:], func=AF.Exp)

            mAB = A.mark()
            ubf = A.alloc("ubf", [128, 4, 512], BF16)
            CS = A.alloc("CS", [128, 24, 40], F32)
            CI = A.alloc("CI", [128, 40], I32)
            S0 = A.alloc("S0", [128, 2, 8], F32)
            car = A.alloc("car", [128, 2, 8], F32)
            tt8 = A.alloc("tt8", [128, 8], F32)
            WB = A.alloc("WB", [128, 40, 2, 128], BF16)
            WC = A.alloc("WC", [128, 40, 2, 32], BF16)
            b_cs, b_S0, b_car, b_tt8, b_WB, b_WC = (P.b(n) for n in ("CS", "S0", "car", "tt8", "WB", "WC"))

            def cs(i_, lo=0, hi=40):
                return CS[:, i_, lo:hi]

            def ctt(o_, a_, b_, op_):
                P.op("dve", "tensor_tensor", R=[b_cs], W=[b_cs], out=cs(o_), in0=cs(a_), in1=cs(b_), op=op_)

            P.dma("sp", CS[:, 0:3, :], lam[l], W=[b_cs])
            P.dma("sp", S0[:], s0h[l], W=[b_S0])
            P.op("pool", "memset", W=[b_WB], ap=WB[:], constant=0.0)
            WB5 = WB[:].rearrange("p (q a) r n -> p q a r n", a=4)
            for a in range(4):
                P.dma("pool", WB5[32 * a:32 * a + 32, :, a, :, :], wbc[l, a], W=[b_WB])
            P.dma("pool", WC[:], wcc[l], W=[b_WC])
            P.op("dve", "tensor_scalar", R=[b_WC], W=[b_WC], out=WC[:, :, 1, :], in0=WC[:, :, 1, :], scalar1=-1.0, scalar2=None, op0=MUL)
            WCn = A.alloc("WCn", [128, 40, 32], BF16)
            P.op("dve", "tensor_scalar", R=[b_WC], W=[b_WC], out=WCn[:], in0=WC[:, :, 0, :], scalar1=-1.0, scalar2=None, op0=MUL)
            P.op("act", "activation", R=[b_cs], W=[b_cs], out=cs(STEP), in_=cs(LST), func=AF.Exp)
            ctt(LR, LRE, STEP, MUL)
            ctt(TH, LIM, STEP, MUL)
            P.op("act", "activation", R=[b_cs], W=[b_cs], out=cs(RR), in_=cs(LR), func=AF.Exp)
            P.op("dve", "tensor_scalar", R=[b_cs], W=[b_cs], out=cs(T1), in0=cs(TH), scalar1=1.0 / TWO_PI, scalar2=None, op0=MUL)
            P.op("dve", "tensor_copy", R=[b_cs], W=[b_cs], out=CI[:], in_=cs(T1))
            P.op("dve", "tensor_copy", R=[b_cs], W=[b_cs], out=cs(T1), in_=CI[:])
            P.op("dve", "scalar_tensor_tensor", R=[b_cs], W=[b_cs], out=cs(THR), in0=cs(T1), scalar=-TWO_PI, in1=cs(TH), op0=MUL, op1=ADD)
            P.op("dve", "tensor_scalar", R=[b_cs], W=[b_cs], out=cs(THR), in0=cs(THR), scalar1=PI_LO, scalar2=-PI_LO, op0=ALU.min, op1=ALU.max)
            P.op("act", "activation", R=[b_cs], W=[b_cs], out=cs(SIN), in_=cs(THR), func=AF.Sin)
            P.op("act", "activation", R=[b_cs], W=[b_cs], out=cs(T2), in_=cs(THR), func=AF.Abs)
            P.op("act", "activation", R=[b_cs, Bc], W=[b_cs], out=cs(COS), in_=cs(T2), func=AF.Sin, scale=-1.0, bias=cst[:, 2:3])
            ctt(ARE, RR, COS, MUL)
            ctt(AIM, RR, SIN, MUL)
            P.op("dve", "tensor_scalar", R=[b_cs], W=[b_cs], out=cs(AM1), in0=cs(ARE), scalar1=-1.0, scalar2=None, op0=ADD)
            ctt(T2, AM1, LRE, MUL)
            ctt(T3, AIM, LIM, MUL)
            ctt(NRE, T2, T3, ADD)
            ctt(T2, AIM, LRE, MUL)
            ctt(T3, AM1, LIM, MUL)
            ctt(NIM, T2, T3, SUB)
            ctt(T2, LRE, LRE, MUL)
            ctt(T3, LIM, LIM, MUL)
            ctt(DEN, T2, T3, ADD)
            P.op("dve", "reciprocal", R=[b_cs], W=[b_cs], out=cs(DEN), in_=cs(DEN))
            ctt(CFRE, NRE, DEN, MUL)
            ctt(CFIM, NIM, DEN, MUL)
            P.op("dve", "tensor_copy", R=[b_cs], W=[b_cs], out=cs(C512), in_=cs(COS))
            P.op("dve", "tensor_copy", R=[b_cs], W=[b_cs], out=cs(S512), in_=cs(SIN))
            for _ in range(9):
                ctt(T2, C512, C512, MUL)
                ctt(T3, S512, S512, MUL)
                ctt(T4, C512, S512, MUL)
                ctt(C512, T2, T3, SUB)
                P.op("dve", "tensor_scalar", R=[b_cs], W=[b_cs], out=cs(S512), in0=cs(T4), scalar1=2.0, scalar2=None, op0=MUL)
            P.op("dve", "tensor_tensor", R=[b_cs, b_S0], W=[b_tt8], out=tt8[:], in0=S0[:, 1, :], in1=cs(SIN, 32, 40), op=MUL)
            P.op("dve", "tensor_tensor", R=[b_cs, b_S0], W=[b_car], out=car[:, 0, :], in0=S0[:, 0, :], in1=cs(COS, 32, 40), op=MUL)
            P.op("dve", "tensor_tensor", R=[b_car, b_tt8], W=[b_car], out=car[:, 0, :], in0=car[:, 0, :], in1=tt8[:], op=SUB)
            P.op("dve", "tensor_tensor", R=[b_cs, b_S0], W=[b_tt8], out=tt8[:], in0=S0[:, 1, :], in1=cs(COS, 32, 40), op=MUL)
            P.op("dve", "tensor_tensor", R=[b_cs, b_S0], W=[b_car], out=car[:, 1, :], in0=S0[:, 0, :], in1=cs(SIN, 32, 40), op=MUL)
            P.op("dve", "tensor_tensor", R=[b_car, b_tt8], W=[b_car], out=car[:, 1, :], in0=car[:, 1, :], in1=tt8[:], op=ADD)
            dump("CS%d" % l, CS[:].rearrange("p a e -> p (a e)"), [128, 960], [b_cs])

            mA = A.mark()
            ubs = A.alloc("ubs", [128, 4, 1024], BF16)
            wA = A.alloc("wA", [128, 8, 1024], BF16)
            hT = A.alloc("hT", [128, 8, 512], BF16)
            ksT = A.alloc("ksT", [128, 2, 1024], BF16)
            vs = A.alloc("vs", [128, 8, 256], BF16)
            x32 = A.alloc("x32", [128, 512], F32)
            sqb = A.alloc("sqb", [128, 512], BF16)
            rsf = A.alloc("rsf", [128, 512], F32)
            xbb = A.alloc("xbb", [128, 512], BF16)
            t1f = A.alloc("t1f", [128, 512], F32)
            t2f = A.alloc("t2f", [128, 512], F32)
            ctok = [A.alloc("ctok", [128, 4, 128], F32) for _ in range(2)]
            sq128 = A.alloc("sq128", [128, 128], F32)
            ssq = A.alloc("ssq", [128, 2], F32)
            knr = A.alloc("knr", [128, 64], F32)
            b_wA, b_hT, b_u, b_ks, b_vs = P.b("wA"), P.b("hT"), P.b("ubf"), P.b("ksT"), P.b("vs")
            b_x32, b_sqb, b_rsf, b_xbb, b_t1, b_t2 = (P.b(n) for n in ("x32", "sqb", "rsf", "xbb", "t1f", "t2f"))
            b_sq128, b_ssq, b_knr = P.b("sq128"), P.b("ssq"), P.b("knr")
            b_xu, b_xk, b_xv, b_xug, b_xkg, b_xvg = (P.b(n) for n in ("xu", "xk", "xv", "xug", "xkg", "xvg"))
            P.dma("sp", knr[:], knrow[l], W=[b_knr])
            for (d0, s0_, n_) in ((0, 0, 512), (512, 1024, 128), (640, 1792, 128), (768, 1152, 128), (896, 1920, 128)):
                P.dma("pool", wA[:, :, d0:d0 + n_], wv[:, :, s0_:s0_ + n_], W=[b_wA])

            def rmsnorm_fm(bank, N, gcol, out_ap, Wb):
                P.op("act", "activation", R=[psb[bank]], W=[b_x32], out=x32[:, :N], in_=ps[bank][:, :N], func=AF.Copy)
                P.op("act", "activation", R=[b_x32], W=[b_sqb], out=sqb[:, :N], in_=x32[:, :N], func=AF.Square)
                P.mm([dict(out=ps[7][:, :N], lhsT=cm[:, HMEAN, :], rhs=sqb[:, :N], start=True, stop=True)],
                     R=[b_sqb, Bc], W=[psb[7]])
                P.op("act", "activation", R=[psb[7], Bc], W=[b_rsf], out=rsf[:, :N], in_=ps[7][:, :N], func=AF.Ln,
                     bias=cst[:, 1:2], scale=1.0)
                P.op("act", "activation", R=[b_rsf], W=[b_rsf], out=rsf[:, :N], in_=rsf[:, :N], func=AF.Exp, scale=-0.5)
                P.op("dve", "scalar_tensor_tensor", R=[b_x32, b_rsf, Blay], W=Wb, out=out_ap, in0=x32[:, :N], scalar=gcol,
                     in1=rsf[:, :N], op0=MUL, op1=MUL)

            def rope_fm(N, pos0, out_ap, Wb):
                P.op("act", "activation", R=[b_x32], W=[b_xbb], out=xbb[:, :N], in_=x32[:, :N], func=AF.Copy)
                P.mm([dict(out=ps[7][:, :N], lhsT=cm[:, ROT, :], rhs=xbb[:, :N], start=True, stop=True)],
                     R=[b_xbb, Bc], W=[psb[7]])
                P.op("dve", "tensor_tensor", R=[b_x32, Bc], W=[b_t1], out=t1f[:, :N], in0=x32[:, :N],
                     in1=ropeT[:, 0, pos0:pos0 + N], op=MUL)
                P.op("dve", "tensor_tensor", R=[psb[7], Bc], W=[b_t2], out=t2f[:, :N], in0=ps[7][:, :N],
                     in1=ropeT[:, 1, pos0:pos0 + N], op=MUL)
                P.op("dve", "tensor_tensor", R=[b_t1, b_t2], W=Wb, out=out_ap, in0=t1f[:, :N], in1=t2f[:, :N], op=ADD)
            kvp["rms"], kvp["rope"] = rmsnorm_fm, rope_fm

            for c in range(3):
                k = k_of[c]
                cols = slice(512 * c, 512 * c + 512)
                for ft in range(8):
                    P.op("dve", "tensor_scalar", R=[xb(ft, c), Bmod], W=[b_hT], out=hT[:, ft, :], in0=xT[:, ft, cols],
                         scalar1=modv[:, 8 + ft, k:k + 1], scalar2=modv[:, ft, k:k + 1], op0=MUL, op1=ADD)
                if c == 0:
                    dump("hT%d" % l, hT[:, 0, :], [128, 512], [b_hT], BF16)
                for m in range(4):
                    bank = nb()
                    P.mm([dict(out=ps[bank][:], lhsT=wA[:, kt, 128 * m:128 * m + 128], rhs=hT[:, kt, :],
                               start=(kt == 0), stop=(kt == 7)) for kt in range(8)], R=[b_wA, b_hT], W=[psb[bank]])
                    u_dst = ubf[:, m, :] if c == 0 else ubs[:, m, 512 * (c - 1):512 * c]
                    P.op("act", "activation", R=[psb[bank]], W=[b_u], out=u_dst, in_=ps[bank][:], func=AF.Copy)
                if c > 0:
                    P.dma("sp", xu.rearrange("(m p) t -> p m t", p=128)[:, :, 512 * (c - 1):512 * c], ubs[:, :, 512 * (c - 1):512 * c],
                          R=[b_u], W=[b_xu])
                if c == 2:
                    P.coll(xu, xu_g, R=[b_xu], W=[b_xug])
                for br, off in ((0, 512), (1, 640)):
                    bank = nb()
                    P.mm([dict(out=ps[bank][:], lhsT=wA[:, kt, off:off + 128], rhs=hT[:, kt, :],
                               start=(kt == 0), stop=(kt == 7)) for kt in range(8)], R=[b_wA, b_hT], W=[psb[bank]])
                    if c == 0:
                        if br == 0:
                            P.op("act", "activation", R=[psb[bank]], W=[Bkp], out=kpT[:, 0, :], in_=ps[bank][:], func=AF.Copy)
                        else:
                            rmsnorm_fm(bank, 512, qknT[:, 1:2], kpT[:, 1, :], [Bkp])
                    else:
                        p0 = 512 * (c - 1)
                        if br == 0:
                            P.op("act", "activation", R=[psb[bank]], W=[b_x32], out=x32[:], in_=ps[bank][:], func=AF.Copy)
                        else:
                            rmsnorm_fm(bank, 512, qknT[:, 1:2], x32[:], [b_x32])
                        rope_fm(512, p0, ksT[:, br, p0:p0 + 512], [b_ks])
                ncol, coff = (512, 512) if c == 0 else (256, 768)
                for tt in range(4):
                    bank = nb()
                    P.mm([dict(out=ps[bank][:, 0:ncol], lhsT=hT[:, kt, 128 * tt:128 * tt + 128], rhs=wA[:, kt, coff:1024],
                               start=(kt == 0), stop=(kt == 7)) for kt in range(8)], R=[b_wA, b_hT], W=[psb[bank]])
                    pb = ps[bank]
                    if c == 0:
                        s_, t_ = tt // 2, tt % 2
                        ct = ctok[tt % 2]
                        b_ct = P.b("ctok", tt % 2)
                        P.op("act", "activation", R=[psb[bank]], W=[b_ct], out=ct[:, 0, :], in_=pb[:, 0:128], func=AF.Copy)
                        P.op("act", "activation", R=[psb[bank]], W=[b_ct], out=ct[:, 1, :], in_=pb[:, 256:384], func=AF.Copy)
                        P.op("act", "activation", R=[psb[bank]], W=[b_ct], out=ct[:, 3, :], in_=pb[:, 384:512], func=AF.Copy)
                        P.op("act", "activation", R=[psb[bank]], W=[b_sq128], out=sq128[:], in_=pb[:, 128:256], func=AF.Square)
                        P.op("dve", "reduce_sum", R=[b_sq128], W=[b_ssq], out=ssq[:], in_=sq128[:].rearrange("p (g d) -> p g d", g=2),
                             axis=mybir.AxisListType.X)
                        P.op("act", "activation", R=[b_ssq, Bc], W=[b_ssq], out=ssq[:], in_=ssq[:], func=AF.Sqrt,
                             bias=cst[:, 1:2], scale=1.0 / 64)
                        P.op("dve", "reciprocal", R=[b_ssq], W=[b_ssq], out=ssq[:], in_=ssq[:])
                        for g in range(2):
                            P.op("dve", "scalar_tensor_tensor", R=[psb[bank], b_ssq, b_knr], W=[b_ct],
                                 out=ct[:, 2, 64 * g:64 * g + 64], in0=pb[:, 128 + 64 * g:192 + 64 * g], scalar=ssq[:, g:g + 1],
                                 in1=knr[:], op0=MUL, op1=MUL)
                        P.dma("sp", co[s_, l][:, 128 * t_:128 * t_ + 128, :].rearrange("f t c -> t f c"), ct[:], R=[b_ct], is_out=True)
                        P.op("dve", "tensor_copy", R=[psb[bank]], W=[Bvp], out=vp[:, tt, 0, :, 0:64],
                             in_=pb[:, 256:384].rearrange("p (g d) -> p g d", g=2))
                        P.op("dve", "tensor_copy", R=[psb[bank]], W=[Bvp], out=vp[:, tt, 1, :, 0:64],
                             in_=pb[:, 384:512].rearrange("p (g d) -> p g d", g=2))
                    else:
                        P.op("act", "activation", R=[psb[bank]], W=[b_vs], out=vs[:, 4 * (c - 1) + tt, :], in_=pb[:, 0:256], func=AF.Copy)
                if c > 0:
                    p0 = 512 * (c - 1)
                    P.dma("sp", xk.rearrange("(b p) t -> p b t", p=128)[:, :, p0:p0 + 512], ksT[:, :, p0:p0 + 512], R=[b_ks], W=[b_xk])
                    P.dma("sp", xv.rearrange("(t p) c -> p t c", p=128)[:, 4 * (c - 1):4 * c, :], vs[:, 4 * (c - 1):4 * c, :], R=[b_vs], W=[b_xv])
            dump("ubf%d" % l, ubf[:, 0, :], [128, 512], [b_u], BF16)
            dump("ubs%d" % l, ubs[:, 0, :], [128, 1024], [b_u], BF16)
            dump("kpT%d" % l, kpT[:].rearrange("p a t -> p (a t)"), [128, 1024], [Bkp], BF16)
            dump("ksT%d" % l, ksT[:].rearrange("p a t -> p (a t)"), [128, 2048], [b_ks], BF16)
            late_colls = [lambda: P.coll(xk, xk_g, R=[b_xk], W=[b_xkg]), lambda: P.coll(xv, xv_g, R=[b_xv], W=[b_xvg])]
            if stop_after == "A":
                P.barrier()
                A.release(mLay)
                return
            P.barrier()
            A.release(mA)

            tabE = [A.alloc("tabE", [128, 2, 512], F32) for _ in range(4)]
            tabD = [A.alloc("tabD", [128, 2, 512], F32) for _ in range(4)]
            ang = A.alloc("ang", [128, 512], F32)
            g1 = A.alloc("g1", [128, 512], F32)
            g2 = A.alloc("g2", [128, 512], F32)
            kI = nc.alloc_sbuf_tensor_at("kIalias%d" % l, [128, 512], I32, offset=A.last_off)
            q1 = A.alloc("q1", [128, 512], BF16)
            q2 = A.alloc("q2", [128, 512], BF16)
            mre = A.alloc("mre", [128, 512], F32)
            mim = A.alloc("mim", [128, 512], F32)
            zre2 = [A.alloc("zre", [128, 512], F32) for _ in range(2)]
            zim2 = [A.alloc("zim", [128, 512], F32) for _ in range(2)]
            b_zre2 = [P.b("zre", i_) for i_ in range(2)]
            b_zim2 = [P.b("zim", i_) for i_ in range(2)]
            b_q1, b_q2 = P.b("q1"), P.b("q2")
            jobn = [0]
            p1 = A.alloc("p1", [128, 512], F32)
            p2 = A.alloc("p2", [128, 512], F32)
            sre = A.alloc("sre", [128, 512], BF16)
            nsi = A.alloc("nsi", [128, 512], BF16)
            useq = A.alloc("useq", [128, 4096], BF16)
            accp = A.alloc("accp", [128, 4, 512], F32)
            accs = A.alloc("accs", [128, 4096], F32)
            fin = A.alloc("fin", [128, 2, 2, 2, 16], F32)
            finT = A.alloc("finT", [128, 128], F32)
            tt2 = A.alloc("tt2", [128, 2], F32)
            (b_ang, b_g1, b_g2, b_mre, b_mim, b_zreX, b_zimX, b_p1, b_p2, b_sre, b_nsi, b_urp, b_useq, b_urs, b_accp,
             b_accs, b_fin, b_tt2, b_xy, b_xyg, b_ypd) = (P.b(n) for n in (
                 "ang", "g1", "g2", "mre", "mim", "zre", "zim", "p1", "p2", "sre", "nsi", "urp", "useq", "urs", "accp",
                 "accs", "fin", "tt2", "xy", "xyg", "ypd"))
            del b_urp, b_urs
            b_tE = [P.b("tabE", a) for a in range(4)]
            b_tD = [P.b("tabD", a) for a in range(4)]
            P.op("pool", "memset", W=[b_accp], ap=accp[:], constant=0.0)
            P.op("pool", "memset", W=[b_accs], ap=accs[:], constant=0.0)
            for rr_ in range(4):
                dyn_dma("sp", lambda e, rr_=rr_, useq=useq: (useq[:, 1024 * rr_:1024 * rr_ + 1024],
                                                  xu_g[bass.ds(pid4(e) * 128 + 512 * rr_, 128), :]),
                        R=[b_xug], W=[b_useq])
            if stop_after == "B1":
                P.barrier()
                A.release(mAB)
                A.release(mLay)
                return

            def gen_table(a, e_, jsel, part=0):
                E, D = tabE[a], tabD[a]
                W_ = 256 if jsel else 512
                th_ = CS[:, THR, e_:e_ + 1]
                if part in (0, 1):
                    P.op("dve", "tensor_scalar", R=[b_cs, Bc], W=[b_ang], out=ang[:, :W_], in0=jT[:, jsel, :W_], scalar1=th_, scalar2=None, op0=MUL)
                    P.op("dve", "tensor_scalar", R=[b_ang], W=[b_g1], out=g1[:, :W_], in0=ang[:, :W_], scalar1=1.0 / TWO_PI, scalar2=None, op0=MUL)
                    P.op("dve", "tensor_copy", R=[b_g1], W=[b_g2], out=kI[:, :W_], in_=g1[:, :W_])
                    P.op("dve", "tensor_copy", R=[b_g2], W=[b_g1], out=g1[:, :W_], in_=kI[:, :W_])
                    P.op("dve", "scalar_tensor_tensor", R=[b_g1, b_ang], W=[b_ang], out=ang[:, :W_], in0=g1[:, :W_], scalar=-TWO_PI, in1=ang[:, :W_],
                         op0=MUL, op1=ADD)
                    P.op("dve", "tensor_scalar", R=[b_ang], W=[b_ang], out=ang[:, :W_], in0=ang[:, :W_], scalar1=PI_LO, scalar2=-PI_LO, op0=ALU.min, op1=ALU.max)
                    P.op("act", "activation", R=[b_ang], W=[b_tD[a]], out=D[:, 1, :W_], in_=ang[:, :W_], func=AF.Sin)
                    P.op("act", "activation", R=[b_ang], W=[b_g1], out=g1[:, :W_], in_=ang[:, :W_], func=AF.Abs)
                    P.op("act", "activation", R=[b_g1, Bc], W=[b_tD[a]], out=D[:, 0, :W_], in_=g1[:, :W_], func=AF.Sin, scale=-1.0, bias=cst[:, 2:3])
                    P.op("act", "activation", R=[b_tD[a], b_cs], W=[b_g2], out=g2[:, :W_], in_=D[:, 1, :W_], func=AF.Identity,
                         scale=CS[:, CFIM, e_:e_ + 1])
                    P.op("act", "activation", R=[b_tD[a], b_cs], W=[b_g1], out=g1[:, :W_], in_=D[:, 1, :W_], func=AF.Identity,
                         scale=CS[:, CFRE, e_:e_ + 1])
                    if jsel:
                        P.op("act", "activation", R=[b_tD[a]], W=[b_tD[a]], out=D[:, :, 256:512], in_=D[:, :, 0:256], func=AF.Copy)
                if part == 1:
                    return
                cre_, cim_ = CS[:, CFRE, e_:e_ + 1], CS[:, CFIM, e_:e_ + 1]
                P.op("dve", "scalar_tensor_tensor", R=[b_tD[a], b_cs, b_g2], W=[b_tE[a]], out=E[:, 0, :W_], in0=D[:, 0, :W_], scalar=cre_,
                     in1=g2[:, :W_], op0=MUL, op1=ADD)
                P.op("dve", "scalar_tensor_tensor", R=[b_tD[a], b_cs, b_g1], W=[b_tE[a]], out=E[:, 1, :W_], in0=D[:, 0, :W_], scalar=cim_,
                     in1=g1[:, :W_], op0=MUL, op1=SUB)
                if jsel:
                    P.op("act", "activation", R=[b_tE[a]], W=[b_tE[a]], out=E[:, :, 256:512], in_=E[:, :, 0:256], func=AF.Copy)

            def pair_job(a, e_, usrc, Ru, segs, own_j, bC):
                E, D = tabE[a], tabD[a]
                zi = jobn[0] % 2
                jobn[0] += 1
                zre, zim, b_zre, b_zim = zre2[zi], zim2[zi], b_zre2[zi], b_zim2[zi]
                bA, bB = (4, 5) if zi == 0 else (0, 1)
                for bnk, ri_ in ((bA, 0), (bB, 1)):
                    P.mm([dict(out=ps[bnk][:, c0_:c0_ + src_.shape[1]], lhsT=WB[:, e_, ri_, :], rhs=src_, start=True, stop=True)
                          for (c0_, src_) in usrc], R=[b_WB] + Ru, W=[psb[bnk]])
                RT = [b_tE[a]]
                P.op("dve", "tensor_tensor", R=[psb[bA]] + RT, W=[b_p1], out=p1[:], in0=ps[bA][:], in1=E[:, 0, :], op=MUL)
                P.op("dve", "tensor_tensor", R=[psb[bB]] + RT, W=[b_p2], out=p2[:], in0=ps[bB][:], in1=E[:, 1, :], op=MUL)
                P.op("dve", "tensor_tensor", R=[psb[bB]] + RT, W=[b_mim], out=mim[:], in0=ps[bB][:], in1=E[:, 0, :], op=MUL)
                P.op("dve", "tensor_tensor", R=[b_p1, b_p2], W=[b_mre], out=mre[:], in0=p1[:], in1=p2[:], op=SUB)
                P.op("dve", "tensor_tensor", R=[psb[bA]] + RT, W=[b_p1], out=p1[:], in0=ps[bA][:], in1=E[:, 1, :], op=MUL)
                P.op("dve", "tensor_tensor", R=[b_p1, b_mim], W=[b_mim], out=mim[:], in0=mim[:], in1=p1[:], op=ADD)
                rcol = CS[:, RR, e_:e_ + 1]
                for (c0, c1) in segs:
                    n_ = c1 - c0
                    i_re = 0.0 if own_j is None else car[:, 0, own_j:own_j + 1]
                    i_im = 0.0 if own_j is None else car[:, 1, own_j:own_j + 1]
                    P.op("dve", "tensor_tensor_scan", R=[b_mre, b_cs, b_car], W=[b_zre], out=zre[:, c0:c1],
                         data0=rcol.to_broadcast([128, n_]), data1=mre[:, c0:c1], initial=i_re, op0=MUL, op1=ADD)
                    P.op("dve", "tensor_tensor_scan", R=[b_mim, b_cs, b_car], W=[b_zim], out=zim[:, c0:c1],
                         data0=rcol.to_broadcast([128, n_]), data1=mim[:, c0:c1], initial=i_im, op0=MUL, op1=ADD)
                ID = AF.Identity
                if own_j is not None:
                    c5, s5 = CS[:, C512, e_:e_ + 1], CS[:, S512, e_:e_ + 1]
                    zlr, zli = zre[:, 511:512], zim[:, 511:512]
                    P.op("act", "activation", R=[b_zim, b_cs], W=[b_tt2], out=tt2[:, 0:1], in_=zli, func=ID, scale=s5)
                    P.op("act", "activation", R=[b_tt2], W=[b_tt2], out=tt2[:, 0:1], in_=tt2[:, 0:1], func=ID, scale=-1.0)
                    P.op("act", "activation", R=[b_zre, b_cs, b_tt2], W=[b_car], out=car[:, 0, own_j:own_j + 1], in_=zlr, func=ID,
                         scale=c5, bias=tt2[:, 0:1])
                    P.op("act", "activation", R=[b_zim, b_cs], W=[b_tt2], out=tt2[:, 1:2], in_=zli, func=ID, scale=c5)
                    P.op("act", "activation", R=[b_zre, b_cs, b_tt2], W=[b_car], out=car[:, 1, own_j:own_j + 1], in_=zlr, func=ID,
                         scale=s5, bias=tt2[:, 1:2])
                else:
                    cc, ss = D[:, 0, 255:256], D[:, 1, 255:256]
                    for s_ in range(2):
                        zfr, zfi = zre[:, 256 * s_ + 255:256 * s_ + 256], zim[:, 256 * s_ + 255:256 * s_ + 256]
                        P.op("act", "activation", R=[b_zim, b_tD[a]], W=[b_tt2], out=tt2[:, 0:1], in_=zfi, func=ID, scale=ss)
                        P.op("act", "activation", R=[b_tt2], W=[b_tt2], out=tt2[:, 0:1], in_=tt2[:, 0:1], func=ID, scale=-1.0)
                        P.op("act", "activation", R=[b_zre, b_tD[a], b_tt2], W=[b_fin], out=fin[:, s_:s_ + 1, e_ // 16, 0, e_ % 16],
                             in_=zfr, func=ID, scale=cc, bias=tt2[:, 0:1])
                        P.op("act", "activation", R=[b_zim, b_tD[a]], W=[b_tt2], out=tt2[:, 1:2], in_=zfi, func=ID, scale=cc)
                        P.op("act", "activation", R=[b_zre, b_tD[a], b_tt2], W=[b_fin], out=fin[:, s_:s_ + 1, e_ // 16, 1, e_ % 16],
                             in_=zfr, func=ID, scale=ss, bias=tt2[:, 1:2])
                RD = [b_tD[a]]

                def stage2(a=a, e_=e_, zre=zre, zim=zim, b_zre=b_zre, b_zim=b_zim, D=D, RD=RD, bC=bC):
                    P.op("pool", "tensor_tensor", R=[b_zre] + RD, W=[b_sre], out=sre[:], in0=zre[:], in1=D[:, 0, :], op=MUL)
                    P.op("pool", "tensor_tensor", R=[b_zim] + RD, W=[b_q1], out=q1[:], in0=zim[:], in1=D[:, 1, :], op=MUL)
                    P.op("pool", "tensor_tensor", R=[b_zre] + RD, W=[b_nsi], out=nsi[:], in0=zre[:], in1=D[:, 1, :], op=MUL)
                    P.op("pool", "tensor_tensor", R=[b_zim] + RD, W=[b_q2], out=q2[:], in0=zim[:], in1=D[:, 0, :], op=MUL)
                    oc = ps[bC][32 * a:32 * a + 32, :]
                    tp = (0, 32 * a)
                    P.mm([dict(out=oc, lhsT=WC[:, e_, 0, :], rhs=sre[:], start=True, stop=False, tile_position=tp),
                          dict(out=oc, lhsT=WCn[:, e_, :], rhs=q1[:], start=False, stop=False, tile_position=tp),
                          dict(out=oc, lhsT=WC[:, e_, 1, :], rhs=nsi[:], start=False, stop=False, tile_position=tp),
                          dict(out=oc, lhsT=WC[:, e_, 1, :], rhs=q2[:], start=False, stop=True, tile_position=tp)],
                         R=[b_WC, b_sre, b_nsi, b_q1, b_q2], W=[psb[bC]])
                pend.append(stage2)

            pend = []

            def flush(keep):
                while len(pend) > keep:
                    pend.pop(0)()

            nq = [0]
            pj = [(d_, quad, a) for d_ in range(2) for quad in range(4) for a in range(4)]
            gen_table(0, 0, 1)
            for i_, (d_, quad, a) in enumerate(pj):
                if a == 0:
                    bC = 6 + (nq[0] % 2)
                    nq[0] += 1
                e_ = 16 * d_ + 4 * quad + a
                if i_ + 1 < len(pj):
                    dn, qn, an = pj[i_ + 1]
                    gen_table(an, 16 * dn + 4 * qn + an, 1, part=1)
                if d_ == 0:
                    src = [(0, ubf[:, quad, 0:512])]
                else:
                    src = [(256 * s_, ubf[:, quad, 256 * s_:256 * s_ + 256][:, ::-1]) for s_ in range(2)]
                pair_job(a, e_, src, [b_u], [(0, 256), (256, 512)], None, bC)
                flush(1)
                if i_ + 1 < len(pj):
                    gen_table(an, 16 * dn + 4 * qn + an, 1, part=2)
                if a == 3:
                    def acc_p(d_=d_, quad=quad, bC=bC):
                        for s_ in range(2):
                            av = accp[:, quad, 256 * s_:256 * s_ + 256]
                            if d_:
                                av = av[:, ::-1]
                            P.op("dve", "tensor_tensor", R=[psb[bC], b_accp], W=[b_accp], out=av, in0=av,
                                 in1=ps[bC][:, 256 * s_:256 * s_ + 256], op=ADD)
                    pend.append(acc_p)
                    if nq[0] in (2, 5):
                        pend.append(late_colls.pop(0))
            flush(0)
            if stop_after == "B2":
                P.barrier()
                A.release(mAB)
                A.release(mLay)
                return
            P.dma("sp", yp_d, accp[:], R=[b_accp], W=[b_ypd])
            P.mm([dict(out=ps[4][:, 0:128], lhsT=fin[:].rearrange("p s d r e -> p (s d r e)"), rhs=identF[:], start=True, stop=True)],
                 R=[b_fin, Bc], W=[psb[4]])
            b_finT = P.b("finT")
            P.op("act", "activation", R=[psb[4]], W=[b_finT], out=finT[:], in_=ps[4][:, 0:128], func=AF.Copy)
            for s_ in range(2):
                P.dma("sp", so[s_, l].rearrange("d r (pi gl) p -> (d r pi) (gl p)", gl=2), finT[64 * s_:64 * s_ + 64, :],
                      R=[b_finT], is_out=True)
            dump("accp%d" % l, accp[:].rearrange("p a t -> p (a t)"), [128, 2048], [b_accp])
            if stop_after == "B3":
                P.barrier()
                A.release(mAB)
                A.release(mLay)
                return
            mgen = mod_vectors_gen(l + 1) if l + 1 < nlayers else iter(())
            for d_ in range(2):
                if "nosamp" in DBGF:
                    break
                for a in range(4):
                    gen_table(a, 32 + 4 * d_ + a, 0)
                for ch in range(8):
                    if "1ch" in DBGF and ch:
                        break
                    bC = 6 + (nq[0] % 2)
                    nq[0] += 1
                    for a in range(4):
                        if d_ == 0:
                            src = [(0, useq[:, 512 * ch:512 * ch + 512])]
                        else:
                            src = [(0, useq[:, 4096 - 512 * ch - 512:4096 - 512 * ch][:, ::-1])]
                        pair_job(a, 32 + 4 * d_ + a, src, [b_useq], [(0, 512)], 4 * d_ + a, bC)
                        flush(1)
                        next(mgen, None)

                    def acc_s(d_=d_, ch=ch, bC=bC):
                        if d_ == 0:
                            av = accs[:, 512 * ch:512 * ch + 512]
                        else:
                            av = accs[:, 4096 - 512 * ch - 512:4096 - 512 * ch][:, ::-1]
                        P.op("dve", "tensor_tensor", R=[psb[bC], b_accs], W=[b_accs], out=av, in0=av, in1=ps[bC][:], op=ADD)
                    pend.append(acc_s)
                    if d_ == 1 and ch % 2 == 1:
                        def xfer(tb=3 - ch // 2):
                            P.dma("sp", xy[tb], accs[:, 1024 * tb:1024 * tb + 1024], R=[b_accs], W=[b_xy])
                            P.coll(xy[tb], xy_g[512 * tb:512 * tb + 512, :], R=[b_xy], W=[b_xyg])
                        pend.append(xfer)
            flush(0)
            for _ in mgen:
                pass
            dump("accs%d" % l, accs[:], [128, 4096], [b_accs])
            P.barrier()
            A.release(mAB)
            if stop_after == "B":
                A.release(mLay)
                return

            mC = A.mark()
            kgT = A.alloc("kgT", [128, 4352], BF16)
            vgA = A.alloc("vgA", [128, 34, 2, 65], BF16)
            kwT = A.alloc("kwT", [128, 1536], BF16)
            vwA = A.alloc("vwA", [128, 12, 2, 65], BF16)
            wglu = A.alloc("wglu", [128, 4, 512], BF16)
            ctb = A.alloc("ctb", [128, 2, 2, 128], BF16)
            b_kg, b_vg, b_kw, b_vw, b_wglu, b_ctb = (P.b(n) for n in ("kgT", "vgA", "kwT", "vwA", "wglu", "ctb"))
            P.dma("pool", wglu[:], w_glu[l].rearrange("(kt p) c -> p kt c", p=128), W=[b_wglu])

            def kv_assembly():
                P.op("pool", "memset", W=[b_vg], ap=vgA[:], constant=1.0)
                P.op("pool", "memset", W=[b_vw], ap=vwA[:], constant=1.0)
                P.dma("sp", kgT[:, 0:4096].rearrange("p (r t) -> p r t", r=4),
                      xk_g.rearrange("(r two p) t -> p r two t", two=2, p=128)[:, :, 1, :], R=[b_xkg], W=[b_kg])
                for g_ in range(2):
                    P.dma("sp", vgA[:, 0:32, g_, 0:64], xv_g[:, 128 + 64 * g_:192 + 64 * g_].rearrange("(t p) d -> p t d", p=128),
                          R=[b_xvg], W=[b_vg])
                dyn_dma("sp", lambda e, kwT=kwT: (kwT[:, 128:1152], xk_g[bass.ds(pid4(e) * 256, 128), :]), R=[b_xkg], W=[b_kw])
                dyn_dma("sp", lambda e, kwT=kwT: (kwT[:, 0:128], xk_g[bass.ds(((pid4(e) + 3) % 4) * 256, 128), 896:1024]), R=[b_xkg], W=[b_kw])
                dyn_dma("sp", lambda e, kwT=kwT: (kwT[:, 1152:1280], xk_g[bass.ds(((pid4(e) + 1) % 4) * 256, 128), 0:128]), R=[b_xkg], W=[b_kw])
                for g_ in range(2):
                    dyn_dma("sp", lambda e, g_=g_, vwA=vwA: (vwA[:, 1:9, g_, 0:64],
                                                    xv_g[bass.ds(pid4(e) * 1024, 1024), 64 * g_:64 * g_ + 64].rearrange("(t p) d -> p t d", p=128)),
                            R=[b_xvg], W=[b_vw])
                dyn_dma("sp", lambda e, vwA=vwA: (vwA[:, 0, :, 0:64],
                                         xv_g[bass.ds(((pid4(e) + 3) % 4) * 1024 + 896, 128), 0:128].rearrange("p (g d) -> p g d", g=2)),
                        R=[b_xvg], W=[b_vw])
                dyn_dma("sp", lambda e, vwA=vwA: (vwA[:, 9, :, 0:64],
                                         xv_g[bass.ds(((pid4(e) + 1) % 4) * 1024, 128), 0:128].rearrange("p (g d) -> p g d", g=2)),
                        R=[b_xvg], W=[b_vw])
                for g_ in range(2):
                    P.dma("pool", vwA[:, 10:12, g_, 0:64], cv[l, 0][:, 64 * g_:64 * g_ + 64].rearrange("(t p) d -> p t d", p=128), W=[b_vw])
                    P.dma("pool", vgA[:, 32:34, g_, 0:64], cv[l, 1][:, 64 * g_:64 * g_ + 64].rearrange("(t p) d -> p t d", p=128), W=[b_vg])
                P.dma("pool", ctb[:], ck[l].rearrange("b (t p) c -> p b t c", p=128), W=[b_ctb])
                for br in range(2):
                    for t_ in range(2):
                        P.mm([dict(out=ps[0][:, 0:128], lhsT=ctb[:, br, t_, :], rhs=cm[:, IDB, :], start=True, stop=True)],
                             R=[b_ctb, Bc], W=[psb[0]])
                        if br == 0:
                            P.op("act", "activation", R=[psb[0]], W=[b_kw], out=kwT[:, 1280 + 128 * t_:1408 + 128 * t_], in_=ps[0][:, 0:128], func=AF.Copy)
                        else:
                            P.op("act", "activation", R=[psb[0]], W=[b_kg], out=kgT[:, 4096 + 128 * t_:4224 + 128 * t_], in_=ps[0][:, 0:128], func=AF.Copy)


            _b2 = [0, 2]

            def nb2():
                _b2[0] = (_b2[0] + 1) % _b2[1]
                return _b2[0]

            wvd = w_down[l].rearrange("(kt p) c -> p kt c", p=128)
            wvu = w_up[l].rearrange("(kt p) c -> p kt c", p=128)
            wvo = w_out[l].rearrange("(kt p) c -> p kt c", p=128)
            wvb0 = w_br[l, 0].rearrange("(kt p) c -> p kt c", p=128)
            wvb1 = w_br[l, 1].rearrange("(h d) c -> d h c", d=64)
            wvb2 = w_br[l, 2].rearrange("(h d) c -> d h c", d=64)

            for c in range(3):
                k = k_of[c]
                cols = slice(512 * c, 512 * c + 512)
                p0 = 512 * (c - 1)
                if c == 1:
                    kv_assembly()
                mCh = A.mark()
                hT = A.alloc("hT", [128, 8, 512], BF16)
                for ft in range(8):
                    P.op("dve", "tensor_scalar", R=[xb(ft, c), Bmod], W=[b_hT], out=hT[:, ft, :], in0=xT[:, ft, cols],
                         scalar1=modv[:, 8 + ft, k:k + 1], scalar2=modv[:, ft, k:k + 1], op0=MUL, op1=ADD)
                mC12 = A.mark()
                ya = A.alloc("ya", [128, 4, 512], BF16)
                ywT = A.alloc("ywT", [64, 8, 512], BF16)
                ygT = A.alloc("ygT", [64, 8, 512], BF16)
                b_ya, b_yw, b_yg = P.b("ya"), P.b("ywT"), P.b("ygT")
                _b2[1] = 6
                cpend = []

                def cflush(keep):
                    while len(cpend) > keep:
                        cpend.pop(0)()
                mC1 = A.mark()
                wblk = [A.alloc("wblk", [128, 8, 512], BF16) for _ in range(2)]
                b_wblk = [P.b("wblk", i_) for i_ in range(2)]
                ssm_in = A.alloc("ssm_in", [128, 4, 512], F32)
                gb = A.alloc("gb", [128, 4, 512], BF16)
                qwT = A.alloc("qwT", [128, 4, 512], BF16)
                qgT = A.alloc("qgT", [128, 4, 512], BF16)
                x32 = A.alloc("x32", [128, 512], F32)
                sqb = A.alloc("sqb", [128, 512], BF16)
                rsf = A.alloc("rsf", [128, 512], F32)
                xbb = A.alloc("xbb", [128, 512], BF16)
                t1f = A.alloc("t1f", [128, 512], F32)
                t2f = A.alloc("t2f", [128, 512], F32)
                sgf = A.alloc("sgf", [128, 512], F32)
                pT = [A.alloc("pT", [128, 512], BF16) for _ in range(4)]
                drow = A.alloc("drow", [128, 512], F32)
                bcs = A.alloc("bcs", [64, 512], F32)
                b_ssm, b_gb, b_qw, b_qg, b_sgf, b_drow, b_bcs = (P.b(n) for n in ("ssm_in", "gb", "qwT", "qgT", "sgf", "drow", "bcs"))
                b_pT = [P.b("pT", i_) for i_ in range(4)]
                if c == 0:
                    P.dma("sp", ssm_in[:], yp_d, R=[b_ypd], W=[b_ssm])
                else:
                    dyn_dma("sp", lambda e, p0=p0, ssm_in=ssm_in: (ssm_in[:], xy_g[bass.ds(pid4(e) * 512, 512), p0:p0 + 512].rearrange("(q p) t -> p q t", p=128)),
                            R=[b_xyg], W=[b_ssm])
                for bi, kind in enumerate(("u", "qw", "qg")):
                    w = wblk[bi % 2]
                    bw = b_wblk[bi % 2]
                    if kind == "u":
                        P.dma("pool", w[:], wv[:, :, 0:512], W=[bw])
                    else:
                        c0 = 512 if kind == "qw" else 1280
                        for j_ in range(4):
                            for hf in range(2):
                                P.dma("pool", w[:, :, 128 * j_ + 64 * hf:128 * j_ + 64 * hf + 64],
                                      wv[:, :, c0 + 256 * hf + 64 * j_:c0 + 256 * hf + 64 * j_ + 64], W=[bw])
                    for m in range(4):
                        bank = nb2()
                        P.mm([dict(out=ps[bank][:], lhsT=w[:, kt, 128 * m:128 * m + 128], rhs=hT[:, kt, :],
                                   start=(kt == 0), stop=(kt == 7)) for kt in range(8)], R=[bw, b_hT], W=[psb[bank]])
                        def chain(kind=kind, m=m, bank=bank):
                            if kind == "u":
                                P.op("dve", "scalar_tensor_tensor", R=[psb[bank], Blay, b_ssm], W=[b_x32], out=x32[:], in0=ps[bank][:],
                                     scalar=dskT[:, m:m + 1], in1=ssm_in[:, m, :], op0=MUL, op1=ADD)
                                P.op("act", "activation", R=[b_x32], W=[b_t1], out=t1f[:], in_=x32[:], func=AF.Square)
                                P.op("dve", "tensor_scalar", R=[b_t1], W=[b_t1], out=t1f[:], in0=t1f[:], scalar1=0.044715, scalar2=1.0, op0=MUL, op1=ADD)
                                P.op("dve", "tensor_tensor", R=[b_t1, b_x32], W=[b_t1], out=t1f[:], in0=t1f[:], in1=x32[:], op=MUL)
                                P.op("act", "activation", R=[b_t1], W=[b_t2], out=t2f[:], in_=t1f[:], func=AF.Sigmoid, scale=1.5957691216057308)
                                P.op("dve", "tensor_tensor", R=[b_t2, b_x32], W=[b_gb], out=gb[:, m, :], in0=x32[:], in1=t2f[:], op=MUL)
                            elif kind == "qw":
                                if c == 0:
                                    P.op("act", "activation", R=[psb[bank]], W=[b_qw], out=qwT[:, m, :], in_=ps[bank][:], func=AF.Copy)
                                else:
                                    P.op("act", "activation", R=[psb[bank]], W=[b_x32], out=x32[:], in_=ps[bank][:], func=AF.Copy)
                                    rope_fm(512, p0, qwT[:, m, :], [b_qw])
                            else:
                                if c == 0:
                                    rmsnorm_fm(bank, 512, qknT[:, 0:1], qgT[:, m, :], [b_qg])
                                else:
                                    rmsnorm_fm(bank, 512, qknT[:, 0:1], x32[:], [b_x32])
                                    rope_fm(512, p0, qgT[:, m, :], [b_qg])
                        cpend.append(chain)
                        cflush(1)
                cflush(0)
                _b2[1] = 2
                for m in range(4):
                    bank = nb2()
                    P.mm([dict(out=ps[bank][:], lhsT=wglu[:, kt, 128 * m:128 * m + 128], rhs=gb[:, kt, :],
                               start=(kt == 0), stop=(kt == 3)) for kt in range(4)], R=[b_wglu, b_gb], W=[psb[bank]])
                    P.op("act", "activation", R=[psb[bank]], W=[b_sgf], out=sgf[:], in_=ps[bank][:], func=AF.Sigmoid)
                    P.op("dve", "tensor_tensor", R=[b_sgf, b_gb], W=[b_ya], out=ya[:, m, :], in0=gb[:, m, :], in1=sgf[:], op=MUL)
                if c == 1:
                    dump("ya%d" % l, ya[:, 0, :], [128, 512], [b_ya], BF16)
                    dump("qgT%d" % l, qgT[:, 0, :], [128, 512], [b_qg], BF16)

                acnt = [0]
                fin_pend = []

                def attn_core(N, qsrc, Rq, ktiles, bO, c0):
                    nt = len(ktiles)
                    LA = 3
                    for ti in range(nt + LA):
                        if ti < nt:
                            kT_ap, Rk, v_ap, Rv, mask = ktiles[ti]
                            bS = ti % 4
                            pt, bp = pT[ti % 4], b_pT[ti % 4]
                            P.mm([dict(out=ps[bS][:, :N], lhsT=kT_ap, rhs=qsrc, start=True, stop=True)], R=Rk + Rq, W=[psb[bS]])
                            P.op("act", "activation", R=[psb[bS]], W=[bp], out=pt[:, :N], in_=ps[bS][:, :N], func=AF.Exp, scale=0.125)
                            if mask is not None:
                                P.op("dve", "tensor_tensor", R=[bp, Bc], W=[bp], out=pt[:, :N], in0=pt[:, :N], in1=mask, op=MUL)
                        if ti >= LA:
                            tj = ti - LA
                            kT_ap, Rk, v_ap, Rv, mask = ktiles[tj]
                            P.mm([dict(out=ps[bO][0:65, c0:c0 + N], lhsT=v_ap, rhs=pT[tj % 4][:, :N], start=(tj == 0), stop=(tj == nt - 1))],
                                 R=Rv + [b_pT[tj % 4]], W=[psb[bO]])

                def attn_fin(N, h, sink, bO, out_ap, Wb):
                    def f():
                        if sink:
                            P.op("dve", "tensor_scalar", R=[psb[bO], Blay], W=[b_drow], out=drow[64:65, :N], in0=ps[bO][64:65, :N],
                                 scalar1=esink[64:65, h:h + 1], scalar2=None, op0=ADD)
                        else:
                            P.op("dve", "tensor_copy", R=[psb[bO]], W=[b_drow], out=drow[64:65, :N], in_=ps[bO][64:65, :N])
                        P.op("act", "activation", R=[b_drow], W=[b_drow], out=drow[64:65, :N], in_=drow[64:65, :N], func=AF.Ln)
                        P.op("act", "activation", R=[b_drow], W=[b_drow], out=drow[64:65, :N], in_=drow[64:65, :N], func=AF.Exp, scale=-1.0)
                        P.mm([dict(out=ps[6][0:64, :N], lhsT=onesF[64:65, 0:64], rhs=drow[64:65, :N], start=True, stop=True)],
                             R=[b_drow, Bc], W=[psb[6]])
                        P.op("act", "activation", R=[psb[6]], W=[b_bcs], out=bcs[:, :N], in_=ps[6][0:64, :N], func=AF.Copy)
                        P.op("dve", "tensor_tensor", R=[psb[bO], b_bcs], W=Wb, out=out_ap, in0=ps[bO][0:64, :N], in1=bcs[:, :N], op=MUL)
                    fin_pend.append(f)
                    while len(fin_pend) > 1:
                        fin_pend.pop(0)()

                def next_bO():
                    acnt[0] += 1
                    return 4 + (acnt[0] % 2)

                for h in range(8):
                    g, j = h // 4, h % 4
                    pr = slice(64 * g, 64 * g + 64)
                    if c == 0:
                        for br, (qT_, bq_, oT_, bo_) in enumerate(((qwT, b_qw, ywT, b_yw), (qgT, b_qg, ygT, b_yg))):
                            bO = next_bO()
                            for s_ in range(2):
                                sc_ = slice(256 * s_, 256 * s_ + 256)
                                kts = [(kpT[pr, br, 256 * s_ + 128 * t_:256 * s_ + 128 * t_ + 128], [Bkp],
                                        vp[:, 2 * s_ + t_, br, g, :], [Bvp], None) for t_ in range(2)]
                                attn_core(256, qT_[pr, j, sc_], [bq_], kts, bO, 256 * s_)
                            attn_fin(512, h, br == 0, bO, oT_[:, h, :], [bo_])
                    else:
                        bO = next_bO()
                        for qb in range(4):
                            nl = 4 * (c - 1) + qb
                            qc = slice(128 * qb, 128 * qb + 128)
                            kts = [(kwT[pr, 128 * nl:128 * nl + 128], [b_kw], vwA[:, nl, g, :], [b_vw], cm[:, MLF if nl == 0 else ML, :]),
                                   (kwT[pr, 128 * nl + 128:128 * nl + 256], [b_kw], vwA[:, nl + 1, g, :], [b_vw], None),
                                   (kwT[pr, 128 * nl + 256:128 * nl + 384], [b_kw], vwA[:, nl + 2, g, :], [b_vw], cm[:, MRL if nl == 7 else MR, :]),
                                   (kwT[pr, 1280:1408], [b_kw], vwA[:, 10, g, :], [b_vw], None),
                                   (kwT[pr, 1408:1536], [b_kw], vwA[:, 11, g, :], [b_vw], None)]
                            attn_core(128, qwT[pr, j, qc], [b_qw], kts, bO, 128 * qb)
                        attn_fin(512, h, True, bO, ywT[:, h, :], [b_yw])
                        bO = next_bO()
                        kts = [(kgT[pr, 128 * t_:128 * t_ + 128], [b_kg], vgA[:, t_, g, :], [b_vg], None) for t_ in range(34)]
                        attn_core(512, qgT[pr, j, :], [b_qg], kts, bO, 0)
                        attn_fin(512, h, False, bO, ygT[:, h, :], [b_yg])
                while fin_pend:
                    fin_pend.pop(0)()
                if c == 1:
                    dump("ywT%d" % l, ywT[:, 0, :], [64, 512], [b_yw], BF16)
                    dump("ygT%d" % l, ygT[:, 0, :], [64, 512], [b_yg], BF16)
                if c == 0:
                    dump("ywTp%d" % l, ywT[:, 0, :], [64, 512], [b_yw], BF16)
                    dump("ygTp%d" % l, ygT[:, 0, :], [64, 512], [b_yg], BF16)
                P.barrier()
                A.release(mC1)

                _b2[1] = 4
                wg = [A.alloc("wg", [128, 8, 3, 256], BF16) for _ in range(2)]
                wbs = [A.alloc("wbs", [128, 4, 256], BF16) for _ in range(2)]
                wbw = [A.alloc("wbw", [64, 8, 256], BF16) for _ in range(2)]
                wbg = [A.alloc("wbg", [64, 8, 256], BF16) for _ in range(2)]
                b_wm = [P.b("wm2", i_) for i_ in range(2)]
                mT = A.alloc("mT", [128, 8, 512], BF16)
                sgf2 = [A.alloc("sgf", [128, 512], F32) for _ in range(2)]
                b_sgf2 = [P.b("sgf2", i_) for i_ in range(2)]
                tmp2 = [A.alloc("tmp2", [128, 512], F32) for _ in range(2)]
                b_tmp2 = [P.b("tmp2", i_) for i_ in range(2)]
                nsg = [0]
                accf = A.alloc("accf", [128, 512], F32)
                tmpf = A.alloc("tmpf", [128, 512], F32)
                wo = [A.alloc("wo", [128, 8, 256], BF16) for _ in range(2)]
                b_wo = [P.b("wo", i_) for i_ in range(2)]
                sqs = [A.alloc("sqs", [128, 512], BF16) for _ in range(2)]
                zbs = [A.alloc("zbs", [128, 512], BF16) for _ in range(2)]
                b_sq = [P.b("sqs", i_) for i_ in range(2)]
                b_zb = [P.b("zbs", i_) for i_ in range(2)]
                m2 = A.alloc("m2", [128, 512], F32)
                lt = A.alloc("lt", [128, 512], F32)
                b_mT, b_acc, b_tmp, b_m2, b_lt = (P.b(n) for n in ("mT", "accf", "tmpf", "m2", "lt"))

                def layer_norm(gsel):
                    for ft in range(8):
                        sq, zb = sqs[ft % 2], zbs[ft % 2]
                        P.op("act", "activation", R=[xb(ft, c)], W=[b_sq[ft % 2]], out=sq[:], in_=xT[:, ft, cols], func=AF.Square)
                        P.op("act", "activation", R=[xb(ft, c)], W=[b_zb[ft % 2]], out=zb[:], in_=xT[:, ft, cols], func=AF.Copy)
                        P.mm([dict(out=ps[6][:], lhsT=lnmean[:], rhs=zb[:], start=(ft == 0), stop=(ft == 7))],
                             R=[b_zb[ft % 2], Bc], W=[psb[6]])
                        P.mm([dict(out=ps[7][:], lhsT=lnmean[:], rhs=sq[:], start=(ft == 0), stop=(ft == 7))],
                             R=[b_sq[ft % 2], Bc], W=[psb[7]])
                    P.op("act", "activation", R=[psb[6]], W=[b_m2], out=m2[:], in_=ps[6][:], func=AF.Square)
                    P.op("dve", "tensor_tensor", R=[psb[7], b_m2], W=[b_m2], out=m2[:], in0=ps[7][:], in1=m2[:], op=SUB)
                    P.op("act", "activation", R=[b_m2, Bc], W=[b_m2], out=m2[:], in_=m2[:], func=AF.Ln, bias=cst[:, 0:1], scale=1.0)
                    P.op("act", "activation", R=[b_m2], W=[b_m2], out=m2[:], in_=m2[:], func=AF.Exp, scale=-0.5)
                    for ft in range(8):
                        P.op("dve", "tensor_tensor", R=[xb(ft, c), psb[6]], W=[b_lt], out=lt[:], in0=xT[:, ft, cols], in1=ps[6][:], op=SUB)
                        P.op("dve", "tensor_tensor", R=[b_lt, b_m2], W=[b_lt], out=lt[:], in0=lt[:], in1=m2[:], op=MUL)
                        P.op("dve", "tensor_scalar", R=[b_lt, Blay], W=[xb(ft, c)], out=xT[:, ft, cols], in0=lt[:],
                             scalar1=lnpT[:, gsel, ft:ft + 1], scalar2=lnpT[:, gsel + 1, ft:ft + 1], op0=MUL, op1=ADD)

                for m in range(8):
                    i2 = (m // 2) % 2
                    mc = 128 * (m % 2)
                    bw = b_wm[i2]
                    if m % 2 == 0:
                        c2_ = 256 * (m // 2)
                        for kk in range(3):
                            P.dma("pool", wg[i2][:, :, kk, :], wv[:, :, 2048 + 1024 * kk + c2_:2048 + 1024 * kk + c2_ + 256], W=[bw])
                        P.dma("pool", wbs[i2][:], wvb0[:, :, c2_:c2_ + 256], W=[bw])
                        P.dma("pool", wbw[i2][:], wvb1[:, :, c2_:c2_ + 256], W=[bw])
                        P.dma("pool", wbg[i2][:], wvb2[:, :, c2_:c2_ + 256], W=[bw])
                    for kk in range(3):
                        bg_ = nb2()
                        P.mm([dict(out=ps[bg_][:], lhsT=wg[i2][:, kt, kk, mc:mc + 128], rhs=hT[:, kt, :], start=(kt == 0), stop=(kt == 7))
                              for kt in range(8)], R=[bw, b_hT], W=[psb[bg_]])
                        sgf, b_sgfc = sgf2[nsg[0] % 2], b_sgf2[nsg[0] % 2]
                        nsg[0] += 1
                        P.op("act", "activation", R=[psb[bg_]], W=[b_sgfc], out=sgf[:], in_=ps[bg_][:], func=AF.Sigmoid)
                        bp_ = 4 + (kk % 2)
                        if kk == 0:
                            P.mm([dict(out=ps[bp_][:], lhsT=wbs[i2][:, kt, mc:mc + 128], rhs=ya[:, kt, :], start=(kt == 0), stop=(kt == 3))
                                  for kt in range(4)], R=[bw, b_ya], W=[psb[bp_]])
                        else:
                            wsel, ysel, by_ = (wbw, ywT, b_yw) if kk == 1 else (wbg, ygT, b_yg)
                            P.mm([dict(out=ps[bp_][:], lhsT=wsel[i2][:, hh, mc:mc + 128], rhs=ysel[:, hh, :], start=(hh == 0), stop=(hh == 7))
                                  for hh in range(8)], R=[bw, by_], W=[psb[bp_]])
                        if kk == 0:
                            P.op("dve", "tensor_tensor", R=[psb[bp_], b_sgfc], W=[b_acc], out=accf[:], in0=ps[bp_][:], in1=sgf[:], op=MUL)
                        else:
                            P.op("dve", "tensor_tensor", R=[psb[bp_], b_sgfc], W=[b_tmp], out=tmpf[:], in0=ps[bp_][:], in1=sgf[:], op=MUL)
                            if kk == 1:
                                P.op("dve", "tensor_tensor", R=[b_acc, b_tmp], W=[b_acc], out=accf[:], in0=accf[:], in1=tmpf[:], op=ADD)
                            else:
                                P.op("dve", "tensor_tensor", R=[b_acc, b_tmp], W=[b_mT], out=mT[:, m, :], in0=accf[:], in1=tmpf[:], op=ADD)
                if c == 1:
                    dump("mT%d" % l, mT[:, 0, :], [128, 512], [b_mT], BF16)
                for ob in range(4):
                    wo_, bwo_ = wo[ob % 2], b_wo[ob % 2]
                    P.dma("pool", wo_[:], wvo[:, :, 256 * ob:256 * ob + 256], W=[bwo_])
                    for mm_ in range(2):
                        mo = 2 * ob + mm_
                        bank = nb2()
                        P.mm([dict(out=ps[bank][:], lhsT=wo_[:, kt, 128 * mm_:128 * mm_ + 128], rhs=mT[:, kt, :],
                                   start=(kt == 0), stop=(kt == 7)) for kt in range(8)], R=[bwo_, b_mT], W=[psb[bank]])
                        tq, btq = tmp2[mo % 2], b_tmp2[mo % 2]
                        P.op("act", "activation", R=[psb[bank], Bmod], W=[btq], out=tq[:], in_=ps[bank][:], func=AF.Copy,
                             scale=modv[:, 16 + mo, k:k + 1])
                        P.op("dve", "scalar_tensor_tensor", R=[xb(mo, c), btq], W=[xb(mo, c)], out=xT[:, mo, cols], in0=xT[:, mo, cols],
                             scalar=ALPHA, in1=tq[:], op0=MUL, op1=ADD)
                layer_norm(0)
                if c == 1:
                    dump("xln1_%d" % l, xT[:, 0, cols], [128, 512], [xb(0, c)])
                P.barrier()
                A.release(mC12)

                hid = A.alloc("hid", [128, 32, 512], BF16)
                wup = [A.alloc("wup", [128, 8, 512], BF16) for _ in range(2)]
                wdn = [A.alloc("wdn", [128, 4, 1024], BF16) for _ in range(2)]
                rl = [A.alloc("rl", [128, 512], F32) for _ in range(2)]
                tmp2 = [A.alloc("tmp2", [128, 512], F32) for _ in range(2)]
                sqs = [A.alloc("sqs", [128, 512], BF16) for _ in range(2)]
                zbs = [A.alloc("zbs", [128, 512], BF16) for _ in range(2)]
                m2 = A.alloc("m2", [128, 512], F32)
                lt = A.alloc("lt", [128, 512], F32)
                b_hid = P.b("hid")
                b_wup = [P.b("wup", i_) for i_ in range(2)]
                b_wdn = [P.b("wdn", i_) for i_ in range(2)]
                b_rl = [P.b("rl", i_) for i_ in range(2)]
                for ft in range(8):
                    P.op("dve", "tensor_scalar", R=[xb(ft, c), Bmod], W=[b_hT], out=hT[:, ft, :], in0=xT[:, ft, cols],
                         scalar1=modv[:, 32 + ft, k:k + 1], scalar2=modv[:, 24 + ft, k:k + 1], op0=MUL, op1=ADD)
                nr = 0
                for jb in range(8):
                    w, bw = wup[jb % 2], b_wup[jb % 2]
                    P.dma("pool", w[:], wvu[:, :, 512 * jb:512 * jb + 512], W=[bw])
                    for t_ in range(4):
                        bank = nb2()
                        P.mm([dict(out=ps[bank][:], lhsT=w[:, kt, 128 * t_:128 * t_ + 128], rhs=hT[:, kt, :],
                                   start=(kt == 0), stop=(kt == 7)) for kt in range(8)], R=[bw, b_hT], W=[psb[bank]])
                        r_, br_ = rl[nr % 2], b_rl[nr % 2]
                        nr += 1
                        P.op("act", "activation", R=[psb[bank]], W=[br_], out=r_[:], in_=ps[bank][:], func=AF.Relu)
                        P.op("dve" if nr % 2 else "pool", "tensor_tensor", R=[br_], W=[b_hid], out=hid[:, 4 * jb + t_, :], in0=r_[:], in1=r_[:], op=MUL)
                for kb in range(8):
                    w, bw = wdn[kb % 2], b_wdn[kb % 2]
                    P.dma("pool", w[:], wvd[:, 4 * kb:4 * kb + 4, :], W=[bw])
                    for mo in range(8):
                        P.mm([dict(out=ps[mo][:], lhsT=w[:, kq, 128 * mo:128 * mo + 128], rhs=hid[:, 4 * kb + kq, :],
                                   start=(kb == 0 and kq == 0), stop=(kb == 7 and kq == 3)) for kq in range(4)],
                             R=[bw, b_hid], W=[psb[mo]])
                for mo in range(8):
                    tq, btq = tmp2[mo % 2], b_tmp2[mo % 2]
                    P.op("act", "activation", R=[psb[mo], Bmod], W=[btq], out=tq[:], in_=ps[mo][:], func=AF.Copy,
                         scale=modv[:, 40 + mo, k:k + 1])
                    P.op("dve", "scalar_tensor_tensor", R=[xb(mo, c), btq], W=[xb(mo, c)], out=xT[:, mo, cols], in0=xT[:, mo, cols],
                         scalar=ALPHA, in1=tq[:], op0=MUL, op1=ADD)
                layer_norm(2)
                P.barrier()
                A.release(mCh)
            P.barrier()
            A.release(mLay)

        for l in range(nlayers):
            layer(l)

        m0 = A.mark()
        ytok = [A.alloc("ytok", [128, 1024], F32) for _ in range(2)]
        for tt in range(12):
            yt = ytok[tt % 2]
            by = P.b("ytok", tt % 2)
            c = tt // 4
            for hh in range(2):
                bank = (2 * tt + hh) % 2
                P.mm([dict(out=ps[bank][:, 128 * q:128 * q + 128],
                           lhsT=xT[:, 4 * hh + q, 128 * tt:128 * tt + 128],
                           rhs=identF[:], start=True, stop=True) for q in range(4)],
                     R=[xb(ft, c) for ft in range(4 * hh, 4 * hh + 4)] + [Bc], W=[psb[bank]])
                if hh:
                    P.op("act", "activation", R=[psb[bank]], W=[by], out=yt[:, 512:1024], in_=ps[bank][:], func=AF.Copy)
                else:
                    P.op("dve", "tensor_copy", R=[psb[bank]], W=[by], out=yt[:, 0:512], in_=ps[bank][:])
            P.dma("sp", yo[128 * tt:128 * tt + 128, :], yt[:], R=[by], is_out=True)
        A.release(m0)
        P.finish(block)
    return nc, A.peak


def _consts():
    ident = np.eye(128, dtype=np.float32)
    rotm = np.zeros((128, 128), np.float32)
    for hb in range(2):
        for j in range(32):
            rotm[64 * hb + j + 32, 64 * hb + j] = -1.0
            rotm[64 * hb + j, 64 * hb + j + 32] = 1.0
    hmean = np.zeros((128, 128), np.float32)
    hmean[:64, :64] = 1.0 / 64
    hmean[64:, 64:] = 1.0 / 64
    jj = np.arange(128)
    mL = (jj[:, None] >= jj[None, :]).astype(np.float32)
    mR = (jj[:, None] <= jj[None, :]).astype(np.float32)
    return ident, rotm, hmean, mL, mR


def _rope_tables(i):
    t = 1024 * i + np.arange(1024)
    row = (t // 64).astype(np.float32)
    col = (t % 64).astype(np.float32)
    inv = (10000.0 ** (-np.arange(16, dtype=np.float32) / 16)).astype(np.float32)
    ang = np.concatenate([row[:, None] * inv, col[:, None] * inv], axis=-1).astype(np.float32)
    c, s = np.cos(ang).astype(np.float32), np.sin(ang).astype(np.float32)
    p = np.arange(128) % 32
    return np.stack([c[:, p].T, s[:, p].T]).astype(np.float32)


def _pair_list(i):
    es = [(e // 16, e % 16) for e in range(32)]
    es += [(j // 4, 4 * i + j % 4) for j in range(8)]
    return es


def _host_inputs(inp, r, shared):
    f = np.float32
    b, i = r // 4, r % 4
    d = dict(shared)
    d["xin"] = np.ascontiguousarray(np.concatenate(
        [inp["x_prompt"][2 * r:2 * r + 2].reshape(512, 1024), inp["x_sample"][b, 1024 * i:1024 * (i + 1)]], 0), f)
    cond = np.stack([inp["c_ctx"], inp["c"][b]])
    d["condT"] = np.ascontiguousarray(cond.reshape(2, 8, 128).transpose(2, 1, 0), f)
    pl = _pair_list(i)
    lamh = np.zeros((2, 128, 3, 40), f)
    wbc = np.zeros((2, 4, 32, 10, 2, 128), f)
    wcc = np.zeros((2, 128, 40, 2, 32), f)
    s0 = np.zeros((2, 128, 2, 8), f)
    for l in range(2):
        for e, (dd, pi) in enumerate(pl):
            for gl in range(2):
                g = 2 * pi + gl
                n0 = 64 * gl
                lamh[l, n0:n0 + 64, 0, e] = inp["ssm_lam_re"][l, dd, g]
                lamh[l, n0:n0 + 64, 1, e] = inp["ssm_lam_im"][l, dd, g]
                lamh[l, n0:n0 + 64, 2, e] = inp["ssm_log_step"][l, dd, g]
                wbc[l, e % 4, 16 * gl:16 * gl + 16, e // 4, 0, n0:n0 + 64] = inp["ssm_b_re"][l, dd, g].T
                wbc[l, e % 4, 16 * gl:16 * gl + 16, e // 4, 1, n0:n0 + 64] = inp["ssm_b_im"][l, dd, g].T
                wcc[l, n0:n0 + 64, e, 0, 16 * gl:16 * gl + 16] = inp["ssm_c_re"][l, dd, g].T
                wcc[l, n0:n0 + 64, e, 1, 16 * gl:16 * gl + 16] = inp["ssm_c_im"][l, dd, g].T
                if e >= 32:
                    s0[l, n0:n0 + 64, 0, e - 32] = inp["state_ssm"][b, l, dd, 0, g]
                    s0[l, n0:n0 + 64, 1, e - 32] = inp["state_ssm"][b, l, dd, 1, g]
    d["lam"], d["wbc"], d["wcc"], d["s0h"] = lamh, wbc, wcc, s0
    d["ck"] = np.ascontiguousarray(np.stack([inp["cache_k_win"][b].reshape(2, 256, 128), inp["cache_k_glb"][b].reshape(2, 256, 128)], 1), f)
    d["cv"] = np.ascontiguousarray(np.stack([inp["cache_v_win"][b].reshape(2, 256, 128), inp["cache_v_glb"][b].reshape(2, 256, 128)], 1), f)
    ident, rotm, hmean, mL, mR = _consts()
    z = np.zeros_like(mL)
    d["cmat"] = np.stack([ident, rotm, hmean, mL, mR, z if i == 0 else mL, z if i == 3 else mR]).astype(f)
    d["rope"] = _rope_tables(i)
    return d


def _shared_inputs(inp):
    f = np.float32
    d = {}
    d["w_mod"] = np.ascontiguousarray(inp["w_mod"], f)
    d["b_modT"] = np.ascontiguousarray(inp["b_mod"].reshape(2, 48, 128).transpose(0, 2, 1), f)
    d["w_in"] = np.ascontiguousarray(inp["w_in"], f)
    d["dskipT"] = np.ascontiguousarray(inp["ssm_d"].reshape(2, 4, 128).transpose(0, 2, 1), f)
    d["w_glu"] = np.ascontiguousarray(inp["w_glu"], f)
    d["w_br"] = np.ascontiguousarray(np.stack([inp["w_br_ssm"], inp["w_br_win"], inp["w_br_glb"]], 1), f)
    d["w_out"] = np.ascontiguousarray(inp["w_out"], f)
    d["w_up"] = np.ascontiguousarray(inp["w_up"], f)
    d["w_down"] = np.ascontiguousarray(inp["w_down"], f)
    lnp = np.stack([inp["ln1_g"], inp["ln1_b"], inp["ln2_g"], inp["ln2_b"]], 1)
    d["lnp"] = np.ascontiguousarray(lnp.reshape(2, 4, 8, 128).transpose(0, 3, 1, 2), f)
    qn = np.tile(inp["q_norm_glb"], (1, 2))
    kn = np.tile(inp["k_norm_glb"], (1, 2))
    d["qkn"] = np.ascontiguousarray(np.stack([qn, kn], -1), f)
    d["knrow"] = np.ascontiguousarray(np.broadcast_to(inp["k_norm_glb"][:, None, :], (2, 128, 64)), f)
    d["sinkb"] = np.ascontiguousarray(np.broadcast_to(inp["sink_win"][:, None, :], (2, 128, 8)), f)
    jj = np.arange(512, dtype=f)
    d["jidx"] = np.ascontiguousarray(np.stack([np.broadcast_to(jj, (128, 512)), np.broadcast_to(jj % 256, (128, 512))]), f)
    return d


_CACHE = {}


def _run(inputs, debug=(), nlayers=2, stop_after=None):
    inp = {k: np.asarray(v) for k, v in inputs.items()}
    key = (tuple(debug), nlayers, stop_after)
    if key not in _CACHE:
        _CACHE[key] = build(debug, nlayers, stop_after)
    nc, peak = _CACHE[key]
    shared = _shared_inputs(inp)
    in_maps = [_host_inputs(inp, r, shared) for r in range(8)]
    res = run_bass_kernel_spmd(nc, in_maps, core_ids=list(range(8)))
    return res.results


def kernel(**inputs):
    R = _run(inputs)
    f = np.float32
    y_prompt = np.zeros((16, 256, 1024), f)
    y_sample = np.zeros((2, 4096, 1024), f)
    st = np.zeros((16, 2, 2, 2, 32, 64), f)
    kw = np.zeros((16, 2, 256, 2, 64), f)
    vw = np.zeros_like(kw)
    kg = np.zeros_like(kw)
    vg = np.zeros_like(kw)
    for r in range(8):
        o = R[r]
        b, i = r // 4, r % 4
        yo = np.asarray(o["yo"])
        y_prompt[2 * r:2 * r + 2] = yo[:512].reshape(2, 256, 1024)
        y_sample[b, 1024 * i:1024 * (i + 1)] = yo[512:]
        co = np.asarray(o["co"])
        for s in range(2):
            kw[2 * r + s] = co[s, :, 0].reshape(2, 256, 2, 64)
            vw[2 * r + s] = co[s, :, 1].reshape(2, 256, 2, 64)
            kg[2 * r + s] = co[s, :, 2].reshape(2, 256, 2, 64)
            vg[2 * r + s] = co[s, :, 3].reshape(2, 256, 2, 64)
        st[2 * r:2 * r + 2] = np.asarray(o["so"])
    return (y_prompt, y_sample, st, kw, vw, kg, vg)
```
